# Optimizing a Trainium2 kernel written in Bass

```python
import math
import jax
import jax.numpy as jnp
from jax import lax
import numpy as np


D_MODEL = 2048
BATCH = 2
SEQ = 4096
DEPTH = 4

HEAD_DIM = 128
N_MIXERS = 4
HEADS_PER_MIXER = D_MODEL // HEAD_DIM // N_MIXERS
GROUP_WIDTH = HEADS_PER_MIXER * HEAD_DIM
MIX_WIDTH = N_MIXERS * GROUP_WIDTH
Q_BLOCK = 128
ROPE_THETA = 500000.0
ROPE_FRACTION = 4
NSA_KV_WIDTH = HEAD_DIM
NSA_CMP_BLOCK = 32
NSA_CMP_STRIDE = 16
NSA_CMP_HIDDEN = 256
NSA_SLC_BLOCK = 64
NSA_TOPK = 16
NSA_LOCAL_BLOCKS = 2
NSA_WINDOW = 512
DIFF_QK_DIM = HEAD_DIM // 2
D_FF = 5632
CONV_WIDTH = 3
EPS = 1e-6
NEG_INF = -1e30
FORCE_SCORE = 1e9

IN_SPLITS = (
    GROUP_WIDTH,
    NSA_KV_WIDTH, NSA_KV_WIDTH,
    NSA_KV_WIDTH, NSA_KV_WIDTH,
    NSA_KV_WIDTH, NSA_KV_WIDTH,
    3 * HEADS_PER_MIXER,
    GROUP_WIDTH, GROUP_WIDTH, GROUP_WIDTH,
    GROUP_WIDTH, GROUP_WIDTH, GROUP_WIDTH, HEADS_PER_MIXER,
    GROUP_WIDTH, GROUP_WIDTH, GROUP_WIDTH,
)
IN_COLS = sum(IN_SPLITS)

kernel_name = 'hybrid_parallel_heads_nsa_sb_fox_diff'


def _rms_norm(x, g):
    xf = x.astype(jnp.float32)
    y = xf * lax.rsqrt(jnp.mean(xf * xf, axis=-1, keepdims=True) + EPS)
    return (y * g.astype(jnp.float32)).astype(x.dtype)


def _rope_tables(seq, rot_dim):
    pos = jnp.arange(seq, dtype=jnp.float32)
    inv = ROPE_THETA ** (-jnp.arange(0, rot_dim, 2, dtype=jnp.float32) / rot_dim)
    ang = pos[:, None] * inv[None, :]
    return jnp.cos(ang), jnp.sin(ang)


def _partial_rope(x, cos, sin):
    half = cos.shape[-1]
    rot = 2 * half
    xf = x.astype(jnp.float32)
    x1, x2, rest = xf[..., :half], xf[..., half:rot], xf[..., rot:]
    out = jnp.concatenate([x1 * cos - x2 * sin, x2 * cos + x1 * sin, rest], axis=-1)
    return out.astype(x.dtype)


def _heads(t, n_heads):
    b, s, _ = t.shape
    return t.reshape(b, s, n_heads, -1).transpose(0, 2, 1, 3)


def _merge_heads(o):
    b, h, s, d = o.shape
    return o.transpose(0, 2, 1, 3).reshape(b, s, h * d)


def _sweep(body, n_blocks):
    out = lax.map(body, jnp.arange(n_blocks))
    nb, b, h, qb, d = out.shape
    return out.transpose(1, 2, 0, 3, 4).reshape(b, h, nb * qb, d)


def _stick_breaking_attention(q, k, v):
    b, h, s_len, d = q.shape
    scale = d ** -0.5
    kpos = jnp.arange(s_len)

    def body(i):
        q0 = i * Q_BLOCK
        qb = lax.dynamic_slice_in_dim(q, q0, Q_BLOCK, axis=2)
        qpos = q0 + jnp.arange(Q_BLOCK)
        past = kpos[None, :] < qpos[:, None]
        z = jnp.einsum('bhqd,bhkd->bhqk', qb, k).astype(jnp.float32) * scale
        log_keep = jnp.where(past, jax.nn.log_sigmoid(-z), 0.0)
        after = lax.cumsum(log_keep, axis=3, reverse=True) - log_keep
        w = jnp.where(past, jnp.exp(jax.nn.log_sigmoid(z) + after), 0.0)
        return jnp.einsum('bhqk,bhkd->bhqd', w.astype(v.dtype), v)

    return _sweep(body, s_len // Q_BLOCK)


def _forgetting_attention(q, k, v, log_f):
    b, h, s_len, d = q.shape
    scale = d ** -0.5
    cum = jnp.cumsum(log_f, axis=-1)
    kpos = jnp.arange(s_len)

    def body(i):
        q0 = i * Q_BLOCK
        qb = lax.dynamic_slice_in_dim(q, q0, Q_BLOCK, axis=2)
        cq = lax.dynamic_slice_in_dim(cum, q0, Q_BLOCK, axis=2)
        qpos = q0 + jnp.arange(Q_BLOCK)
        causal = kpos[None, :] <= qpos[:, None]
        s = jnp.einsum('bhqd,bhkd->bhqk', qb, k).astype(jnp.float32) * scale
        s = s + cq[..., :, None] - cum[..., None, :]
        p = jax.nn.softmax(jnp.where(causal, s, NEG_INF), axis=-1)
        return jnp.einsum('bhqk,bhkd->bhqd', p.astype(v.dtype), v)

    return _sweep(body, s_len // Q_BLOCK)


def _differential_attention(q1, q2, k1, k2, v, lam):
    b, h, s_len, dq = q1.shape
    scale = dq ** -0.5
    kpos = jnp.arange(s_len)

    def body(i):
        q0 = i * Q_BLOCK
        qpos = q0 + jnp.arange(Q_BLOCK)
        causal = kpos[None, :] <= qpos[:, None]

        def probs(q, k):
            qb = lax.dynamic_slice_in_dim(q, q0, Q_BLOCK, axis=2)
            s = jnp.einsum('bhqd,bhkd->bhqk', qb, k).astype(jnp.float32) * scale
            return jax.nn.softmax(jnp.where(causal, s, NEG_INF), axis=-1)

        w = probs(q1, k1) - lam * probs(q2, k2)
        return jnp.einsum('bhqk,bhkd->bhqd', w.astype(v.dtype), v)

    return _sweep(body, s_len // Q_BLOCK)


def _nsa_attention(q, q_rot, k_cmp, v_cmp, k_slc, v_slc, k_win, v_win, gates,
                   pe_k, pe_v, cmp_k1, cmp_k2, cmp_v1, cmp_v2):
    b, h, s_len, d = q.shape
    scale = d ** -0.5
    n_c = (s_len - NSA_CMP_BLOCK) // NSA_CMP_STRIDE + 1
    n_sel = s_len // NSA_SLC_BLOCK
    tpos = jnp.arange(s_len)

    blk = np.arange(n_c)[:, None] * NSA_CMP_STRIDE + np.arange(NSA_CMP_BLOCK)[None, :]

    def compress(t, pe, w1, w2):
        tb = (t[:, blk] + pe).reshape(b, n_c, NSA_CMP_BLOCK * d)
        return jax.nn.gelu(tb @ w1) @ w2

    kc = compress(k_cmp, pe_k, cmp_k1, cmp_k2)
    vc = compress(v_cmp, pe_v, cmp_v1, cmp_v2)
    c_end = np.arange(n_c) * NSA_CMP_STRIDE + NSA_CMP_BLOCK - 1
    c_valid = c_end[None, :] <= tpos[:, None]
    sc = jnp.einsum('bhsd,bcd->bhsc', q, kc).astype(jnp.float32) * scale
    p_cmp = jax.nn.softmax(jnp.where(c_valid, sc, NEG_INF), axis=-1) * c_valid
    o_cmp = jnp.einsum('bhsc,bcd->bhsd', p_cmp.astype(vc.dtype), vc)

    overlap = np.zeros((n_c, n_sel), np.float32)
    for r in range(NSA_CMP_BLOCK // NSA_CMP_STRIDE):
        sub = np.arange(n_c) * NSA_CMP_STRIDE + r * NSA_CMP_STRIDE
        np.add.at(overlap, (np.arange(n_c), sub // NSA_SLC_BLOCK), 1.0)
    imp = jnp.einsum('bsc,cj->bsj', p_cmp.sum(axis=1), overlap)
    blk_id = jnp.arange(n_sel)
    cur = tpos // NSA_SLC_BLOCK
    sel_valid = blk_id[None, :] <= cur[:, None]
    forced = sel_valid & ((blk_id[None, :] == 0) |
                          (blk_id[None, :] >= cur[:, None] - (NSA_LOCAL_BLOCKS - 1)))
    score = jnp.where(forced, FORCE_SCORE, jnp.where(sel_valid, imp, NEG_INF))
    n_top = min(NSA_TOPK, n_sel)
    _, top_idx = lax.top_k(score, n_top)

    ks_blocks = k_slc.reshape(b, n_sel, NSA_SLC_BLOCK, d)
    vs_blocks = v_slc.reshape(b, n_sel, NSA_SLC_BLOCK, d)
    gather = jax.vmap(lambda blocks, idx: blocks[idx])

    def slc_body(i):
        q0 = i * Q_BLOCK
        qb = lax.dynamic_slice_in_dim(q_rot, q0, Q_BLOCK, axis=2)
        ib = lax.dynamic_slice_in_dim(top_idx, q0, Q_BLOCK, axis=1)
        ks = gather(ks_blocks, ib)
        vs = gather(vs_blocks, ib)
        qpos = q0 + jnp.arange(Q_BLOCK)
        kpos = ib[..., None] * NSA_SLC_BLOCK + jnp.arange(NSA_SLC_BLOCK)
        valid = kpos <= qpos[None, :, None, None]
        s = jnp.einsum('bhqd,bqnld->bhqnl', qb, ks).astype(jnp.float32) * scale
        s = jnp.where(valid[:, None], s, NEG_INF).reshape(b, h, Q_BLOCK, -1)
        p = jax.nn.softmax(s, axis=-1).reshape(b, h, Q_BLOCK, n_top, NSA_SLC_BLOCK)
        return jnp.einsum('bhqnl,bqnld->bhqd', p.astype(vs.dtype), vs)

    o_slc = _sweep(slc_body, s_len // Q_BLOCK)

    span = NSA_WINDOW + Q_BLOCK
    kw_pad = jnp.pad(k_win, ((0, 0), (NSA_WINDOW, 0), (0, 0)))
    vw_pad = jnp.pad(v_win, ((0, 0), (NSA_WINDOW, 0), (0, 0)))

    def win_body(i):
        q0 = i * Q_BLOCK
        qb = lax.dynamic_slice_in_dim(q_rot, q0, Q_BLOCK, axis=2)
        kb = lax.dynamic_slice_in_dim(kw_pad, q0, span, axis=1)
        vb = lax.dynamic_slice_in_dim(vw_pad, q0, span, axis=1)
        qpos = q0 + jnp.arange(Q_BLOCK)
        kpos = q0 - NSA_WINDOW + jnp.arange(span)
        valid = ((kpos[None, :] <= qpos[:, None]) & (kpos[None, :] > qpos[:, None] - NSA_WINDOW)
                 & (kpos[None, :] >= 0))
        s = jnp.einsum('bhqd,bkd->bhqk', qb, kb).astype(jnp.float32) * scale
        p = jax.nn.softmax(jnp.where(valid, s, NEG_INF), axis=-1)
        return jnp.einsum('bhqk,bkd->bhqd', p.astype(vb.dtype), vb)

    o_win = _sweep(win_body, s_len // Q_BLOCK)

    return gates[..., 0:1] * o_cmp + gates[..., 1:2] * o_slc + gates[..., 2:3] * o_win


def setup_inputs(seed: int = 0) -> dict:
    key = jax.random.key(seed)
    ks = jax.random.split(key, 20)
    f32 = jnp.float32

    def nrm(k, shape, scale):
        return jax.random.normal(k, shape, f32) * scale

    def gain(k, shape):
        return 1.0 + 0.05 * jax.random.normal(k, shape, f32)

    cmp_in = NSA_CMP_BLOCK * HEAD_DIM
    return {
        'x': nrm(ks[0], (BATCH, SEQ, D_MODEL), 1.0),
        'norm_mix_pre': gain(ks[1], (DEPTH, D_MODEL)),
        'norm_mix_post': gain(ks[2], (DEPTH, D_MODEL)),
        'norm_mlp_pre': gain(ks[3], (DEPTH, D_MODEL)),
        'norm_mlp_post': gain(ks[4], (DEPTH, D_MODEL)),
        'w_in': nrm(ks[5], (DEPTH, D_MODEL, IN_COLS), D_MODEL ** -0.5),
        'w_out': nrm(ks[6], (DEPTH, MIX_WIDTH, D_MODEL), MIX_WIDTH ** -0.5),
        'nsa_pe_k': nrm(ks[7], (DEPTH, NSA_CMP_BLOCK, HEAD_DIM), 0.1),
        'nsa_pe_v': nrm(ks[8], (DEPTH, NSA_CMP_BLOCK, HEAD_DIM), 0.1),
        'nsa_cmp_k1': nrm(ks[9], (DEPTH, cmp_in, NSA_CMP_HIDDEN), cmp_in ** -0.5),
        'nsa_cmp_k2': nrm(ks[10], (DEPTH, NSA_CMP_HIDDEN, HEAD_DIM), NSA_CMP_HIDDEN ** -0.5),
        'nsa_cmp_v1': nrm(ks[11], (DEPTH, cmp_in, NSA_CMP_HIDDEN), cmp_in ** -0.5),
        'nsa_cmp_v2': nrm(ks[12], (DEPTH, NSA_CMP_HIDDEN, HEAD_DIM), NSA_CMP_HIDDEN ** -0.5),
        'fox_forget_bias': 2.0 + nrm(ks[13], (DEPTH, HEADS_PER_MIXER), 0.5),
        'diff_lambda': nrm(ks[14], (DEPTH, 4, DIFF_QK_DIM), 0.1),
        'diff_norm': gain(ks[15], (DEPTH, HEAD_DIM)),
        'mlp_w_up': nrm(ks[16], (DEPTH, D_MODEL, 2 * D_FF), D_MODEL ** -0.5),
        'mlp_conv_w': nrm(ks[17], (DEPTH, CONV_WIDTH, 2 * D_FF), 0.6),
        'mlp_conv_b': nrm(ks[18], (DEPTH, 2 * D_FF), 0.01),
        'mlp_w_down': nrm(ks[19], (DEPTH, D_FF, D_MODEL), D_FF ** -0.5),
    }


def reference(x, norm_mix_pre, norm_mix_post, norm_mlp_pre, norm_mlp_post, w_in, w_out,
              nsa_pe_k, nsa_pe_v, nsa_cmp_k1, nsa_cmp_k2, nsa_cmp_v1, nsa_cmp_v2,
              fox_forget_bias, diff_lambda, diff_norm, mlp_w_up, mlp_conv_w, mlp_conv_b,
              mlp_w_down):
    b, s_len, _ = x.shape
    h_n = HEADS_PER_MIXER
    cos_h, sin_h = _rope_tables(s_len, HEAD_DIM // ROPE_FRACTION)
    cos_d, sin_d = _rope_tables(s_len, DIFF_QK_DIM // ROPE_FRACTION)
    split_at = np.cumsum(IN_SPLITS)[:-1].tolist()

    for l in range(DEPTH):
        hin = _rms_norm(x, norm_mix_pre[l])
        proj = hin @ w_in[l]
        (a_q, a_kc, a_vc, a_ks, a_vs, a_kw, a_vw, a_g,
         b_q, b_k, b_v,
         c_q, c_k, c_v, c_f,
         d_q, d_k, d_v) = jnp.split(proj, split_at, axis=-1)

        qa = _heads(a_q, h_n)
        gates = jax.nn.sigmoid(a_g.reshape(b, s_len, h_n, 3).transpose(0, 2, 1, 3))
        o_a = _nsa_attention(qa, _partial_rope(qa, cos_h, sin_h), a_kc, a_vc,
                             _partial_rope(a_ks, cos_h, sin_h), a_vs,
                             _partial_rope(a_kw, cos_h, sin_h), a_vw, gates,
                             nsa_pe_k[l], nsa_pe_v[l], nsa_cmp_k1[l], nsa_cmp_k2[l],
                             nsa_cmp_v1[l], nsa_cmp_v2[l])

        o_b = _stick_breaking_attention(_heads(b_q, h_n), _heads(b_k, h_n), _heads(b_v, h_n))

        log_f = jax.nn.log_sigmoid((c_f + fox_forget_bias[l]).astype(jnp.float32)).transpose(0, 2, 1)
        o_c = _forgetting_attention(_heads(c_q, h_n), _heads(c_k, h_n), _heads(c_v, h_n), log_f)

        dqh = _heads(d_q, h_n)
        dkh = _heads(d_k, h_n)
        q1 = _partial_rope(dqh[..., :DIFF_QK_DIM], cos_d, sin_d)
        q2 = _partial_rope(dqh[..., DIFF_QK_DIM:], cos_d, sin_d)
        k1 = _partial_rope(dkh[..., :DIFF_QK_DIM], cos_d, sin_d)
        k2 = _partial_rope(dkh[..., DIFF_QK_DIM:], cos_d, sin_d)
        lam_init = 0.8 - 0.6 * math.exp(-0.3 * l)
        lp = diff_lambda[l].astype(jnp.float32)
        lam = jnp.exp(jnp.sum(lp[0] * lp[1])) - jnp.exp(jnp.sum(lp[2] * lp[3])) + lam_init
        o_d = _differential_attention(q1, q2, k1, k2, _heads(d_v, h_n), lam)
        o_d = _rms_norm(o_d, diff_norm[l]) * (1.0 - lam_init)

        mixed = jnp.concatenate([_merge_heads(o_a), _merge_heads(o_b),
                                 _merge_heads(o_c), _merge_heads(o_d)], axis=-1) @ w_out[l]
        x = x + _rms_norm(mixed, norm_mix_post[l])

        hin = _rms_norm(x, norm_mlp_pre[l])
        u = hin @ mlp_w_up[l]
        u_pad = jnp.pad(u, ((0, 0), (CONV_WIDTH - 1, 0), (0, 0)))
        conv = mlp_conv_b[l] + mlp_conv_w[l][0] * u_pad[:, 0:s_len]
        for tap in range(1, CONV_WIDTH):
            conv = conv + mlp_conv_w[l][tap] * u_pad[:, tap:tap + s_len]
        gate, up = jnp.split(conv, 2, axis=-1)
        y = (jax.nn.gelu(gate, approximate=True) * up) @ mlp_w_down[l]
        x = x + _rms_norm(y, norm_mlp_post[l])

    return x
```

```python
import math
import numpy as np
import ml_dtypes
import concourse.bass as bass
import concourse.mybir as mybir
from concourse.bass_utils import run_bass_kernel_spmd
from contextlib import ExitStack

F32 = mybir.dt.float32
BF16 = mybir.dt.bfloat16
ALU = mybir.AluOpType
AF = mybir.ActivationFunctionType
AX = mybir.AxisListType
NPBF = ml_dtypes.bfloat16

ENGS = ['sync', 'scalar', 'vector', 'gpsimd', 'tensor']

D = 2048
SEQ = 4096
NB = 2
DEPTH = 4
HD = 128
DFF = 5632
TOK = 1024
EPS = 1e-6
N_C = 255


class _Op:
    __slots__ = ('fn', 'waits', 'inc', 'chan', 'idx')

    def __init__(self, fn):
        self.fn = fn
        self.waits = []
        self.inc = False
        self.chan = None
        self.idx = 0


class Prog:
    def __init__(self, nc, es):
        self.nc = nc
        self.es = es
        self.ops = {e: [] for e in ENGS}
        self.chan_cnt = {}
        self.last_w = {}
        self.readers = {}
        self.waited = {e: {} for e in ENGS}

    def sb(self, name, shape, dt):
        return self.es.enter_context(self.nc.sbuf_tensor(name, list(shape), dt))

    def ps(self, name, shape, dt=F32):
        return self.es.enter_context(self.nc.psum_tensor(name, list(shape), dt))

    def _dep(self, eng, op, key, idx):
        if key == 'tensor' and eng == 'tensor':
            return
        if self.waited[eng].get(key, -1) >= idx:
            return
        self.waited[eng][key] = idx
        op.waits.append((key, idx))
        if key in self.ops:
            self.ops[key][idx].inc = True

    def op(self, eng, fn, reads=(), writes=(), chan=None):
        o = _Op(fn)
        for r in reads:
            t = self.last_w.get(r)
            if t is not None:
                self._dep(eng, o, t[0], t[1])
        for w in writes:
            t = self.last_w.get(w)
            if t is not None:
                self._dep(eng, o, t[0], t[1])
            rd = self.readers.get(w)
            if rd:
                for k, i in rd.items():
                    self._dep(eng, o, k, i)
        lst = self.ops[eng]
        o.idx = len(lst)
        lst.append(o)
        if chan is not None:
            o.chan = chan
            c = self.chan_cnt.get(chan, 0) + 1
            self.chan_cnt[chan] = c
            tok = (('dma', chan), c)
        else:
            tok = (eng, o.idx)
        for r in reads:
            rd = self.readers.setdefault(r, {})
            if rd.get(tok[0], -1) < tok[1]:
                rd[tok[0]] = tok[1]
        for w in writes:
            self.last_w[w] = tok
            self.readers[w] = {}
        return tok

    def dma(self, eng, out, in_, reads=(), writes=(), chan=None):
        return self.op(eng, lambda e: e.dma_start(out=out, in_=in_), reads, writes, chan=chan)

    def finish(self, keys, eng='sync'):
        o = _Op(None)
        for k in keys:
            t = self.last_w.get(k)
            if t is not None:
                self._dep(eng, o, t[0], t[1])
        o.idx = len(self.ops[eng])
        self.ops[eng].append(o)

    def emit(self):
        nc = self.nc
        es = self.es
        sems = {}
        for e in ENGS:
            sems[e] = es.enter_context(nc.semaphore('s_' + e))
        for i, ch in enumerate(self.chan_cnt):
            sems[('dma', ch)] = es.enter_context(nc.semaphore('d%d' % i))
        val = {}
        for e in ENGS:
            c = 0
            v = []
            for o in self.ops[e]:
                if o.inc and o.chan is None:
                    c += 1
                v.append(c)
            val[e] = v
        block = es.enter_context(nc.Block())

        def mk(ename):
            def body(eng):
                for o in self.ops[ename]:
                    for key, idx in o.waits:
                        if key in val:
                            eng.wait_ge(sems[key], val[key][idx])
                        else:
                            eng.wait_ge(sems[key], 16 * idx)
                    if o.fn is None:
                        continue
                    ins = o.fn(eng)
                    if o.chan is not None:
                        ins.then_inc(sems[('dma', o.chan)], 16)
                    elif o.inc:
                        ins.then_inc(sems[ename], 1)
            return body
        for e in ENGS:
            if self.ops[e]:
                getattr(block, e)(mk(e))

    def mm(self, ps_ap, pairs, reads, writes, start=True, stop=True):
        pairs = list(pairs)

        def fn(e):
            n = len(pairs)
            ins = None
            for i, (l, r) in enumerate(pairs):
                ins = e.matmul(ps_ap, lhsT=l, rhs=r, start=(start and i == 0),
                               stop=(stop and i == n - 1))
            return ins
        return self.op('tensor', fn, reads, writes)

    def act(self, out, in_, func, reads, writes, bias=None, scale=None, eng='scalar'):
        kw = {}
        if bias is not None:
            kw['bias'] = bias
        if scale is not None:
            kw['scale'] = scale
        return self.op(eng, lambda e: e.activation(out=out, in_=in_, func=func, **kw), reads, writes)

    def tt(self, eng, out, in0, in1, op, reads, writes):
        return self.op(eng, lambda e: e.tensor_tensor(out=out, in0=in0, in1=in1, op=op), reads, writes)

    def ts(self, eng, out, in0, s1, s2, op0, op1, reads, writes):
        if s2 is None:
            return self.op(eng, lambda e: e.tensor_scalar(out=out, in0=in0, scalar1=s1, scalar2=None, op0=op0),
                           reads, writes)
        return self.op(eng, lambda e: e.tensor_scalar(out=out, in0=in0, scalar1=s1, scalar2=s2, op0=op0, op1=op1),
                       reads, writes)

    def stt(self, eng, out, in0, scalar, in1, op0, op1, reads, writes):
        return self.op(eng, lambda e: e.scalar_tensor_tensor(out=out, in0=in0, scalar=scalar, in1=in1,
                                                              op0=op0, op1=op1), reads, writes)

    def cp(self, eng, out, in_, reads, writes):
        if eng == 'scalar':
            return self.op(eng, lambda e: e.copy(out=out, in_=in_), reads, writes)
        return self.op(eng, lambda e: e.tensor_copy(out=out, in_=in_), reads, writes)

    def memset(self, eng, ap, v, writes):
        return self.op(eng, lambda e: e.memset(ap, v), [], writes)


class PsumRot:
    def __init__(self, P, names, n):
        self.tiles = [(nm, P.ps(nm, [128, 512])) for nm in names[:n]]
        self.i = 0

    def next(self):
        t = self.tiles[self.i % len(self.tiles)]
        self.i += 1
        return t


def rms_stats(P, src_fn, nk, N, ones_bf, sq_bufs, ps_name, ps_tile, rstd_ap, rstd_key, src_keys, tag):
    for k in range(nk):
        nm, sq = sq_bufs[k % len(sq_bufs)]
        P.act(sq[:, 0:N], src_fn(k), AF.Square, [src_keys(k)], [nm])
        P.mm(ps_tile[:, 0:N], [(ones_bf[:, :], sq[:, 0:N])], [nm, 'ones'], [ps_name],
             start=(k == 0), stop=(k == nk - 1))
    P.act(rstd_ap, ps_tile[:, 0:N], AF.Sqrt, [ps_name], [rstd_key], bias=EPS, scale=1.0 / (nk * 128))
    P.op('vector', lambda e: e.reciprocal(out=rstd_ap, in_=rstd_ap), [rstd_key], [rstd_key])


OFF = dict(a_q=0, a_kc=512, a_vc=640, a_ks=768, a_vs=896, a_kw=1024, a_vw=1152, a_g=1280,
           b_q=1292, b_k=1804, b_v=2316, c_q=2828, c_k=3340, c_v=3852, c_f=4364,
           d_q=4368, d_k=4880, d_v=5392)
SC128 = 128 ** -0.5
SC64 = 64 ** -0.5
NFM = 36


def _swap_h(cols):
    p = np.arange(128)
    p[:16] += 16
    p[16:32] -= 16
    return cols[p]


def _swap_d(cols):
    p = np.arange(128)
    for b in (0, 64):
        p[b:b + 8] += 8
        p[b + 8:b + 16] -= 8
    return cols[p]


def a_plan():
    tiles = []
    def c128(o):
        return np.arange(o, o + 128)
    for h in range(4):
        c = c128(OFF['a_q'] + 128 * h)
        tiles.append((c, ('rope', 'H', SC128, h, 4 + h)))
        tiles.append((_swap_h(c), ('swap',)))
    tiles.append((c128(OFF['a_kc']), ('plain', 1.0, 8)))
    tiles.append((c128(OFF['a_vc']), ('plain', 1.0, 9)))
    for nm, oi in (('a_ks', 10), ('a_kw', 11)):
        c = c128(OFF[nm])
        tiles.append((c, ('rope', 'H', 1.0, oi, None)))
        tiles.append((_swap_h(c), ('swap',)))
    for nm, o0, sc in (('b_q', 12, SC128), ('b_k', 16, 1.0), ('c_q', 20, SC128), ('c_k', 24, 1.0)):
        for h in range(4):
            tiles.append((c128(OFF[nm] + 128 * h), ('plain', sc, o0 + h)))
    for nm, o0, sc in (('d_q', 28, SC64), ('d_k', 32, 1.0)):
        for h in range(4):
            c = c128(OFF[nm] + 128 * h)
            tiles.append((c, ('rope', 'D', sc, o0 + h, None)))
            tiles.append((_swap_d(c), ('swap',)))
    sm = np.zeros(128, np.int64)
    sm[:12] = np.arange(OFF['a_g'], OFF['a_g'] + 12)
    sm[12:16] = np.arange(OFF['c_f'], OFF['c_f'] + 4)
    tiles.append((sm, ('small',)))
    tiles.append((np.zeros(128, np.int64), ('skip',)))
    assert len(tiles) == 48
    tm = np.concatenate([np.arange(OFF['a_vs'], OFF['a_vs'] + 128), np.arange(OFF['a_vw'], OFF['a_vw'] + 128),
                         np.arange(OFF['b_v'], OFF['b_v'] + 512), np.arange(OFF['c_v'], OFF['c_v'] + 512),
                         np.arange(OFF['d_v'], OFF['d_v'] + 512), np.zeros(256, np.int64)])
    return tiles, tm


NTM = 1792


def host_w_in_blocks(w_in_l):
    tiles, tm = a_plan()
    cols = np.concatenate([t[0] for t in tiles] + [tm])
    w = w_in_l[:, cols]
    w = w.reshape(16, 128, 16, 512)
    return np.ascontiguousarray(w.transpose(2, 1, 0, 3))


def rope_tables(t0, n):
    pos = np.arange(t0, t0 + n, dtype=np.float32)
    theta = np.float32(500000.0)

    def tab(rot, period, ):
        half = rot // 2
        inv = theta ** (-np.arange(0, rot, 2, dtype=np.float32) / np.float32(rot))
        ang = (pos[:, None] * inv[None, :]).astype(np.float32)
        cos = np.cos(ang).T.astype(np.float32)
        sin = np.sin(ang).T.astype(np.float32)
        ct = np.ones((128, n), np.float32)
        st = np.zeros((128, n), np.float32)
        for b in range(0, 128, period):
            ct[b:b + half] = cos
            ct[b + half:b + rot] = cos
            st[b:b + half] = -sin
            st[b + half:b + rot] = sin
        return ct, st
    cH, sH = tab(32, 128)
    cD, sD = tab(16, 64)
    return np.ascontiguousarray(np.stack([cH * np.float32(SC128), sH * np.float32(SC128), cH, sH,
                                          cD * np.float32(SC64), sD * np.float32(SC64), cD, sD],
                                         axis=1).astype(np.float32))


def build_A():
    nc = bass.Bass("TRN2", target_bir_lowering=False)
    xT = nc.dram_tensor("xT", [128, 16, TOK], F32, kind="ExternalInput").ap()
    gpre = nc.dram_tensor("gpre", [128, 16], F32, kind="ExternalInput").ap()
    wblk = nc.dram_tensor("wblk", [16, 128, 16 * 512], F32, kind="ExternalInput").ap()
    tabs = nc.dram_tensor("tabs", [128, 8, TOK], F32, kind="ExternalInput").ap()
    fm_out = nc.dram_tensor("fm_out", [NFM, 128, TOK], BF16, kind="ExternalOutput").ap()
    sm_out = nc.dram_tensor("sm_out", [128, TOK], F32, kind="ExternalOutput").ap()
    tm_out = nc.dram_tensor("tm_out", [TOK, NTM], BF16, kind="ExternalOutput").ap()
    tiles, _ = a_plan()
    with ExitStack() as es:
        P = Prog(nc, es)
        x_sb = P.sb("x_sb", [128, 16, TOK], F32)
        hin = P.sb("hin", [128, 16, TOK], BF16)
        g_sb = P.sb("g_sb", [128, 16], F32)
        tab_sb = P.sb("tab_sb", [128, 8, TOK], F32)
        ones = P.sb("ones", [128, 128], BF16)
        rstd = P.sb("rstd", [128, 512], F32)
        sqb = [("sq%d" % i, P.sb("sq%d" % i, [128, 512], BF16)) for i in range(3)]
        NWB = 3
        wb = [P.sb("wb%d" % i, [128, 16 * 512], BF16) for i in range(NWB)]
        stg = [P.sb("stg%d" % i, [128, 512], BF16) for i in range(4)]
        t1 = [P.sb("t1_%d" % i, [128, 512], F32) for i in range(2)]
        t2 = [P.sb("t2_%d" % i, [128, 512], F32) for i in range(2)]
        smb = P.sb("smb", [128, TOK], F32)
        rot = PsumRot(P, ["ps%d" % i for i in range(7)], 7)
        ps_ss = P.ps("ps_ss", [128, 512])

        P.memset('vector', ones[:, :], 1.0, ['ones'])
        for q in range(4):
            P.dma('sync', x_sb[:, 4 * q:4 * q + 4, :], xT[:, 4 * q:4 * q + 4, :], [], [('x', q)], chan='x%d' % q)
        P.dma('sync', g_sb[:, :], gpre, [], ['g'], chan='g')
        P.dma('sync', tab_sb[:, :, :], tabs, [], ['tab'], chan='tab')
        wq = {'next': 0}

        def load_w(b):
            i = b % NWB
            P.dma('gpsimd', wb[i][:, :], wblk[b], [], [('wb', i)], chan='wb%d' % i)
        for b in range(NWB):
            load_w(b)
        for tg in range(2):
            sl = slice(tg * 512, tg * 512 + 512)
            rms_stats(P, lambda k: x_sb[:, k, sl], 16, 512, ones, sqb, 'ps_ss', ps_ss, rstd[:, :], 'rstd',
                      lambda k: ('x', k // 4), 'A')
            for k in range(16):
                P.stt('vector', hin[:, k, sl], x_sb[:, k, sl], g_sb[:, k:k + 1], rstd[:, :], ALU.mult, ALU.mult,
                      [('x', k // 4), 'g', 'rstd'], [('hin', tg)])
        sti = [0]

        def stage_out(dst_ap):
            i = sti[0] % 4
            sti[0] += 1
            return i, stg[i]
        pend = {}
        for b in range(12):
            bi = b % NWB
            for ct in range(4):
                cols, job = tiles[b * 4 + ct]
                if job[0] == 'skip':
                    continue
                for tg in range(2):
                    sl = slice(tg * 512, tg * 512 + 512)
                    pn, pt = rot.next()
                    P.mm(pt[:, :], [(wb[bi][:, k * 512 + ct * 128:k * 512 + ct * 128 + 128], hin[:, k, sl])
                                    for k in range(16)], [('wb', bi), ('hin', tg)], [pn])
                    if job[0] == 'plain':
                        i, st = stage_out(None)
                        if job[1] == 1.0:
                            P.cp('scalar', st[:, :], pt[:, :], [pn], [('stg', i)])
                        else:
                            P.ts('vector', st[:, :], pt[:, :], float(job[1]), None, ALU.mult, None, [pn], [('stg', i)])
                        P.dma('sync', fm_out[job[2], :, sl], st[:, :], [('stg', i)], [('fm', job[2], tg)],
                              chan='stg%d' % i)
                    elif job[0] == 'small':
                        P.cp('scalar', smb[:, sl], pt[:, :], [pn], [('smb', tg)])
                        P.dma('sync', sm_out[:, sl], smb[:, sl], [('smb', tg)], [('sm', tg)], chan='smo%d' % tg)
                    elif job[0] == 'rope':
                        pend[tg] = (pn, pt, job)
                    elif job[0] == 'swap':
                        mn, mt, mj = pend[tg]
                        tb = (0 if mj[1] == 'H' else 4) + (0 if mj[2] != 1.0 else 2)
                        j = tg
                        P.tt('vector', t1[j][:, :], mt[:, :], tab_sb[:, tb, sl], ALU.mult, [mn, 'tab'], [('t1', j)])
                        P.tt('vector', t2[j][:, :], pt[:, :], tab_sb[:, tb + 1, sl], ALU.mult, [pn, 'tab'], [('t2', j)])
                        i, st = stage_out(None)
                        P.tt('vector', st[:, :], t1[j][:, :], t2[j][:, :], ALU.add, [('t1', j), ('t2', j)], [('stg', i)])
                        P.dma('sync', fm_out[mj[3], :, sl], st[:, :], [('stg', i)], [('fm', mj[3], tg)],
                              chan='stg%d' % i)
                        if mj[4] is not None:
                            i, st = stage_out(None)
                            P.ts('vector', st[:, :], mt[:, :], float(mj[2]), None, ALU.mult, None, [mn], [('stg', i)])
                            P.dma('sync', fm_out[mj[4], :, sl], st[:, :], [('stg', i)], [('fm', mj[4], tg)],
                                  chan='stg%d' % i)
            if b + NWB < 16:
                load_w(b + NWB)
        for b in range(12, 16):
            bi = b % NWB
            ncol = 512 if b < 15 else 256
            c0 = (b - 12) * 512
            for tt_ in range(8):
                tg = tt_ // 4
                pn, pt = rot.next()
                P.mm(pt[:, 0:ncol], [(hin[:, k, tt_ * 128:tt_ * 128 + 128], wb[bi][:, k * 512:k * 512 + ncol])
                                     for k in range(16)], [('wb', bi), ('hin', tg)], [pn])
                i, st = stage_out(None)
                P.cp('scalar', st[:, 0:ncol], pt[:, 0:ncol], [pn], [('stg', i)])
                P.dma('sync', tm_out[tt_ * 128:tt_ * 128 + 128, c0:c0 + ncol], st[:, 0:ncol], [('stg', i)],
                      [('tm', b, tt_)], chan='stg%d' % i)
            if b + NWB < 16:
                load_w(b + NWB)
        P.finish([k for k in P.last_w if isinstance(k, tuple) and k[0] in ('fm', 'sm', 'tm')])
        P.emit()
    return nc


NBLK_C = 42


def host_c_weights(w_out_l, w_up_l, w_down_l, conv_w_l, conv_b_l):
    blocks = np.zeros((NBLK_C, 128, 8192), np.float32)
    wo = w_out_l.reshape(16, 128, 4, 512)
    blocks[0:4] = wo.transpose(2, 1, 0, 3).reshape(4, 128, 8192)
    ucols = []
    for j in range(22):
        for ct in range(4):
            i = 2 * j + ct // 2
            base = i * 128 if ct % 2 == 0 else DFF + i * 128
            ucols.append(np.arange(base, base + 128))
    ucols = np.concatenate(ucols)
    wu = w_up_l[:, ucols].reshape(16, 128, 22, 512)
    blocks[4:26] = wu.transpose(2, 1, 0, 3).reshape(22, 128, 8192)
    wd = w_down_l.reshape(44, 128, 16, 128)
    blocks[26:42, :, 0:44 * 128] = wd.transpose(2, 1, 0, 3).reshape(16, 128, 44 * 128)
    cw = np.zeros((128, 88, 4), np.float32)
    cc = ucols.reshape(88, 128)
    for tap in range(3):
        cw[:, :, tap] = conv_w_l[tap][cc].T
    cw[:, :, 3] = conv_b_l[cc].T
    return blocks, cw


def build_C():
    nc = bass.Bass("TRN2", target_bir_lowering=False)
    NT = TOK + 2
    xT = nc.dram_tensor("xT", [128, 16, NT], F32, kind="ExternalInput").ap()
    oT = nc.dram_tensor("oT", [128, 16, NT], BF16, kind="ExternalInput").ap()
    gn = nc.dram_tensor("gn", [128, 3, 16], F32, kind="ExternalInput").ap()
    cwd = nc.dram_tensor("cw", [128, 88, 4], F32, kind="ExternalInput").ap()
    wblk = nc.dram_tensor("wblk", [NBLK_C, 128, 8192], F32, kind="ExternalInput").ap()
    x_out = nc.dram_tensor("x_out", [128, 16, TOK], F32, kind="ExternalOutput").ap()
    with ExitStack() as es:
        P = Prog(nc, es)
        xg = P.sb("xg", [128, 16, 512], F32)
        og = P.sb("og", [128, 16, 512], BF16)
        mixed = P.sb("mixed", [128, 16, 512], F32)
        hin2 = P.sb("hin2", [128, 16, 512], BF16)
        hT = P.sb("hT", [128, 44, 512], BF16)
        g_sb = P.sb("g_sb", [128, 3, 16], F32)
        cw = P.sb("cw_sb", [128, 88, 4], F32)
        utail = P.sb("utail", [128, 88, 2], F32)
        ones = P.sb("ones", [128, 128], BF16)
        rstd = P.sb("rstd", [128, 512], F32)
        sqb = [("sq%d" % i, P.sb("sq%d" % i, [128, 512], BF16)) for i in range(3)]
        NWB = 3
        wb = [P.sb("wb%d" % i, [128, 8192], BF16) for i in range(NWB)]
        ubuf = [P.sb("ubuf%d" % i, [128, 516], F32) for i in range(2)]
        cbuf = [P.sb("cbuf%d" % i, [128, 512], F32) for i in range(2)]
        gbuf = [P.sb("gbuf%d" % i, [128, 512], F32) for i in range(2)]
        rot = PsumRot(P, ["ps%d" % i for i in range(7)], 7)
        ps_ss = P.ps("ps_ss", [128, 512])

        P.memset('vector', ones[:, :], 1.0, ['ones'])
        P.dma('sync', g_sb[:, :, :], gn, [], ['g'], chan='g')
        P.dma('sync', cw[:, :, :], cwd, [], ['cw'], chan='cw')
        groups = [(0, 2, True), (2, 512, False), (514, 512, False)]
        seq = []
        for gi, (c0, N, halo) in enumerate(groups):
            nb = 26 if halo else 42
            for b in range(nb):
                seq.append(b)
        wstate = {'issued': 0}

        def load_next():
            n = wstate['issued']
            if n >= len(seq):
                return
            b = seq[n]
            i = n % NWB
            ne = 8192 if b < 26 else 44 * 128
            P.dma('gpsimd', wb[i][:, 0:ne], wblk[b, :, 0:ne], [], [('wb', i)], chan='wb%d' % i)
            wstate['issued'] = n + 1
        used = [0]

        def cur_w():
            i = used[0] % NWB
            used[0] += 1
            return i
        for _ in range(NWB):
            load_next()
        for gi, (c0, N, halo) in enumerate(groups):
            for q in range(4):
                P.dma('sync', xg[:, 4 * q:4 * q + 4, 0:N], xT[:, 4 * q:4 * q + 4, c0:c0 + N], [], [('xg', q)],
                      chan='xg%d' % q)
            P.dma('sync', og[:, :, 0:N], oT[:, :, c0:c0 + N], [], ['og'], chan='og')
            for j in range(4):
                bi = cur_w()
                for ct in range(4):
                    m = 4 * j + ct
                    pn, pt = rot.next()
                    P.mm(pt[:, 0:N], [(wb[bi][:, k * 512 + ct * 128:k * 512 + ct * 128 + 128], og[:, k, 0:N])
                                      for k in range(16)], [('wb', bi), 'og'], [pn])
                    P.cp('scalar', mixed[:, m, 0:N], pt[:, 0:N], [pn], [('mx', m)])
                load_next()
            rms_stats(P, lambda k: mixed[:, k, 0:N], 16, N, ones, sqb, 'ps_ss', ps_ss, rstd[:, 0:N], 'rstd',
                      lambda k: ('mx', k), 'C1')
            for m in range(16):
                P.stt('vector', mixed[:, m, 0:N], mixed[:, m, 0:N], g_sb[:, 0, m:m + 1], rstd[:, 0:N],
                      ALU.mult, ALU.mult, [('mx', m), 'g', 'rstd'], [('mx', m)])
                P.tt('vector', xg[:, m, 0:N], xg[:, m, 0:N], mixed[:, m, 0:N], ALU.add,
                     [('mx', m), ('xg', m // 4)], [('xg', m // 4)])
            rms_stats(P, lambda k: xg[:, k, 0:N], 16, N, ones, sqb, 'ps_ss', ps_ss, rstd[:, 0:N], 'rstd',
                      lambda k: ('xg', k // 4), 'C2')
            for m in range(16):
                P.stt('vector', hin2[:, m, 0:N], xg[:, m, 0:N], g_sb[:, 1, m:m + 1], rstd[:, 0:N],
                      ALU.mult, ALU.mult, [('xg', m // 4), 'g', 'rstd'], ['hin2'])
            for j in range(22):
                bi = cur_w()
                for ct in range(4):
                    ut = 4 * j + ct
                    i = 2 * j + ct // 2
                    pn, pt = rot.next()
                    P.mm(pt[:, 0:N], [(wb[bi][:, k * 512 + ct * 128:k * 512 + ct * 128 + 128], hin2[:, k, 0:N])
                                      for k in range(16)], [('wb', bi), 'hin2'], [pn])
                    if halo:
                        P.cp('scalar', utail[:, ut, :], pt[:, 0:2], [pn], [('ut', ut)])
                        continue
                    u = ubuf[ut % 2]
                    uk = ('ub', ut % 2)
                    c = cbuf[ut % 2]
                    ck = ('cb', ut % 2)
                    P.cp('scalar', u[:, 2:2 + N], pt[:, 0:N], [pn], [uk])
                    P.cp('gpsimd', u[:, 0:2], utail[:, ut, :], [('ut', ut)], [uk])
                    P.ts('vector', c[:, 0:N], u[:, 2:2 + N], cw[:, ut, 2:3], cw[:, ut, 3:4], ALU.mult, ALU.add,
                         [uk, 'cw'], [ck])
                    P.stt('vector', c[:, 0:N], u[:, 1:1 + N], cw[:, ut, 1:2], c[:, 0:N], ALU.mult, ALU.add,
                          [uk, 'cw', ck], [ck])
                    P.stt('vector', c[:, 0:N], u[:, 0:N], cw[:, ut, 0:1], c[:, 0:N], ALU.mult, ALU.add,
                          [uk, 'cw', ck], [ck])
                    P.cp('gpsimd', utail[:, ut, :], u[:, N:N + 2], [uk], [('ut', ut)])
                    if ct % 2 == 0:
                        gb = gbuf[i % 2]
                        P.act(gb[:, 0:N], c[:, 0:N], AF.Gelu_apprx_tanh, [ck], [('gb', i % 2)])
                    else:
                        gb = gbuf[i % 2]
                        P.tt('gpsimd', hT[:, i, 0:N], gb[:, 0:N], c[:, 0:N], ALU.mult, [('gb', i % 2), ck], [('h', i)])
                load_next()
            if halo:
                continue
            for m in range(16):
                bi = cur_w()
                pn, pt = rot.next()
                P.mm(pt[:, 0:N], [(wb[bi][:, i * 128:i * 128 + 128], hT[:, i, 0:N]) for i in range(44)],
                     [('wb', bi)] + [('h', i) for i in range(44)], [pn])
                P.cp('scalar', mixed[:, m, 0:N], pt[:, 0:N], [pn], [('mx', m)])
                load_next()
            rms_stats(P, lambda k: mixed[:, k, 0:N], 16, N, ones, sqb, 'ps_ss', ps_ss, rstd[:, 0:N], 'rstd',
                      lambda k: ('mx', k), 'C3')
            for m in range(16):
                P.stt('vector', mixed[:, m, 0:N], mixed[:, m, 0:N], g_sb[:, 2, m:m + 1], rstd[:, 0:N],
                      ALU.mult, ALU.mult, [('mx', m), 'g', 'rstd'], [('mx', m)])
                P.tt('vector', xg[:, m, 0:N], xg[:, m, 0:N], mixed[:, m, 0:N], ALU.add,
                     [('mx', m), ('xg', m // 4)], [('xg', m // 4)])
            for q in range(4):
                P.dma('sync', x_out[:, 4 * q:4 * q + 4, c0 - 2:c0 - 2 + N], xg[:, 4 * q:4 * q + 4, 0:N],
                      [('xg', q)], [('xo', gi, q)], chan='xo%d' % q)
        P.finish([k for k in P.last_w if isinstance(k, tuple) and k[0] == 'xo'])
        P.emit()
    return nc


BIGNEG = 30000.0


def causal_masks():
    s = np.arange(128)[:, None]
    t = np.arange(512)[None, :]
    le = np.stack([(128 * r + s <= t) for r in range(4)]).astype(np.float32)
    lt = np.stack([(128 * r + s < t) for r in range(4)]).astype(np.float32)
    win = np.stack([(t < s + 128 * r) for r in range(4)]).astype(np.float32)
    return le.astype(NPBF), lt.astype(NPBF), win.astype(NPBF)


class AttnCtx:
    def __init__(self, P):
        self.P = P
        self.pss = [("pss%d" % i, P.ps("pss%d" % i, [128, 512])) for i in range(3)]
        self.pso = [("pso%d" % i, P.ps("pso%d" % i, [128, 512])) for i in range(2)]
        self.psl = [("psl%d" % i, P.ps("psl%d" % i, [128, 512])) for i in range(2)]
        self.psx = ("psx", P.ps("psx", [128, 512]))
        self.pT = [("pT%d" % i, P.sb("pT%d" % i, [128, 512], BF16)) for i in range(3)]
        self.ones = P.sb("ones", [128, 128], BF16)
        P.memset('vector', self.ones[:, :], 1.0, ['ones'])
        self.si = 0
        self.gi = 0

    def next_s(self):
        t = self.pss[self.si % 3]
        p = self.pT[self.si % 3]
        self.si += 1
        return t, p

    def next_acc(self):
        o = self.pso[self.gi % 2]
        l = self.psl[self.gi % 2]
        self.gi += 1
        return o, l


def attn_group(cx, steps, want_l=True):
    P = cx.P
    (on, ot), (ln, lt) = cx.next_acc()
    n = len(steps)
    for i, st in enumerate(steps):
        (sn, stile), (pn, pt) = cx.next_s()
        P.mm(stile[:, :], st['pairs'], st['reads'], [sn])
        P.act(pt[:, :], stile[:, :], AF.Exp, [sn], [pn])
        if st.get('mask') is not None:
            P.tt('gpsimd', pt[:, :], pt[:, :], st['mask'], ALU.mult, [pn, st['mkey']], [pn])
        P.mm(ot[:, :], [(st['v'], pt[:, :])], [pn] + st['vreads'], [on], start=(i == 0), stop=(i == n - 1))
        if want_l:
            P.mm(lt[:, :], [(cx.ones[:, :], pt[:, :])], [pn, 'ones'], [ln], start=(i == 0), stop=(i == n - 1))
    return (on, ot), (ln, lt)


def load_qkv(P, nc, names, with_v=True):
    out = {}
    for nm in names:
        if nm.startswith('v'):
            d_ = nc.dram_tensor(nm, [128, 32, 128], BF16, kind="ExternalInput").ap()
            s_ = P.sb(nm + "_sb", [128, 32, 128], BF16)
            P.dma('sync', s_[:, :, :], d_, [], [nm], chan=nm)
        else:
            d_ = nc.dram_tensor(nm, [128, SEQ], BF16, kind="ExternalInput").ap()
            s_ = P.sb(nm + "_sb", [128, SEQ], BF16)
            P.dma('sync', s_[:, :], d_, [], [nm], chan=nm)
        out[nm] = s_
    return out


def load_masks(P, nc, name):
    d_ = nc.dram_tensor(name, [128, 4, 512], BF16, kind="ExternalInput").ap()
    s_ = P.sb(name + "_sb", [128, 4, 512], BF16)
    P.dma('sync', s_[:, :, :], d_, [], [name], chan=name)
    return s_


class OutStage:
    def __init__(self, P, out_ap, n=2):
        self.P = P
        self.out = out_ap
        self.bufs = [P.sb("ostg%d" % i, [128, 512], BF16) for i in range(n)]
        self.i = 0

    def next(self):
        i = self.i % len(self.bufs)
        self.i += 1
        return i, self.bufs[i]

    def store(self, i, g):
        self.P.dma('sync', self.out[:, g * 512:(g + 1) * 512], self.bufs[i][:, :], [('ostg', i)], [('out', g)],
                   chan='ostg%d' % i)


def build_fox():
    nc = bass.Bass("TRN2", target_bir_lowering=False)
    cf = nc.dram_tensor("cf", [6, SEQ], F32, kind="ExternalInput").ap()
    fb = nc.dram_tensor("fb", [6, 1], F32, kind="ExternalInput").ap()
    coef = nc.dram_tensor("coef", [6, 8], F32, kind="ExternalInput").ap()
    o_out = nc.dram_tensor("o_out", [128, SEQ], BF16, kind="ExternalOutput").ap()
    with ExitStack() as es:
        P = Prog(nc, es)
        cx = AttnCtx(P)
        T = load_qkv(P, nc, ['q', 'k', 'v'])
        nmask = load_masks(P, nc, 'nmask')
        ident_d = nc.dram_tensor("ident", [128, 128], BF16, kind="ExternalInput").ap()
        ident = P.sb("ident_sb", [128, 128], BF16)
        P.dma('sync', ident[:, :], ident_d, [], ['ident'], chan='ident')
        ost = OutStage(P, o_out)
        cf_sb = P.sb("cf_sb", [6, SEQ], F32)
        w1 = P.sb("w1", [6, SEQ], F32)
        w2 = P.sb("w2", [6, SEQ], F32)
        w3 = P.sb("w3", [6, SEQ], F32)
        hb = P.sb("hb", [6, SEQ], BF16)
        hi = P.sb("hi", [6, SEQ], F32)
        mid = P.sb("mid", [6, SEQ], F32)
        lo = P.sb("lo", [6, SEQ], F32)
        kaug = P.sb("kaug", [6, SEQ], BF16)
        qaug = P.sb("qaug", [6, SEQ], BF16)
        fb_sb = P.sb("fb_sb", [6, 1], F32)
        co = P.sb("co", [6, 8], F32)
        rl = P.sb("rl", [128, 512], F32)
        P.dma('sync', cf_sb[:, :], cf, [], ['cf'], chan='cf')
        P.dma('sync', fb_sb[:, :], fb, [], ['fb'], chan='fb')
        P.dma('sync', co[:, :], coef, [], ['co'], chan='co')
        P.ts('vector', fb_sb[:, :], fb_sb[:, :], -1.0, None, ALU.mult, None, ['fb'], ['fb'])
        P.act(w1[:, :], cf_sb[:, :], AF.Exp, ['cf', 'fb'], ['w1'], bias=fb_sb[:, 0:1], scale=-1.0)
        P.act(w1[:, :], w1[:, :], AF.Ln, ['w1'], ['w1'], bias=1.0)
        P.ts('vector', w1[:, :], w1[:, :], -1.0, None, ALU.mult, None, ['w1'], ['w1'])
        P.memset('vector', w2[:, :], 1.0, ['w2'])
        P.op('vector', lambda e: e.tensor_tensor_scan(out=w3[:, :], data0=w2[:, :], data1=w1[:, :], initial=0.0,
                                                      op0=ALU.mult, op1=ALU.add), ['w1', 'w2'], ['w3'])
        P.cp('vector', hb[:, :], w3[:, :], ['w3'], ['hb'])
        P.cp('vector', hi[:, :], hb[:, :], ['hb'], ['hi'])
        P.tt('vector', w1[:, :], w3[:, :], hi[:, :], ALU.subtract, ['w3', 'hi'], ['w1'])
        P.cp('vector', hb[:, :], w1[:, :], ['w1'], ['hb'])
        P.cp('vector', mid[:, :], hb[:, :], ['hb'], ['mid'])
        P.tt('vector', w2[:, :], w1[:, :], mid[:, :], ALU.subtract, ['w1', 'mid'], ['w2'])
        P.cp('vector', hb[:, :], w2[:, :], ['w2'], ['hb'])
        P.cp('vector', lo[:, :], hb[:, :], ['hb'], ['lo'])
        for dst, dk, c0 in ((kaug, 'kaug', 0), (qaug, 'qaug', 4)):
            P.ts('vector', w1[:, :], hi[:, :], co[:, c0:c0 + 1], co[:, c0 + 3:c0 + 4], ALU.mult, ALU.add,
                 ['hi', 'co'], ['w1'])
            P.stt('vector', w2[:, :], mid[:, :], co[:, c0 + 1:c0 + 2], w1[:, :], ALU.mult, ALU.add,
                  ['mid', 'co', 'w1'], ['w2'])
            P.stt('vector', dst[:, :], lo[:, :], co[:, c0 + 2:c0 + 3], w2[:, :], ALU.mult, ALU.add,
                  ['lo', 'co', 'w2'], [dk])
        q, k, v = T['q'], T['k'], T['v']
        for g in range(8):
            qs = slice(g * 512, g * 512 + 512)
            steps = []
            for S in range(4 * g + 4):
                ks = slice(S * 128, S * 128 + 128)
                st = dict(pairs=[(k[:, ks], q[:, qs]), (kaug[:, ks], qaug[:, qs])], reads=['q', 'k', 'kaug', 'qaug'],
                          v=v[:, S, :], vreads=['v'])
                if S >= 4 * g:
                    st['pairs'].append((ident[:, :], nmask[:, S - 4 * g, :]))
                    st['reads'] += ['ident', 'nmask']
                steps.append(st)
            (on, ot), (ln, lt) = attn_group(cx, steps)
            P.op('vector', lambda e, lt=lt: e.reciprocal(out=rl[:, :], in_=lt[:, :]), [ln], ['rl'])
            i, sb_ = ost.next()
            P.tt('vector', sb_[:, :], ot[:, :], rl[:, :], ALU.mult, [on, 'rl'], [('ostg', i)])
            ost.store(i, g)
        P.finish([('out', g) for g in range(8)])
        P.emit()
    return nc


def build_diff():
    nc = bass.Bass("TRN2", target_bir_lowering=False)
    lamp = nc.dram_tensor("lamp", [128, 256], F32, kind="ExternalInput").ap()
    dn = nc.dram_tensor("dn", [128, 2], F32, kind="ExternalInput").ap()
    o_out = nc.dram_tensor("o_out", [128, SEQ], BF16, kind="ExternalOutput").ap()
    with ExitStack() as es:
        P = Prog(nc, es)
        cx = AttnCtx(P)
        T = load_qkv(P, nc, ['q', 'k', 'v'])
        mle = load_masks(P, nc, 'mle')
        ost = OutStage(P, o_out)
        lp = P.sb("lp", [128, 256], F32)
        dn_sb = P.sb("dn_sb", [128, 2], F32)
        pr = P.sb("pr", [128, 128], F32)
        sc = P.sb("sc", [128, 8], F32)
        rl = P.sb("rl", [128, 512], F32)
        o1 = P.sb("o1", [128, 512], F32)
        o2 = P.sb("o2", [128, 512], F32)
        sq = P.sb("sq", [128, 512], BF16)
        rstd = P.sb("rstd", [128, 512], F32)
        P.dma('sync', lp[:, :], lamp, [], ['lp'], chan='lp')
        P.dma('sync', dn_sb[:, :], dn, [], ['dn'], chan='dn')
        P.tt('vector', pr[:, 0:64], lp[:, 0:64], lp[:, 64:128], ALU.mult, ['lp'], ['pr'])
        P.tt('vector', pr[:, 64:128], lp[:, 128:192], lp[:, 192:256], ALU.mult, ['lp', 'pr'], ['pr'])
        P.op('vector', lambda e: e.reduce_sum(out=sc[:, 0:1], in_=pr[:, 0:64], axis=AX.X), ['pr'], ['sc'])
        P.op('vector', lambda e: e.reduce_sum(out=sc[:, 1:2], in_=pr[:, 64:128], axis=AX.X), ['pr', 'sc'], ['sc'])
        P.act(sc[:, 2:4], sc[:, 0:2], AF.Exp, ['sc'], ['sc'])
        P.tt('vector', sc[:, 4:5], sc[:, 3:4], sc[:, 2:3], ALU.subtract, ['sc'], ['sc'])
        P.tt('vector', sc[:, 4:5], sc[:, 4:5], dn_sb[:, 1:2], ALU.subtract, ['sc', 'dn'], ['sc'])
        P.ts('vector', sc[:, 5:6], dn_sb[:, 1:2], -1.0, 1.0, ALU.mult, ALU.add, ['dn', 'sc'], ['sc'])
        P.tt('vector', sc[:, 5:6], sc[:, 5:6], dn_sb[:, 0:1], ALU.mult, ['sc', 'dn'], ['sc'])
        q, k, v = T['q'], T['k'], T['v']
        for g in range(8):
            qs = slice(g * 512, g * 512 + 512)
            for half, od, ok_ in ((0, o1, 'o1'), (1, o2, 'o2')):
                hs = slice(64 * half, 64 * half + 64)
                steps = []
                for S in range(4 * g + 4):
                    ks = slice(S * 128, S * 128 + 128)
                    st = dict(pairs=[(k[hs, ks], q[hs, qs])], reads=['q', 'k'], v=v[:, S, :], vreads=['v'])
                    if S >= 4 * g:
                        st['mask'] = mle[:, S - 4 * g, :]
                        st['mkey'] = 'mle'
                    steps.append(st)
                (on, ot), (ln, lt) = attn_group(cx, steps)
                P.op('vector', lambda e, lt=lt: e.reciprocal(out=rl[:, :], in_=lt[:, :]), [ln], ['rl'])
                P.tt('vector', od[:, :], ot[:, :], rl[:, :], ALU.mult, [on, 'rl'], [ok_])
            P.stt('vector', o1[:, :], o2[:, :], sc[:, 4:5], o1[:, :], ALU.mult, ALU.add, ['o1', 'o2', 'sc'], ['o1'])
            P.act(sq[:, :], o1[:, :], AF.Square, ['o1'], ['sq'])
            xn, xt = cx.psx
            P.mm(xt[:, :], [(cx.ones[:, :], sq[:, :])], ['sq', 'ones'], [xn])
            P.act(rstd[:, :], xt[:, :], AF.Sqrt, [xn], ['rstd'], bias=EPS, scale=1.0 / 128)
            P.op('vector', lambda e: e.reciprocal(out=rstd[:, :], in_=rstd[:, :]), ['rstd'], ['rstd'])
            i, sb_ = ost.next()
            P.stt('vector', sb_[:, :], o1[:, :], sc[:, 5:6], rstd[:, :], ALU.mult, ALU.mult,
                  ['o1', 'sc', 'rstd'], [('ostg', i)])
            ost.store(i, g)
        P.finish([('out', g) for g in range(8)])
        P.emit()
    return nc


def build_sb():
    nc = bass.Bass("TRN2", target_bir_lowering=False)
    trid = nc.dram_tensor("ntri", [128, 128], BF16, kind="ExternalInput").ap()
    o_out = nc.dram_tensor("o_out", [128, SEQ], BF16, kind="ExternalOutput").ap()
    with ExitStack() as es:
        P = Prog(nc, es)
        cx = AttnCtx(P)
        T = load_qkv(P, nc, ['q', 'k', 'v'])
        mlt = load_masks(P, nc, 'mlt')
        ost = OutStage(P, o_out)
        ntri = P.sb("ntri_sb", [128, 128], BF16)
        nones = P.sb("nones", [128, 128], BF16)
        ebuf = [P.sb("ebuf%d" % i, [128, 512], F32) for i in range(2)]
        Lm = [P.sb("Lm%d" % i, [128, 512], BF16) for i in range(3)]
        Lsum = P.sb("Lsum", [128, 512], F32)
        Lsb = [P.sb("Lsb%d" % i, [128, 512], BF16) for i in range(2)]
        P.dma('sync', ntri[:, :], trid, [], ['ntri'], chan='ntri')
        P.memset('vector', nones[:, :], -1.0, ['nones'])
        q, k, v = T['q'], T['k'], T['v']
        cnt = 0
        for g in range(8):
            qs = slice(g * 512, g * 512 + 512)
            (on, ot), _ = cx.next_acc()
            Slist = list(range(4 * g + 3, -1, -1))
            for i, S in enumerate(Slist):
                ks = slice(S * 128, S * 128 + 128)
                (zn, zt), (pn, pt) = cx.next_s()
                (an, at), _ = cx.next_s()
                e_ = ebuf[cnt % 2]
                ek = ('e', cnt % 2)
                L_ = Lm[cnt % 3]
                Lk = ('L', cnt % 3)
                P.mm(zt[:, :], [(k[:, ks], q[:, qs])], ['q', 'k'], [zn])
                P.act(e_[:, :], zt[:, :], AF.Exp, [zn], [ek])
                P.act(L_[:, :], e_[:, :], AF.Ln, [ek], [Lk], bias=1.0)
                diag = S >= 4 * g
                if diag:
                    P.tt('gpsimd', L_[:, :], L_[:, :], mlt[:, S - 4 * g, :], ALU.mult, [Lk, 'mlt'], [Lk])
                pairs = [(k[:, ks], q[:, qs]), (ntri[:, :], L_[:, :])]
                rd = ['q', 'k', 'ntri', Lk]
                if i > 0:
                    lb = Lsb[(cnt - 1) % 2]
                    pairs.append((nones[:, :], lb[:, :]))
                    rd += ['nones', ('Lsb', (cnt - 1) % 2)]
                P.mm(at[:, :], pairs, rd, [an])
                P.act(pt[:, :], at[:, :], AF.Exp, [an], [pn])
                if diag:
                    P.tt('gpsimd', pt[:, :], pt[:, :], mlt[:, S - 4 * g, :], ALU.mult, [pn, 'mlt'], [pn])
                P.mm(ot[:, :], [(v[:, S, :], pt[:, :])], [pn, 'v'], [on], start=(i == 0), stop=(i == len(Slist) - 1))
                if i < len(Slist) - 1:
                    if i == 0:
                        P.cp('vector', Lsum[:, :], L_[:, :], [Lk], ['Lsum'])
                    else:
                        P.tt('vector', Lsum[:, :], Lsum[:, :], L_[:, :], ALU.add, [Lk, 'Lsum'], ['Lsum'])
                    P.cp('vector', Lsb[cnt % 2][:, :], Lsum[:, :], ['Lsum'], [('Lsb', cnt % 2)])
                cnt += 1
            j, sb_ = ost.next()
            P.cp('vector', sb_[:, :], ot[:, :], [on], [('ostg', j)])
            ost.store(j, g)
        P.finish([('out', g) for g in range(8)])
        P.emit()
    return nc


def tm_layout(v):
    return np.ascontiguousarray(v.reshape(32, 128, 128).transpose(1, 0, 2))


def fox_coef():
    co = np.zeros((6, 8), np.float32)
    co[0, 0] = -1; co[1, 1] = -1; co[2, 2] = -1; co[3:6, 3] = 1
    co[0:3, 7] = 1; co[3, 4] = 1; co[4, 5] = 1; co[5, 6] = 1
    return co


def ntri_const():
    j = np.arange(128)[:, None]
    s = np.arange(128)[None, :]
    return (-(j >= s).astype(np.float32)).astype(NPBF)


def mask_layout(m):
    return np.ascontiguousarray(m.transpose(1, 0, 2))


def nsa_consts():
    c = np.arange(128)[:, None]
    t = np.arange(512)[None, :]
    cm = []
    for g in range(5):
        cm.append((16 * c + 31 <= 512 * g + t))
    for g in range(4, 8):
        cm.append((16 * (c + 128) + 31 <= 512 * g + t) & (c + 128 < N_C))
    cmask = np.stack(cm).astype(np.float32).astype(NPBF)
    overlap = np.zeros((256, 64), np.float32)
    for r in range(2):
        sub = np.arange(N_C) * 16 + r * 16
        np.add.at(overlap, (np.arange(N_C), sub // 64), 1.0)
    ovl = np.ascontiguousarray(overlap.reshape(2, 128, 64).transpose(1, 0, 2)).astype(NPBF)
    tt_ = np.arange(SEQ)[:, None]
    j = np.arange(64)[None, :]
    cur = tt_ // 64
    valid = j <= cur
    forced = valid & ((j == 0) | (j >= cur - 1))
    A = (valid & ~forced).astype(np.float32)
    Bm = np.where(forced, np.float32(1e9), np.where(valid, np.float32(0), np.float32(-1e30))).astype(np.float32)
    ab = np.stack([A.reshape(32, 128, 64).transpose(1, 0, 2), Bm.reshape(32, 128, 64).transpose(1, 0, 2)], axis=1)
    eall = ((np.arange(SEQ)[None, :] // 64) == np.arange(64)[:, None]).astype(np.float32) * BIGNEG
    ident = np.eye(128, dtype=np.float32)
    return dict(cmask=np.ascontiguousarray(cmask.transpose(1, 0, 2)), ovl=ovl, absel=np.ascontiguousarray(ab),
                eall=eall.astype(NPBF), ident=ident.astype(NPBF))


def build_nsa():
    nc = bass.Bass("TRN2", target_bir_lowering=False)
    gates_d = nc.dram_tensor("gates", [3, SEQ], F32, kind="ExternalInput").ap()
    w1k_d = nc.dram_tensor("w1k", [128, 8192], F32, kind="ExternalInput").ap()
    w1v_d = nc.dram_tensor("w1v", [128, 8192], F32, kind="ExternalInput").ap()
    w2_d = nc.dram_tensor("w2", [128, 4, 128], F32, kind="ExternalInput").ap()
    pe_d = nc.dram_tensor("pe", [128, 64], F32, kind="ExternalInput").ap()
    cmask_d = nc.dram_tensor("cmask", [128, 9, 512], BF16, kind="ExternalInput").ap()
    ovl_d = nc.dram_tensor("ovl", [128, 2, 64], BF16, kind="ExternalInput").ap()
    ab_d = nc.dram_tensor("absel", [128, 2, 32, 64], F32, kind="ExternalInput").ap()
    eall_d = nc.dram_tensor("eall", [64, SEQ], BF16, kind="ExternalInput").ap()
    ident_d = nc.dram_tensor("ident", [128, 128], BF16, kind="ExternalInput").ap()
    o_out = nc.dram_tensor("o_out", [128, SEQ], BF16, kind="ExternalOutput").ap()
    with ExitStack() as es:
        P = Prog(nc, es)
        cx = AttnCtx(P)
        T = load_qkv(P, nc, ['aqr', 'aq0', 'aq1', 'aq2', 'aq3', 'ks', 'kw', 'vs', 'vw'])
        xkv_d = [nc.dram_tensor(nm, [128, SEQ], BF16, kind="ExternalInput").ap() for nm in ('xk', 'xv')]
        xkv = P.sb("xkv", [128, SEQ], BF16)
        mle = load_masks(P, nc, 'mle')
        mwin = load_masks(P, nc, 'mwin')
        ost = OutStage(P, o_out)
        cmask = P.sb("cmask_sb", [128, 9, 512], BF16)
        ovl = P.sb("ovl_sb", [128, 2, 64], BF16)
        ab = P.sb("ab_sb", [128, 2, 32, 64], F32)
        eall = P.sb("eall_sb", [64, SEQ], BF16)
        ident = P.sb("ident_sb", [128, 128], BF16)
        selT = P.sb("selT", [64, SEQ], BF16)
        res = P.sb("res", [128, SEQ], F32)
        w1 = P.sb("w1", [128, 8192], BF16)
        w2 = P.sb("w2_sb", [128, 4, 128], BF16)
        pe32 = P.sb("pe32", [128, 64], F32)
        peb = P.sb("peb", [128, 64], BF16)
        bias = P.sb("bias", [128, 4], F32)
        hid = [P.sb("hid%d" % j, [128, 256], BF16) for j in range(2)]
        kcT = P.sb("kcT", [128, 256], BF16)
        vc = P.sb("vc", [128, 2, 128], BF16)
        ecm = [P.sb("ecm%d" % i, [128, 512], BF16) for i in range(4)]
        PT = [P.sb("PT%d" % i, [128, 512], F32) for i in range(2)]
        PTb = [P.sb("PTb%d" % i, [128, 512], BF16) for i in range(2)]
        tmp = P.sb("tmp", [128, 512], F32)
        rl = P.sb("rl", [128, 512], F32)
        gsb = [P.sb("gsb%d" % i, [128, 512], F32) for i in range(3)]
        sc = P.sb("sc", [128, 64], F32)
        sc2 = P.sb("sc2", [128, 64], F32)
        m8 = P.sb("m8", [128, 16], F32)
        selm = P.sb("selm", [128, 64], BF16)
        for s_, d_, k_ in ((cmask[:, :, :], cmask_d, 'cmask'), (ovl[:, :, :], ovl_d, 'ovl'), (ab[:, :, :, :], ab_d, 'ab'),
                           (eall[:, :], eall_d, 'eall'), (ident[:, :], ident_d, 'ident'), (pe32[:, :], pe_d, 'pe32')):
            P.dma('sync', s_, d_, [], [k_], chan=k_)
        P.dma('gpsimd', w2[:, :, :], w2_d, [], ['w2'], chan='w2')
        P.cp('vector', peb[:, :], pe32[:, :], ['pe32'], ['peb'])
        P.memset('vector', kcT[:, :], 0.0, ['kcT'])
        P.memset('vector', hid[0][:, :], 0.0, [('hid', 0)])
        P.memset('vector', hid[1][:, :], 0.0, [('hid', 1)])
        xn, xt = cx.psx
        for kv in range(2):
            P.dma('gpsimd', w1[:, :], (w1k_d, w1v_d)[kv], [], ['w1'], chan='w1')
            P.dma('sync', xkv[:, :], xkv_d[kv], [], ['xkv'], chan='xkv')
            xr = xkv[:, :].rearrange("p (c r) -> p c r", r=16)
            for j in range(2):
                P.mm(xt[:, 0:1], [(w1[:, l * 256 + j * 128:l * 256 + j * 128 + 128], peb[:, kv * 32 + l:kv * 32 + l + 1])
                                  for l in range(32)], ['w1', 'peb'], [xn])
                P.cp('vector', bias[:, 2 * kv + j:2 * kv + j + 1], xt[:, 0:1], [xn], ['bias'])
                (sn, st), _ = cx.next_s()
                P.mm(st[:, 0:255], [(w1[:, l * 256 + j * 128:l * 256 + j * 128 + 128],
                                     (xr[:, 0:255, l] if l < 16 else xr[:, 1:256, l - 16])) for l in range(32)],
                     ['w1', 'xkv'], [sn])
                P.act(hid[j][:, 0:255], st[:, 0:255], AF.Gelu_apprx_tanh, [sn, 'bias'], [('hid', j)],
                      bias=bias[:, 2 * kv + j:2 * kv + j + 1])
            if kv == 0:
                P.mm(xt[:, 0:255], [(w2[:, j, :], hid[j][:, 0:255]) for j in range(2)],
                     ['w2', ('hid', 0), ('hid', 1)], [xn])
                P.cp('vector', kcT[:, 0:255], xt[:, 0:255], [xn], ['kcT'])
            else:
                for ct in range(2):
                    P.mm(xt[:, 0:128], [(hid[j][:, ct * 128:ct * 128 + 128], w2[:, 2 + j, :]) for j in range(2)],
                         ['w2', ('hid', 0), ('hid', 1)], [xn])
                    P.cp('vector', vc[:, ct, :], xt[:, 0:128], [xn], ['vc'])

        def load_gate(r, g):
            P.dma('sync', gsb[r][:, :], gates_d[r:r + 1, g * 512:(g + 1) * 512].partition_broadcast(128), [],
                  [('gs', r)], chan='gs%d' % r)
            P.act(gsb[r][:, :], gsb[r][:, :], AF.Sigmoid, [('gs', r)], [('gs', r)])
        aq = [T['aq%d' % h] for h in range(4)]
        ei = 0
        for g in range(8):
            qs = slice(g * 512, g * 512 + 512)
            cts = [0] if g < 4 else [0, 1]
            load_gate(0, g)
            for h in range(4):
                (on, ot), (ln, lt) = cx.next_acc()
                es_ = []
                for ci, ct in enumerate(cts):
                    (sn, st), _ = cx.next_s()
                    P.mm(st[:, :], [(kcT[:, ct * 128:ct * 128 + 128], aq[h][:, qs])], ['kcT', 'aq%d' % h], [sn])
                    e_ = ecm[ei % 4]
                    ek = ('ecm', ei % 4)
                    ei += 1
                    P.act(e_[:, :], st[:, :], AF.Exp, [sn], [ek])
                    mi = None
                    if ct == 0 and g <= 4:
                        mi = g
                    elif ct == 1:
                        mi = 5 + (g - 4)
                    if mi is not None:
                        P.tt('gpsimd', e_[:, :], e_[:, :], cmask[:, mi, :], ALU.mult, [ek, 'cmask'], [ek])
                    P.mm(lt[:, :], [(cx.ones[:, :], e_[:, :])], [ek, 'ones'], [ln], start=(ci == 0),
                         stop=(ci == len(cts) - 1))
                    if h == 0:
                        P.mm(ot[:, :], [(vc[:, ct, :], e_[:, :])], [ek, 'vc'], [on], start=(ci == 0),
                             stop=(ci == len(cts) - 1))
                    es_.append((ct, e_, ek))
                P.ts('vector', rl[:, :], lt[:, :], 1e-30, None, ALU.max, None, [ln], ['rl'])
                P.op('vector', lambda e: e.reciprocal(out=rl[:, :], in_=rl[:, :]), ['rl'], ['rl'])
                for ct, e_, ek in es_:
                    if h == 0:
                        P.tt('vector', PT[ct][:, :], e_[:, :], rl[:, :], ALU.mult, [ek, 'rl'], [('PT', ct)])
                    else:
                        P.tt('vector', tmp[:, :], e_[:, :], rl[:, :], ALU.mult, [ek, 'rl'], ['tmp'])
                        P.tt('vector', PT[ct][:, :], PT[ct][:, :], tmp[:, :], ALU.add, ['tmp', ('PT', ct)], [('PT', ct)])
                if h == 0:
                    P.tt('vector', tmp[:, :], ot[:, :], rl[:, :], ALU.mult, [on, 'rl'], ['tmp'])
                    P.tt('vector', res[:, qs], tmp[:, :], gsb[0][:, :], ALU.mult, ['tmp', ('gs', 0)], [('res', g)])
            for ct in cts:
                P.cp('gpsimd', PTb[ct][:, :], PT[ct][:, :], [('PT', ct)], [('PTb', ct)])
            for qt in range(4):
                Tq = 4 * g + qt
                P.mm(xt[:, 0:64], [(PTb[ct][:, qt * 128:qt * 128 + 128], ovl[:, ct, :]) for ct in cts],
                     [('PTb', ct) for ct in cts] + ['ovl'], [xn])
                P.tt('vector', sc[:, :], xt[:, 0:64], ab[:, 0, Tq, :], ALU.mult, [xn, 'ab'], ['sc'])
                P.tt('vector', sc[:, :], sc[:, :], ab[:, 1, Tq, :], ALU.add, ['sc', 'ab'], ['sc'])
                P.op('vector', lambda e: e.max(out=m8[:, 0:8], in_=sc[:, :]), ['sc'], ['m8'])
                P.op('vector', lambda e: e.match_replace(out=sc2[:, :], in_to_replace=m8[:, 0:8], in_values=sc[:, :],
                                                         imm_value=-1e30), ['sc', 'm8'], ['sc2'])
                P.op('vector', lambda e: e.max(out=m8[:, 8:16], in_=sc2[:, :]), ['sc2', 'm8'], ['m8'])
                P.ts('vector', selm[:, :], sc[:, :], m8[:, 15:16], -1.0, ALU.is_ge, ALU.add, ['sc', 'm8'], ['selm'])
                P.mm(xt[0:64, 128:256], [(selm[:, :], ident[:, :])], ['selm', 'ident'], [xn])
                P.cp('vector', selT[:, Tq * 128:Tq * 128 + 128], xt[0:64, 128:256], [xn], [('selT', g)])
        for phase in (3, 4):
            r_ = phase - 2
            kk = T['ks'] if phase == 3 else T['kw']
            kn = 'ks' if phase == 3 else 'kw'
            vv = T['vs'] if phase == 3 else T['vw']
            vn = 'vs' if phase == 3 else 'vw'
            for g in range(8):
                qs = slice(g * 512, g * 512 + 512)
                load_gate(r_, g)
                steps = []
                S0 = 0 if phase == 3 else max(0, 4 * g - 4)
                for S in range(S0, 4 * g + 4):
                    ks = slice(S * 128, S * 128 + 128)
                    pairs = [(kk[:, ks], T['aqr'][:, qs])]
                    rd = [kn, 'aqr']
                    if phase == 3:
                        pairs.append((eall[:, ks], selT[:, qs]))
                        rd += ['eall', ('selT', g)]
                    st = dict(pairs=pairs, reads=rd, v=vv[:, S, :], vreads=[vn])
                    if S >= 4 * g:
                        st['mask'] = mle[:, S - 4 * g, :]
                        st['mkey'] = 'mle'
                    elif phase == 4:
                        st['mask'] = mwin[:, S - (4 * g - 4), :]
                        st['mkey'] = 'mwin'
                    steps.append(st)
                (on, ot), (ln, lt) = attn_group(cx, steps)
                P.op('vector', lambda e, lt=lt: e.reciprocal(out=rl[:, :], in_=lt[:, :]), [ln], ['rl'])
                P.tt('vector', tmp[:, :], ot[:, :], rl[:, :], ALU.mult, [on, 'rl'], ['tmp'])
                P.tt('vector', tmp[:, :], tmp[:, :], gsb[r_][:, :], ALU.mult, ['tmp', ('gs', r_)], ['tmp'])
                if phase == 3:
                    P.tt('vector', res[:, qs], res[:, qs], tmp[:, :], ALU.add, ['tmp', ('res', g)], [('res', g)])
                else:
                    i, sb_ = ost.next()
                    P.tt('vector', sb_[:, :], res[:, qs], tmp[:, :], ALU.add, ['tmp', ('res', g)], [('ostg', i)])
                    ost.store(i, g)
        P.finish([('out', g) for g in range(8)])
        P.emit()
    return nc


def nsa_weights(l, p):
    w1k = np.ascontiguousarray(p['nsa_cmp_k1'][l].reshape(32, 128, 256).transpose(1, 0, 2).reshape(128, 8192))
    w1v = np.ascontiguousarray(p['nsa_cmp_v1'][l].reshape(32, 128, 256).transpose(1, 0, 2).reshape(128, 8192))
    w2 = np.stack([p['nsa_cmp_k2'][l][0:128], p['nsa_cmp_k2'][l][128:256],
                   p['nsa_cmp_v2'][l][0:128], p['nsa_cmp_v2'][l][128:256]], axis=1)
    pe = np.concatenate([p['nsa_pe_k'][l].T, p['nsa_pe_v'][l].T], axis=1)
    return dict(w1k=w1k, w1v=w1v, w2=np.ascontiguousarray(w2.astype(np.float32)),
                pe=np.ascontiguousarray(pe.astype(np.float32)))


_PROGS = {}


def _prog(name):
    if name not in _PROGS:
        _PROGS[name] = dict(A=build_A, C=build_C, nsa=build_nsa, sb=build_sb, fox=build_fox, diff=build_diff)[name]()
    return _PROGS[name]


def _run(name, in_maps):
    res = run_bass_kernel_spmd(_prog(name), in_maps, core_ids=list(range(8)))
    return res.results


def _halo(arr_b, j, n):
    t0 = j * n
    if j == 0:
        sl = np.concatenate([np.zeros((2, arr_b.shape[1]), arr_b.dtype), arr_b[0:n]], axis=0)
    else:
        sl = arr_b[t0 - 2:t0 + n]
    return np.ascontiguousarray(sl.T.reshape(16, 128, n + 2).transpose(1, 0, 2))


def kernel(x, norm_mix_pre, norm_mix_post, norm_mlp_pre, norm_mlp_post, w_in, w_out,
           nsa_pe_k, nsa_pe_v, nsa_cmp_k1, nsa_cmp_k2, nsa_cmp_v1, nsa_cmp_v2,
           fox_forget_bias, diff_lambda, diff_norm, mlp_w_up, mlp_conv_w, mlp_conv_b, mlp_w_down):
    p = dict(nsa_pe_k=np.asarray(nsa_pe_k), nsa_pe_v=np.asarray(nsa_pe_v), nsa_cmp_k1=np.asarray(nsa_cmp_k1),
             nsa_cmp_k2=np.asarray(nsa_cmp_k2), nsa_cmp_v1=np.asarray(nsa_cmp_v1), nsa_cmp_v2=np.asarray(nsa_cmp_v2))
    x = np.array(x, dtype=np.float32, copy=True)
    w_in = np.asarray(w_in); w_out = np.asarray(w_out); mlp_w_up = np.asarray(mlp_w_up)
    mlp_w_down = np.asarray(mlp_w_down); mlp_conv_w = np.asarray(mlp_conv_w); mlp_conv_b = np.asarray(mlp_conv_b)
    le, lt, win = causal_masks()
    mle, mlt, mwin = mask_layout(le), mask_layout(lt), mask_layout(win)
    nmask = mask_layout(((le.astype(np.float32) - 1.0) * BIGNEG).astype(NPBF))
    ident = np.eye(128, dtype=np.float32).astype(NPBF)
    ncst = nsa_consts()
    ntri = ntri_const()
    coef = fox_coef()
    tabs = [rope_tables(j * TOK, TOK) for j in range(4)]

    def fmaj(v):
        return np.ascontiguousarray(np.asarray(v, np.float32).reshape(16, 128).T)
    for l in range(DEPTH):
        wblk = host_w_in_blocks(w_in[l]).reshape(16, 128, 8192)
        gpre = fmaj(norm_mix_pre[l])
        in_maps = []
        for c in range(8):
            b, j = c // 4, c % 4
            xs = x[b, j * TOK:(j + 1) * TOK, :]
            xT = np.ascontiguousarray(xs.T.reshape(16, 128, TOK).transpose(1, 0, 2))
            in_maps.append(dict(xT=xT, gpre=gpre, wblk=wblk, tabs=tabs[j]))
        rA = _run('A', in_maps)
        del wblk, in_maps
        fm = [np.concatenate([np.asarray(rA[4 * b + j]['fm_out']) for j in range(4)], axis=2) for b in range(NB)]
        sm = [np.concatenate([np.asarray(rA[4 * b + j]['sm_out'])[0:16] for j in range(4)], axis=1) for b in range(NB)]
        tm = [np.concatenate([np.asarray(rA[4 * b + j]['tm_out']) for j in range(4)], axis=0) for b in range(NB)]
        del rA
        nw = nsa_weights(l, p)
        lam_init = 0.8 - 0.6 * math.exp(-0.3 * l)
        lamp = np.ascontiguousarray(np.tile(np.asarray(diff_lambda[l], np.float32).reshape(1, 256), (128, 1)))
        dn = np.ascontiguousarray(np.stack([np.asarray(diff_norm[l], np.float32),
                                            np.full(128, lam_init, np.float32)], axis=1))
        maps = dict(nsa=[], sb=[], fox=[], diff=[])
        for c in range(8):
            b, h = c // 4, c % 4
            f, s_, t_ = fm[b], sm[b], tm[b]
            order = [h] + [i for i in range(4) if i != h]
            m = dict(aqr=f[h], xk=f[8], xv=f[9], ks=f[10], kw=f[11], vs=tm_layout(t_[:, 0:128]),
                     vw=tm_layout(t_[:, 128:256]), gates=np.ascontiguousarray(s_[3 * h:3 * h + 3]), mle=mle, mwin=mwin)
            for i, hh in enumerate(order):
                m['aq%d' % i] = f[4 + hh]
            m.update(ncst)
            m.update(nw)
            maps['nsa'].append(m)
            maps['sb'].append(dict(q=f[12 + h], k=f[16 + h], v=tm_layout(t_[:, 256 + 128 * h:384 + 128 * h]),
                                   mlt=mlt, ntri=ntri))
            maps['fox'].append(dict(q=f[20 + h], k=f[24 + h], v=tm_layout(t_[:, 768 + 128 * h:896 + 128 * h]),
                                    nmask=nmask, ident=ident,
                                    cf=np.ascontiguousarray(np.tile(s_[12 + h][None, :], (6, 1))),
                                    fb=np.full((6, 1), np.asarray(fox_forget_bias)[l][h], np.float32), coef=coef))
            maps['diff'].append(dict(q=f[28 + h], k=f[32 + h], v=tm_layout(t_[:, 1280 + 128 * h:1408 + 128 * h]),
                                     mle=mle, lamp=lamp, dn=dn))
        o_b = [np.zeros((SEQ, D), NPBF) for _ in range(NB)]
        for mi, name in enumerate(('nsa', 'sb', 'fox', 'diff')):
            r = _run(name, maps[name])
            for c in range(8):
                b, h = c // 4, c % 4
                o_b[b][:, mi * 512 + h * 128:mi * 512 + h * 128 + 128] = np.asarray(r[c]['o_out']).T
            maps[name] = None
        del fm, sm, tm
        blocks, cw = host_c_weights(w_out[l], mlp_w_up[l], mlp_w_down[l], mlp_conv_w[l], mlp_conv_b[l])
        gn = np.ascontiguousarray(np.stack([fmaj(norm_mix_post[l]), fmaj(norm_mlp_pre[l]), fmaj(norm_mlp_post[l])],
                                           axis=1))
        in_maps = []
        for c in range(8):
            b, j = c // 4, c % 4
            in_maps.append(dict(xT=_halo(x[b], j, TOK), oT=_halo(o_b[b], j, TOK), gn=gn, cw=cw, wblk=blocks))
        rC = _run('C', in_maps)
        del blocks, in_maps
        for c in range(8):
            b, j = c // 4, c % 4
            xo = np.asarray(rC[c]['x_out'])
            x[b, j * TOK:(j + 1) * TOK, :] = xo.transpose(1, 0, 2).reshape(D, TOK).T
        del rC
    return x
```

```python
import math
import numpy as np
import ml_dtypes
import concourse.bass as bass
import concourse.mybir as mybir
from concourse.bass_utils import run_bass_kernel_spmd
from contextlib import ExitStack

F32 = mybir.dt.float32
BF16 = mybir.dt.bfloat16
ALU = mybir.AluOpType
AF = mybir.ActivationFunctionType
AX = mybir.AxisListType
NPBF = ml_dtypes.bfloat16

ENGS = ['sync', 'scalar', 'vector', 'gpsimd', 'tensor']

D = 2048
SEQ = 4096
NB = 2
DEPTH = 4
HD = 128
DFF = 5632
TOK = 1024
EPS = 1e-6
N_C = 255


class _Op:
    __slots__ = ('fn', 'waits', 'inc', 'chan', 'idx')

    def __init__(self, fn):
        self.fn = fn
        self.waits = []
        self.inc = False
        self.chan = None
        self.idx = 0


class Prog:
    def __init__(self, nc, es):
        self.nc = nc
        self.es = es
        self.ops = {e: [] for e in ENGS}
        self.chan_cnt = {}
        self.last_w = {}
        self.readers = {}
        self.waited = {e: {} for e in ENGS}

    def sb(self, name, shape, dt):
        return self.es.enter_context(self.nc.sbuf_tensor(name, list(shape), dt))

    def ps(self, name, shape, dt=F32):
        return self.es.enter_context(self.nc.psum_tensor(name, list(shape), dt))

    def _dep(self, eng, op, key, idx):
        if key == 'tensor' and eng == 'tensor':
            return
        if self.waited[eng].get(key, -1) >= idx:
            return
        self.waited[eng][key] = idx
        op.waits.append((key, idx))
        if key in self.ops:
            self.ops[key][idx].inc = True

    def op(self, eng, fn, reads=(), writes=(), chan=None):
        o = _Op(fn)
        for r in reads:
            t = self.last_w.get(r)
            if t is not None:
                self._dep(eng, o, t[0], t[1])
        for w in writes:
            t = self.last_w.get(w)
            if t is not None:
                self._dep(eng, o, t[0], t[1])
            rd = self.readers.get(w)
            if rd:
                for k, i in rd.items():
                    self._dep(eng, o, k, i)
        lst = self.ops[eng]
        o.idx = len(lst)
        lst.append(o)
        if chan is not None:
            o.chan = chan
            c = self.chan_cnt.get(chan, 0) + 1
            self.chan_cnt[chan] = c
            tok = (('dma', chan), c)
        else:
            tok = (eng, o.idx)
        for r in reads:
            rd = self.readers.setdefault(r, {})
            if rd.get(tok[0], -1) < tok[1]:
                rd[tok[0]] = tok[1]
        for w in writes:
            self.last_w[w] = tok
            self.readers[w] = {}
        return tok

    def dma(self, eng, out, in_, reads=(), writes=(), chan=None):
        return self.op(eng, lambda e: e.dma_start(out=out, in_=in_), reads, writes, chan=chan)

    def finish(self, keys, eng='sync'):
        o = _Op(None)
        for k in keys:
            t = self.last_w.get(k)
            if t is not None:
                self._dep(eng, o, t[0], t[1])
        o.idx = len(self.ops[eng])
        self.ops[eng].append(o)

    def emit(self):
        nc = self.nc
        es = self.es
        sems = {}
        for e in ENGS:
            sems[e] = es.enter_context(nc.semaphore('s_' + e))
        for i, ch in enumerate(self.chan_cnt):
            sems[('dma', ch)] = es.enter_context(nc.semaphore('d%d' % i))
        val = {}
        for e in ENGS:
            c = 0
            v = []
            for o in self.ops[e]:
                if o.inc and o.chan is None:
                    c += 1
                v.append(c)
            val[e] = v
        block = es.enter_context(nc.Block())

        def mk(ename):
            def body(eng):
                for o in self.ops[ename]:
                    for key, idx in o.waits:
                        if key in val:
                            eng.wait_ge(sems[key], val[key][idx])
                        else:
                            eng.wait_ge(sems[key], 16 * idx)
                    if o.fn is None:
                        continue
                    ins = o.fn(eng)
                    if o.chan is not None:
                        ins.then_inc(sems[('dma', o.chan)], 16)
                    elif o.inc:
                        ins.then_inc(sems[ename], 1)
            return body
        for e in ENGS:
            if self.ops[e]:
                getattr(block, e)(mk(e))

    def mm(self, ps_ap, pairs, reads, writes, start=True, stop=True):
        pairs = list(pairs)

        def fn(e):
            n = len(pairs)
            ins = None
            for i, (l, r) in enumerate(pairs):
                ins = e.matmul(ps_ap, lhsT=l, rhs=r, start=(start and i == 0),
                               stop=(stop and i == n - 1))
            return ins
        return self.op('tensor', fn, reads, writes)

    def act(self, out, in_, func, reads, writes, bias=None, scale=None, eng='scalar'):
        kw = {}
        if bias is not None:
            kw['bias'] = bias
        if scale is not None:
            kw['scale'] = scale
        return self.op(eng, lambda e: e.activation(out=out, in_=in_, func=func, **kw), reads, writes)

    def tt(self, eng, out, in0, in1, op, reads, writes):
        return self.op(eng, lambda e: e.tensor_tensor(out=out, in0=in0, in1=in1, op=op), reads, writes)

    def ts(self, eng, out, in0, s1, s2, op0, op1, reads, writes):
        if s2 is None:
            return self.op(eng, lambda e: e.tensor_scalar(out=out, in0=in0, scalar1=s1, scalar2=None, op0=op0),
                           reads, writes)
        return self.op(eng, lambda e: e.tensor_scalar(out=out, in0=in0, scalar1=s1, scalar2=s2, op0=op0, op1=op1),
                       reads, writes)

    def stt(self, eng, out, in0, scalar, in1, op0, op1, reads, writes):
        return self.op(eng, lambda e: e.scalar_tensor_tensor(out=out, in0=in0, scalar=scalar, in1=in1,
                                                              op0=op0, op1=op1), reads, writes)

    def cp(self, eng, out, in_, reads, writes):
        if eng == 'scalar':
            return self.op(eng, lambda e: e.copy(out=out, in_=in_), reads, writes)
        return self.op(eng, lambda e: e.tensor_copy(out=out, in_=in_), reads, writes)

    def memset(self, eng, ap, v, writes):
        return self.op(eng, lambda e: e.memset(ap, v), [], writes)


class PsumRot:
    def __init__(self, P, names, n):
        self.tiles = [(nm, P.ps(nm, [128, 512])) for nm in names[:n]]
        self.i = 0

    def next(self):
        t = self.tiles[self.i % len(self.tiles)]
        self.i += 1
        return t


def rms_stats(P, src_fn, nk, N, ones_bf, sq_bufs, ps_name, ps_tile, rstd_ap, rstd_key, src_keys, tag):
    for k in range(nk):
        nm, sq = sq_bufs[k % len(sq_bufs)]
        P.act(sq[:, 0:N], src_fn(k), AF.Square, [src_keys(k)], [nm])
        P.mm(ps_tile[:, 0:N], [(ones_bf[:, :], sq[:, 0:N])], [nm, 'ones'], [ps_name],
             start=(k == 0), stop=(k == nk - 1))
    P.act(rstd_ap, ps_tile[:, 0:N], AF.Sqrt, [ps_name], [rstd_key], bias=EPS, scale=1.0 / (nk * 128))
    P.op('vector', lambda e: e.reciprocal(out=rstd_ap, in_=rstd_ap), [rstd_key], [rstd_key])


OFF = dict(a_q=0, a_kc=512, a_vc=640, a_ks=768, a_vs=896, a_kw=1024, a_vw=1152, a_g=1280,
           b_q=1292, b_k=1804, b_v=2316, c_q=2828, c_k=3340, c_v=3852, c_f=4364,
           d_q=4368, d_k=4880, d_v=5392)
SC128 = 128 ** -0.5
SC64 = 64 ** -0.5
NFM = 36


def _swap_h(cols):
    p = np.arange(128)
    p[:16] += 16
    p[16:32] -= 16
    return cols[p]


def _swap_d(cols):
    p = np.arange(128)
    for b in (0, 64):
        p[b:b + 8] += 8
        p[b + 8:b + 16] -= 8
    return cols[p]


def a_plan():
    tiles = []
    def c128(o):
        return np.arange(o, o + 128)
    for h in range(4):
        c = c128(OFF['a_q'] + 128 * h)
        tiles.append((c, ('rope', 'H', SC128, h, 4 + h)))
        tiles.append((_swap_h(c), ('swap',)))
    tiles.append((c128(OFF['a_kc']), ('plain', 1.0, 8)))
    tiles.append((c128(OFF['a_vc']), ('plain', 1.0, 9)))
    for nm, oi in (('a_ks', 10), ('a_kw', 11)):
        c = c128(OFF[nm])
        tiles.append((c, ('rope', 'H', 1.0, oi, None)))
        tiles.append((_swap_h(c), ('swap',)))
    for nm, o0, sc in (('b_q', 12, SC128), ('b_k', 16, 1.0), ('c_q', 20, SC128), ('c_k', 24, 1.0)):
        for h in range(4):
            tiles.append((c128(OFF[nm] + 128 * h), ('plain', sc, o0 + h)))
    for nm, o0, sc in (('d_q', 28, SC64), ('d_k', 32, 1.0)):
        for h in range(4):
            c = c128(OFF[nm] + 128 * h)
            tiles.append((c, ('rope', 'D', sc, o0 + h, None)))
            tiles.append((_swap_d(c), ('swap',)))
    sm = np.zeros(128, np.int64)
    sm[:12] = np.arange(OFF['a_g'], OFF['a_g'] + 12)
    sm[12:16] = np.arange(OFF['c_f'], OFF['c_f'] + 4)
    tiles.append((sm, ('small',)))
    tiles.append((np.zeros(128, np.int64), ('skip',)))
    assert len(tiles) == 48
    tm = np.concatenate([np.arange(OFF['a_vs'], OFF['a_vs'] + 128), np.arange(OFF['a_vw'], OFF['a_vw'] + 128),
                         np.arange(OFF['b_v'], OFF['b_v'] + 512), np.arange(OFF['c_v'], OFF['c_v'] + 512),
                         np.arange(OFF['d_v'], OFF['d_v'] + 512), np.zeros(256, np.int64)])
    return tiles, tm


NTM = 1792


def host_w_in_blocks(w_in_l):
    tiles, tm = a_plan()
    cols = np.concatenate([t[0] for t in tiles] + [tm])
    w = w_in_l[:, cols]
    w = w.reshape(16, 128, 16, 512)
    return np.ascontiguousarray(w.transpose(2, 1, 0, 3))


def rope_tables(t0, n):
    pos = np.arange(t0, t0 + n, dtype=np.float32)
    theta = np.float32(500000.0)

    def tab(rot, period, ):
        half = rot // 2
        inv = theta ** (-np.arange(0, rot, 2, dtype=np.float32) / np.float32(rot))
        ang = (pos[:, None] * inv[None, :]).astype(np.float32)
        cos = np.cos(ang).T.astype(np.float32)
        sin = np.sin(ang).T.astype(np.float32)
        ct = np.ones((128, n), np.float32)
        st = np.zeros((128, n), np.float32)
        for b in range(0, 128, period):
            ct[b:b + half] = cos
            ct[b + half:b + rot] = cos
            st[b:b + half] = -sin
            st[b + half:b + rot] = sin
        return ct, st
    cH, sH = tab(32, 128)
    cD, sD = tab(16, 64)
    return np.ascontiguousarray(np.stack([cH * np.float32(SC128), sH * np.float32(SC128), cH, sH,
                                          cD * np.float32(SC64), sD * np.float32(SC64), cD, sD],
                                         axis=1).astype(np.float32))


def build_A():
    nc = bass.Bass("TRN2", target_bir_lowering=False)
    xT = nc.dram_tensor("xT", [128, 16, TOK], F32, kind="ExternalInput").ap()
    gpre = nc.dram_tensor("gpre", [128, 16], F32, kind="ExternalInput").ap()
    wblk = nc.dram_tensor("wblk", [16, 128, 16 * 512], F32, kind="ExternalInput").ap()
    tabs = nc.dram_tensor("tabs", [128, 8, TOK], F32, kind="ExternalInput").ap()
    fm_out = nc.dram_tensor("fm_out", [NFM, 128, TOK], BF16, kind="ExternalOutput").ap()
    sm_out = nc.dram_tensor("sm_out", [128, TOK], F32, kind="ExternalOutput").ap()
    tm_out = nc.dram_tensor("tm_out", [TOK, NTM], BF16, kind="ExternalOutput").ap()
    tiles, _ = a_plan()
    with ExitStack() as es:
        P = Prog(nc, es)
        x_sb = P.sb("x_sb", [128, 16, TOK], F32)
        hin = P.sb("hin", [128, 16, TOK], BF16)
        g_sb = P.sb("g_sb", [128, 16], F32)
        tab_sb = P.sb("tab_sb", [128, 8, TOK], F32)
        ones = P.sb("ones", [128, 128], BF16)
        rstd = P.sb("rstd", [128, 512], F32)
        sqb = [("sq%d" % i, P.sb("sq%d" % i, [128, 512], BF16)) for i in range(3)]
        NWB = 3
        wb = [P.sb("wb%d" % i, [128, 16 * 512], BF16) for i in range(NWB)]
        stg = [P.sb("stg%d" % i, [128, 512], BF16) for i in range(4)]
        t1 = [P.sb("t1_%d" % i, [128, 512], F32) for i in range(2)]
        t2 = [P.sb("t2_%d" % i, [128, 512], F32) for i in range(2)]
        smb = P.sb("smb", [128, TOK], F32)
        rot = PsumRot(P, ["ps%d" % i for i in range(7)], 7)
        ps_ss = P.ps("ps_ss", [128, 512])

        P.memset('vector', ones[:, :], 1.0, ['ones'])
        for q in range(4):
            P.dma('sync', x_sb[:, 4 * q:4 * q + 4, :], xT[:, 4 * q:4 * q + 4, :], [], [('x', q)], chan='x%d' % q)
        P.dma('sync', g_sb[:, :], gpre, [], ['g'], chan='g')
        P.dma('sync', tab_sb[:, :, :], tabs, [], ['tab'], chan='tab')
        wq = {'next': 0}

        def load_w(b):
            i = b % NWB
            P.dma('gpsimd', wb[i][:, :], wblk[b], [], [('wb', i)], chan='wb%d' % i)
        for b in range(NWB):
            load_w(b)
        for tg in range(2):
            sl = slice(tg * 512, tg * 512 + 512)
            rms_stats(P, lambda k: x_sb[:, k, sl], 16, 512, ones, sqb, 'ps_ss', ps_ss, rstd[:, :], 'rstd',
                      lambda k: ('x', k // 4), 'A')
            for k in range(16):
                P.stt('vector', hin[:, k, sl], x_sb[:, k, sl], g_sb[:, k:k + 1], rstd[:, :], ALU.mult, ALU.mult,
                      [('x', k // 4), 'g', 'rstd'], [('hin', tg)])
        sti = [0]

        def stage_out(dst_ap):
            i = sti[0] % 4
            sti[0] += 1
            return i, stg[i]
        pend = {}
        for b in range(12):
            bi = b % NWB
            for ct in range(4):
                cols, job = tiles[b * 4 + ct]
                if job[0] == 'skip':
                    continue
                for tg in range(2):
                    sl = slice(tg * 512, tg * 512 + 512)
                    pn, pt = rot.next()
                    P.mm(pt[:, :], [(wb[bi][:, k * 512 + ct * 128:k * 512 + ct * 128 + 128], hin[:, k, sl])
                                    for k in range(16)], [('wb', bi), ('hin', tg)], [pn])
                    if job[0] == 'plain':
                        i, st = stage_out(None)
                        if job[1] == 1.0:
                            P.cp('scalar', st[:, :], pt[:, :], [pn], [('stg', i)])
                        else:
                            P.ts('vector', st[:, :], pt[:, :], float(job[1]), None, ALU.mult, None, [pn], [('stg', i)])
                        P.dma('sync', fm_out[job[2], :, sl], st[:, :], [('stg', i)], [('fm', job[2], tg)],
                              chan='stg%d' % i)
                    elif job[0] == 'small':
                        P.cp('scalar', smb[:, sl], pt[:, :], [pn], [('smb', tg)])
                        P.dma('sync', sm_out[:, sl], smb[:, sl], [('smb', tg)], [('sm', tg)], chan='smo%d' % tg)
                    elif job[0] == 'rope':
                        pend[tg] = (pn, pt, job)
                    elif job[0] == 'swap':
                        mn, mt, mj = pend[tg]
                        tb = (0 if mj[1] == 'H' else 4) + (0 if mj[2] != 1.0 else 2)
                        j = tg
                        P.tt('vector', t1[j][:, :], mt[:, :], tab_sb[:, tb, sl], ALU.mult, [mn, 'tab'], [('t1', j)])
                        P.tt('vector', t2[j][:, :], pt[:, :], tab_sb[:, tb + 1, sl], ALU.mult, [pn, 'tab'], [('t2', j)])
                        i, st = stage_out(None)
                        P.tt('vector', st[:, :], t1[j][:, :], t2[j][:, :], ALU.add, [('t1', j), ('t2', j)], [('stg', i)])
                        P.dma('sync', fm_out[mj[3], :, sl], st[:, :], [('stg', i)], [('fm', mj[3], tg)],
                              chan='stg%d' % i)
                        if mj[4] is not None:
                            i, st = stage_out(None)
                            P.ts('vector', st[:, :], mt[:, :], float(mj[2]), None, ALU.mult, None, [mn], [('stg', i)])
                            P.dma('sync', fm_out[mj[4], :, sl], st[:, :], [('stg', i)], [('fm', mj[4], tg)],
                                  chan='stg%d' % i)
            if b + NWB < 16:
                load_w(b + NWB)
        for b in range(12, 16):
            bi = b % NWB
            ncol = 512 if b < 15 else 256
            c0 = (b - 12) * 512
            for tt_ in range(8):
                tg = tt_ // 4
                pn, pt = rot.next()
                P.mm(pt[:, 0:ncol], [(hin[:, k, tt_ * 128:tt_ * 128 + 128], wb[bi][:, k * 512:k * 512 + ncol])
                                     for k in range(16)], [('wb', bi), ('hin', tg)], [pn])
                i, st = stage_out(None)
                P.cp('scalar', st[:, 0:ncol], pt[:, 0:ncol], [pn], [('stg', i)])
                P.dma('sync', tm_out[tt_ * 128:tt_ * 128 + 128, c0:c0 + ncol], st[:, 0:ncol], [('stg', i)],
                      [('tm', b, tt_)], chan='stg%d' % i)
            if b + NWB < 16:
                load_w(b + NWB)
        P.finish([k for k in P.last_w if isinstance(k, tuple) and k[0] in ('fm', 'sm', 'tm')])
        P.emit()
    return nc


NBLK_C = 42


def host_c_weights(w_out_l, w_up_l, w_down_l, conv_w_l, conv_b_l):
    blocks = np.zeros((NBLK_C, 128, 8192), np.float32)
    wo = w_out_l.reshape(16, 128, 4, 512)
    blocks[0:4] = wo.transpose(2, 1, 0, 3).reshape(4, 128, 8192)
    ucols = []
    for j in range(22):
        for ct in range(4):
            i = 2 * j + ct // 2
            base = i * 128 if ct % 2 == 0 else DFF + i * 128
            ucols.append(np.arange(base, base + 128))
    ucols = np.concatenate(ucols)
    wu = w_up_l[:, ucols].reshape(16, 128, 22, 512)
    blocks[4:26] = wu.transpose(2, 1, 0, 3).reshape(22, 128, 8192)
    wd = w_down_l.reshape(44, 128, 16, 128)
    blocks[26:42, :, 0:44 * 128] = wd.transpose(2, 1, 0, 3).reshape(16, 128, 44 * 128)
    cw = np.zeros((128, 88, 4), np.float32)
    cc = ucols.reshape(88, 128)
    for tap in range(3):
        cw[:, :, tap] = conv_w_l[tap][cc].T
    cw[:, :, 3] = conv_b_l[cc].T
    return blocks, cw


def build_C():
    nc = bass.Bass("TRN2", target_bir_lowering=False)
    NT = TOK + 2
    xT = nc.dram_tensor("xT", [128, 16, NT], F32, kind="ExternalInput").ap()
    oT = nc.dram_tensor("oT", [128, 16, NT], BF16, kind="ExternalInput").ap()
    gn = nc.dram_tensor("gn", [128, 3, 16], F32, kind="ExternalInput").ap()
    cwd = nc.dram_tensor("cw", [128, 88, 4], F32, kind="ExternalInput").ap()
    wblk = nc.dram_tensor("wblk", [NBLK_C, 128, 8192], F32, kind="ExternalInput").ap()
    x_out = nc.dram_tensor("x_out", [128, 16, TOK], F32, kind="ExternalOutput").ap()
    with ExitStack() as es:
        P = Prog(nc, es)
        xg = P.sb("xg", [128, 16, 512], F32)
        og = P.sb("og", [128, 16, 512], BF16)
        mixed = P.sb("mixed", [128, 16, 512], F32)
        hin2 = P.sb("hin2", [128, 16, 512], BF16)
        hT = P.sb("hT", [128, 44, 512], BF16)
        g_sb = P.sb("g_sb", [128, 3, 16], F32)
        cw = P.sb("cw_sb", [128, 88, 4], F32)
        utail = P.sb("utail", [128, 88, 2], F32)
        ones = P.sb("ones", [128, 128], BF16)
        rstd = P.sb("rstd", [128, 512], F32)
        sqb = [("sq%d" % i, P.sb("sq%d" % i, [128, 512], BF16)) for i in range(2)]
        NWB = 3
        wb = [P.sb("wb%d" % i, [128, 8192], BF16) for i in range(NWB)]
        ubuf = [P.sb("ubuf%d" % i, [128, 516], F32) for i in range(2)]
        cbuf = [P.sb("cbuf%d" % i, [128, 512], F32) for i in range(2)]
        gbuf = [P.sb("gbuf%d" % i, [128, 512], F32) for i in range(2)]
        rot = PsumRot(P, ["ps%d" % i for i in range(7)], 7)
        ps_ss = P.ps("ps_ss", [128, 512])

        P.memset('vector', ones[:, :], 1.0, ['ones'])
        P.dma('sync', g_sb[:, :, :], gn, [], ['g'], chan='g')
        P.dma('sync', cw[:, :, :], cwd, [], ['cw'], chan='cw')
        xh = P.sb("xh", [128, 16, 2], F32)
        oh = P.sb("oh", [128, 16, 2], BF16)
        mixh = P.sb("mixh", [128, 16, 2], F32)
        hin2h = P.sb("hin2h", [128, 16, 2], BF16)
        BUF = dict(G=(xg, og, mixed, hin2), H=(xh, oh, mixh, hin2h))
        passes = [[('H', 0, 2), ('G', 2, 512)], [('G', 514, 512)]]
        seq = list(range(42)) * 2
        wstate = {'issued': 0}

        def load_next():
            n = wstate['issued']
            if n >= len(seq):
                return
            b = seq[n]
            i = n % NWB
            ne = 8192 if b < 26 else 44 * 128
            P.dma('gpsimd', wb[i][:, 0:ne], wblk[b, :, 0:ne], [], [('wb', i)], chan='wb%d' % i)
            wstate['issued'] = n + 1
        used = [0]

        def cur_w():
            i = used[0] % NWB
            used[0] += 1
            return i
        for _ in range(NWB):
            load_next()
        for pi, subs in enumerate(passes):
            for (tg, c0, N) in subs:
                xb, ob, mb, hb = BUF[tg]
                if tg == 'G':
                    for q in range(4):
                        P.dma('sync', xb[:, 4 * q:4 * q + 4, 0:N], xT[:, 4 * q:4 * q + 4, c0:c0 + N], [],
                              [('xg', tg, q)], chan='xg%d' % q)
                else:
                    P.dma('sync', xb[:, :, 0:N], xT[:, :, c0:c0 + N], [], [('xg', tg, q) for q in range(4)],
                          chan='xh')
                P.dma('sync', ob[:, :, 0:N], oT[:, :, c0:c0 + N], [], ['og' + tg], chan='og' + tg)
            for j in range(4):
                bi = cur_w()
                for ct in range(4):
                    m = 4 * j + ct
                    for (tg, c0, N) in subs:
                        xb, ob, mb, hb = BUF[tg]
                        pn, pt = rot.next()
                        P.mm(pt[:, 0:N], [(wb[bi][:, k * 512 + ct * 128:k * 512 + ct * 128 + 128], ob[:, k, 0:N])
                                          for k in range(16)], [('wb', bi), 'og' + tg], [pn])
                        P.cp('scalar', mb[:, m, 0:N], pt[:, 0:N], [pn], [('mx', tg, m)])
                load_next()
            for (tg, c0, N) in subs:
                xb, ob, mb, hb = BUF[tg]
                rms_stats(P, lambda k: mb[:, k, 0:N], 16, N, ones, sqb, 'ps_ss', ps_ss, rstd[:, 0:N], 'rstd',
                          lambda k: ('mx', tg, k), 'C1')
                for m in range(16):
                    P.stt('vector', mb[:, m, 0:N], mb[:, m, 0:N], g_sb[:, 0, m:m + 1], rstd[:, 0:N],
                          ALU.mult, ALU.mult, [('mx', tg, m), 'g', 'rstd'], [('mx', tg, m)])
                    P.tt('vector', xb[:, m, 0:N], xb[:, m, 0:N], mb[:, m, 0:N], ALU.add,
                         [('mx', tg, m), ('xg', tg, m // 4)], [('xg', tg, m // 4)])
                rms_stats(P, lambda k: xb[:, k, 0:N], 16, N, ones, sqb, 'ps_ss', ps_ss, rstd[:, 0:N], 'rstd',
                          lambda k: ('xg', tg, k // 4), 'C2')
                for m in range(16):
                    P.stt('vector', hb[:, m, 0:N], xb[:, m, 0:N], g_sb[:, 1, m:m + 1], rstd[:, 0:N],
                          ALU.mult, ALU.mult, [('xg', tg, m // 4), 'g', 'rstd'], ['hin2' + tg])
            for j in range(22):
                bi = cur_w()
                for ct in range(4):
                    ut = 4 * j + ct
                    i = 2 * j + ct // 2
                    for (tg, c0, N) in subs:
                        xb, ob, mb, hb = BUF[tg]
                        pn, pt = rot.next()
                        P.mm(pt[:, 0:N], [(wb[bi][:, k * 512 + ct * 128:k * 512 + ct * 128 + 128], hb[:, k, 0:N])
                                          for k in range(16)], [('wb', bi), 'hin2' + tg], [pn])
                        if tg == 'H':
                            P.cp('scalar', utail[:, ut, :], pt[:, 0:2], [pn], [('ut', ut)])
                            continue
                        u = ubuf[ut % 2]
                        uk = ('ub', ut % 2)
                        c = cbuf[ut % 2]
                        ck = ('cb', ut % 2)
                        P.cp('scalar', u[:, 2:2 + N], pt[:, 0:N], [pn], [uk])
                        P.cp('vector', u[:, 0:2], utail[:, ut, :], [('ut', ut)], [uk])
                        P.ts('vector', c[:, 0:N], u[:, 2:2 + N], cw[:, ut, 2:3], cw[:, ut, 3:4], ALU.mult, ALU.add,
                             [uk, 'cw'], [ck])
                        P.stt('vector', c[:, 0:N], u[:, 1:1 + N], cw[:, ut, 1:2], c[:, 0:N], ALU.mult, ALU.add,
                              [uk, 'cw', ck], [ck])
                        P.stt('vector', c[:, 0:N], u[:, 0:N], cw[:, ut, 0:1], c[:, 0:N], ALU.mult, ALU.add,
                              [uk, 'cw', ck], [ck])
                        P.cp('vector', utail[:, ut, :], u[:, N:N + 2], [uk], [('ut', ut)])
                        gb = gbuf[i % 2]
                        if ct % 2 == 0:
                            P.act(gb[:, 0:N], c[:, 0:N], AF.Gelu_apprx_tanh, [ck], [('gb', i % 2)])
                        else:
                            P.tt('vector', hT[:, i, 0:N], gb[:, 0:N], c[:, 0:N], ALU.mult, [('gb', i % 2), ck],
                                 [('h', i)])
                load_next()
            tg, c0, N = subs[-1]
            xb, ob, mb, hb = BUF[tg]
            for m in range(16):
                bi = cur_w()
                pn, pt = rot.next()
                P.mm(pt[:, 0:N], [(wb[bi][:, i * 128:i * 128 + 128], hT[:, i, 0:N]) for i in range(44)],
                     [('wb', bi)] + [('h', i) for i in range(44)], [pn])
                P.cp('scalar', mb[:, m, 0:N], pt[:, 0:N], [pn], [('mx', tg, m)])
                load_next()
            rms_stats(P, lambda k: mb[:, k, 0:N], 16, N, ones, sqb, 'ps_ss', ps_ss, rstd[:, 0:N], 'rstd',
                      lambda k: ('mx', tg, k), 'C3')
            for m in range(16):
                P.stt('vector', mb[:, m, 0:N], mb[:, m, 0:N], g_sb[:, 2, m:m + 1], rstd[:, 0:N],
                      ALU.mult, ALU.mult, [('mx', tg, m), 'g', 'rstd'], [('mx', tg, m)])
                P.tt('vector', xb[:, m, 0:N], xb[:, m, 0:N], mb[:, m, 0:N], ALU.add,
                     [('mx', tg, m), ('xg', tg, m // 4)], [('xg', tg, m // 4)])
            for q in range(4):
                P.dma('sync', x_out[:, 4 * q:4 * q + 4, c0 - 2:c0 - 2 + N], xb[:, 4 * q:4 * q + 4, 0:N],
                      [('xg', tg, q)], [('xo', pi, q)], chan='xo%d' % q)
        P.finish([k for k in P.last_w if isinstance(k, tuple) and k[0] == 'xo'])
        P.emit()
    return nc


BIGNEG = 30000.0


def causal_masks():
    s = np.arange(128)[:, None]
    t = np.arange(512)[None, :]
    le = np.stack([(128 * r + s <= t) for r in range(4)]).astype(np.float32)
    lt = np.stack([(128 * r + s < t) for r in range(4)]).astype(np.float32)
    win = np.stack([(t < s + 128 * r) for r in range(4)]).astype(np.float32)
    return le.astype(NPBF), lt.astype(NPBF), win.astype(NPBF)


class AttnCtx:
    def __init__(self, P):
        self.P = P
        self.pss = [("pss%d" % i, P.ps("pss%d" % i, [128, 512])) for i in range(3)]
        self.pso = [("pso%d" % i, P.ps("pso%d" % i, [128, 512])) for i in range(2)]
        self.psl = [("psl%d" % i, P.ps("psl%d" % i, [128, 512])) for i in range(2)]
        self.psx = ("psx", P.ps("psx", [128, 512]))
        self.pT = [("pT%d" % i, P.sb("pT%d" % i, [128, 512], BF16)) for i in range(6)]
        self.ones = P.sb("ones", [128, 128], BF16)
        P.memset('vector', self.ones[:, :], 1.0, ['ones'])
        self.si = 0
        self.gi = 0

    def next_s(self):
        t = self.pss[self.si % 3]
        p = self.pT[self.si % 6]
        self.si += 1
        return t, p

    def next_acc(self):
        o = self.pso[self.gi % 2]
        l = self.psl[self.gi % 2]
        self.gi += 1
        return o, l


def attn_group(cx, steps, want_l=True, lag=2):
    P = cx.P
    (on, ot), (ln, lt) = cx.next_acc()
    n = len(steps)
    slots = []
    for i in range(n + lag):
        if i < n:
            st = steps[i]
            (sn, stile), (pn, pt) = cx.next_s()
            P.mm(stile[:, :], st['pairs'], st['reads'], [sn])
            P.act(pt[:, :], stile[:, :], AF.Exp, [sn], [pn])
            if st.get('mask') is not None:
                P.tt('gpsimd', pt[:, :], pt[:, :], st['mask'], ALU.mult, [pn, st['mkey']], [pn])
            slots.append((pn, pt))
        j = i - lag
        if j >= 0:
            st = steps[j]
            pn, pt = slots[j]
            P.mm(ot[:, :], [(st['v'], pt[:, :])], [pn] + st['vreads'], [on], start=(j == 0), stop=(j == n - 1))
            if want_l:
                P.mm(lt[:, :], [(cx.ones[:, :], pt[:, :])], [pn, 'ones'], [ln], start=(j == 0), stop=(j == n - 1))
    return (on, ot), (ln, lt)


def load_qkv(P, nc, names, with_v=True):
    out = {}
    for nm in names:
        if nm.startswith('v'):
            d_ = nc.dram_tensor(nm, [128, 32, 128], BF16, kind="ExternalInput").ap()
            s_ = P.sb(nm + "_sb", [128, 32, 128], BF16)
            P.dma('sync', s_[:, :, :], d_, [], [nm], chan=nm)
        else:
            d_ = nc.dram_tensor(nm, [128, SEQ], BF16, kind="ExternalInput").ap()
            s_ = P.sb(nm + "_sb", [128, SEQ], BF16)
            P.dma('sync', s_[:, :], d_, [], [nm], chan=nm)
        out[nm] = s_
    return out


def load_masks(P, nc, name):
    d_ = nc.dram_tensor(name, [128, 4, 512], BF16, kind="ExternalInput").ap()
    s_ = P.sb(name + "_sb", [128, 4, 512], BF16)
    P.dma('sync', s_[:, :, :], d_, [], [name], chan=name)
    return s_


class OutStage:
    def __init__(self, P, out_ap, n=2):
        self.P = P
        self.out = out_ap
        self.bufs = [P.sb("ostg%d" % i, [128, 512], BF16) for i in range(n)]
        self.i = 0

    def next(self):
        i = self.i % len(self.bufs)
        self.i += 1
        return i, self.bufs[i]

    def store(self, i, g):
        self.P.dma('sync', self.out[:, g * 512:(g + 1) * 512], self.bufs[i][:, :], [('ostg', i)], [('out', g)],
                   chan='ostg%d' % i)


def build_fox():
    nc = bass.Bass("TRN2", target_bir_lowering=False)
    cf = nc.dram_tensor("cf", [6, SEQ], F32, kind="ExternalInput").ap()
    fb = nc.dram_tensor("fb", [6, 1], F32, kind="ExternalInput").ap()
    coef = nc.dram_tensor("coef", [6, 8], F32, kind="ExternalInput").ap()
    o_out = nc.dram_tensor("o_out", [128, SEQ], BF16, kind="ExternalOutput").ap()
    with ExitStack() as es:
        P = Prog(nc, es)
        cx = AttnCtx(P)
        T = load_qkv(P, nc, ['q', 'k', 'v'])
        nmask = load_masks(P, nc, 'nmask')
        ident_d = nc.dram_tensor("ident", [128, 128], BF16, kind="ExternalInput").ap()
        ident = P.sb("ident_sb", [128, 128], BF16)
        P.dma('sync', ident[:, :], ident_d, [], ['ident'], chan='ident')
        ost = OutStage(P, o_out)
        cf_sb = P.sb("cf_sb", [6, SEQ], F32)
        w1 = P.sb("w1", [6, SEQ], F32)
        w2 = P.sb("w2", [6, SEQ], F32)
        w3 = P.sb("w3", [6, SEQ], F32)
        hb = P.sb("hb", [6, SEQ], BF16)
        hi = P.sb("hi", [6, SEQ], F32)
        mid = P.sb("mid", [6, SEQ], F32)
        lo = P.sb("lo", [6, SEQ], F32)
        kaug = P.sb("kaug", [6, SEQ], BF16)
        qaug = P.sb("qaug", [6, SEQ], BF16)
        fb_sb = P.sb("fb_sb", [6, 1], F32)
        co = P.sb("co", [6, 8], F32)
        rl = P.sb("rl", [128, 512], F32)
        P.dma('sync', cf_sb[:, :], cf, [], ['cf'], chan='cf')
        P.dma('sync', fb_sb[:, :], fb, [], ['fb'], chan='fb')
        P.dma('sync', co[:, :], coef, [], ['co'], chan='co')
        P.ts('vector', fb_sb[:, :], fb_sb[:, :], -1.0, None, ALU.mult, None, ['fb'], ['fb'])
        P.act(w1[:, :], cf_sb[:, :], AF.Exp, ['cf', 'fb'], ['w1'], bias=fb_sb[:, 0:1], scale=-1.0)
        P.act(w1[:, :], w1[:, :], AF.Ln, ['w1'], ['w1'], bias=1.0)
        P.ts('vector', w1[:, :], w1[:, :], -1.0, None, ALU.mult, None, ['w1'], ['w1'])
        P.memset('vector', w2[:, :], 1.0, ['w2'])
        P.op('vector', lambda e: e.tensor_tensor_scan(out=w3[:, :], data0=w2[:, :], data1=w1[:, :], initial=0.0,
                                                      op0=ALU.mult, op1=ALU.add), ['w1', 'w2'], ['w3'])
        P.cp('vector', hb[:, :], w3[:, :], ['w3'], ['hb'])
        P.cp('vector', hi[:, :], hb[:, :], ['hb'], ['hi'])
        P.tt('vector', w1[:, :], w3[:, :], hi[:, :], ALU.subtract, ['w3', 'hi'], ['w1'])
        P.cp('vector', hb[:, :], w1[:, :], ['w1'], ['hb'])
        P.cp('vector', mid[:, :], hb[:, :], ['hb'], ['mid'])
        P.tt('vector', w2[:, :], w1[:, :], mid[:, :], ALU.subtract, ['w1', 'mid'], ['w2'])
        P.cp('vector', hb[:, :], w2[:, :], ['w2'], ['hb'])
        P.cp('vector', lo[:, :], hb[:, :], ['hb'], ['lo'])
        for dst, dk, c0 in ((kaug, 'kaug', 0), (qaug, 'qaug', 4)):
            P.ts('vector', w1[:, :], hi[:, :], co[:, c0:c0 + 1], co[:, c0 + 3:c0 + 4], ALU.mult, ALU.add,
                 ['hi', 'co'], ['w1'])
            P.stt('vector', w2[:, :], mid[:, :], co[:, c0 + 1:c0 + 2], w1[:, :], ALU.mult, ALU.add,
                  ['mid', 'co', 'w1'], ['w2'])
            P.stt('vector', dst[:, :], lo[:, :], co[:, c0 + 2:c0 + 3], w2[:, :], ALU.mult, ALU.add,
                  ['lo', 'co', 'w2'], [dk])
        q, k, v = T['q'], T['k'], T['v']
        for g in range(8):
            qs = slice(g * 512, g * 512 + 512)
            steps = []
            for S in range(4 * g + 4):
                ks = slice(S * 128, S * 128 + 128)
                st = dict(pairs=[(k[:, ks], q[:, qs]), (kaug[:, ks], qaug[:, qs])], reads=['q', 'k', 'kaug', 'qaug'],
                          v=v[:, S, :], vreads=['v'])
                if S >= 4 * g:
                    st['pairs'].append((ident[:, :], nmask[:, S - 4 * g, :]))
                    st['reads'] += ['ident', 'nmask']
                steps.append(st)
            (on, ot), (ln, lt) = attn_group(cx, steps)
            P.op('vector', lambda e, lt=lt: e.reciprocal(out=rl[:, :], in_=lt[:, :]), [ln], ['rl'])
            i, sb_ = ost.next()
            P.tt('vector', sb_[:, :], ot[:, :], rl[:, :], ALU.mult, [on, 'rl'], [('ostg', i)])
            ost.store(i, g)
        P.finish([('out', g) for g in range(8)])
        P.emit()
    return nc


def build_diff():
    nc = bass.Bass("TRN2", target_bir_lowering=False)
    lamp = nc.dram_tensor("lamp", [128, 256], F32, kind="ExternalInput").ap()
    dn = nc.dram_tensor("dn", [128, 2], F32, kind="ExternalInput").ap()
    o_out = nc.dram_tensor("o_out", [128, SEQ], BF16, kind="ExternalOutput").ap()
    with ExitStack() as es:
        P = Prog(nc, es)
        cx = AttnCtx(P)
        T = load_qkv(P, nc, ['q', 'k', 'v'])
        mle = load_masks(P, nc, 'mle')
        ost = OutStage(P, o_out)
        lp = P.sb("lp", [128, 256], F32)
        dn_sb = P.sb("dn_sb", [128, 2], F32)
        pr = P.sb("pr", [128, 128], F32)
        sc = P.sb("sc", [128, 8], F32)
        rl = P.sb("rl", [128, 512], F32)
        o1 = P.sb("o1", [128, 512], F32)
        o2 = P.sb("o2", [128, 512], F32)
        sq = P.sb("sq", [128, 512], BF16)
        rstd = P.sb("rstd", [128, 512], F32)
        P.dma('sync', lp[:, :], lamp, [], ['lp'], chan='lp')
        P.dma('sync', dn_sb[:, :], dn, [], ['dn'], chan='dn')
        P.tt('vector', pr[:, 0:64], lp[:, 0:64], lp[:, 64:128], ALU.mult, ['lp'], ['pr'])
        P.tt('vector', pr[:, 64:128], lp[:, 128:192], lp[:, 192:256], ALU.mult, ['lp', 'pr'], ['pr'])
        P.op('vector', lambda e: e.reduce_sum(out=sc[:, 0:1], in_=pr[:, 0:64], axis=AX.X), ['pr'], ['sc'])
        P.op('vector', lambda e: e.reduce_sum(out=sc[:, 1:2], in_=pr[:, 64:128], axis=AX.X), ['pr', 'sc'], ['sc'])
        P.act(sc[:, 2:4], sc[:, 0:2], AF.Exp, ['sc'], ['sc'])
        P.tt('vector', sc[:, 4:5], sc[:, 3:4], sc[:, 2:3], ALU.subtract, ['sc'], ['sc'])
        P.tt('vector', sc[:, 4:5], sc[:, 4:5], dn_sb[:, 1:2], ALU.subtract, ['sc', 'dn'], ['sc'])
        P.ts('vector', sc[:, 5:6], dn_sb[:, 1:2], -1.0, 1.0, ALU.mult, ALU.add, ['dn', 'sc'], ['sc'])
        P.tt('vector', sc[:, 5:6], sc[:, 5:6], dn_sb[:, 0:1], ALU.mult, ['sc', 'dn'], ['sc'])
        q, k, v = T['q'], T['k'], T['v']
        for g in range(8):
            qs = slice(g * 512, g * 512 + 512)
            for half, od, ok_ in ((0, o1, 'o1'), (1, o2, 'o2')):
                hs = slice(64 * half, 64 * half + 64)
                steps = []
                for S in range(4 * g + 4):
                    ks = slice(S * 128, S * 128 + 128)
                    st = dict(pairs=[(k[hs, ks], q[hs, qs])], reads=['q', 'k'], v=v[:, S, :], vreads=['v'])
                    if S >= 4 * g:
                        st['mask'] = mle[:, S - 4 * g, :]
                        st['mkey'] = 'mle'
                    steps.append(st)
                (on, ot), (ln, lt) = attn_group(cx, steps)
                P.op('vector', lambda e, lt=lt: e.reciprocal(out=rl[:, :], in_=lt[:, :]), [ln], ['rl'])
                P.tt('vector', od[:, :], ot[:, :], rl[:, :], ALU.mult, [on, 'rl'], [ok_])
            P.stt('vector', o1[:, :], o2[:, :], sc[:, 4:5], o1[:, :], ALU.mult, ALU.add, ['o1', 'o2', 'sc'], ['o1'])
            P.act(sq[:, :], o1[:, :], AF.Square, ['o1'], ['sq'])
            xn, xt = cx.psx
            P.mm(xt[:, :], [(cx.ones[:, :], sq[:, :])], ['sq', 'ones'], [xn])
            P.act(rstd[:, :], xt[:, :], AF.Sqrt, [xn], ['rstd'], bias=EPS, scale=1.0 / 128)
            P.op('vector', lambda e: e.reciprocal(out=rstd[:, :], in_=rstd[:, :]), ['rstd'], ['rstd'])
            i, sb_ = ost.next()
            P.stt('vector', sb_[:, :], o1[:, :], sc[:, 5:6], rstd[:, :], ALU.mult, ALU.mult,
                  ['o1', 'sc', 'rstd'], [('ostg', i)])
            ost.store(i, g)
        P.finish([('out', g) for g in range(8)])
        P.emit()
    return nc


def build_sb():
    nc = bass.Bass("TRN2", target_bir_lowering=False)
    trid = nc.dram_tensor("ntri", [128, 128], BF16, kind="ExternalInput").ap()
    o_out = nc.dram_tensor("o_out", [128, SEQ], BF16, kind="ExternalOutput").ap()
    with ExitStack() as es:
        P = Prog(nc, es)
        cx = AttnCtx(P)
        T = load_qkv(P, nc, ['q', 'k', 'v'])
        mlt = load_masks(P, nc, 'mlt')
        ost = OutStage(P, o_out)
        ntri = P.sb("ntri_sb", [128, 128], BF16)
        nones = P.sb("nones", [128, 128], BF16)
        ebuf = [P.sb("ebuf%d" % i, [128, 512], F32) for i in range(2)]
        Lm = [P.sb("Lm%d" % i, [128, 512], BF16) for i in range(3)]
        Lsum = P.sb("Lsum", [128, 512], F32)
        Lsb = [P.sb("Lsb%d" % i, [128, 512], BF16) for i in range(2)]
        P.dma('sync', ntri[:, :], trid, [], ['ntri'], chan='ntri')
        P.memset('vector', nones[:, :], -1.0, ['nones'])
        q, k, v = T['q'], T['k'], T['v']
        zb = [cx.pss[0], cx.pss[1]]
        ab = [cx.pss[2], cx.psx]
        for g in range(8):
            qs = slice(g * 512, g * 512 + 512)
            (on, ot), _ = cx.next_acc()
            Slist = list(range(4 * g + 3, -1, -1))
            n = len(Slist)
            for kk in range(n + 2):
                if kk < n:
                    S = Slist[kk]
                    ks = slice(S * 128, S * 128 + 128)
                    zn, zt = zb[kk % 2]
                    e_ = ebuf[kk % 2]
                    ek = ('e', kk % 2)
                    L_ = Lm[kk % 3]
                    Lk = ('L', kk % 3)
                    P.mm(zt[:, :], [(k[:, ks], q[:, qs])], ['q', 'k'], [zn])
                    P.act(e_[:, :], zt[:, :], AF.Exp, [zn], [ek])
                    P.act(L_[:, :], e_[:, :], AF.Ln, [ek], [Lk], bias=1.0)
                    if S >= 4 * g:
                        P.tt('gpsimd', L_[:, :], L_[:, :], mlt[:, S - 4 * g, :], ALU.mult, [Lk, 'mlt'], [Lk])
                if 1 <= kk <= n:
                    i = kk - 1
                    S = Slist[i]
                    ks = slice(S * 128, S * 128 + 128)
                    an, at = ab[i % 2]
                    L_ = Lm[i % 3]
                    Lk = ('L', i % 3)
                    pn, pt = cx.pT[i % 3]
                    pairs = [(k[:, ks], q[:, qs]), (ntri[:, :], L_[:, :])]
                    rd = ['q', 'k', 'ntri', Lk]
                    if i > 0:
                        pairs.append((nones[:, :], Lsb[(i - 1) % 2][:, :]))
                        rd += ['nones', ('Lsb', (i - 1) % 2)]
                    P.mm(at[:, :], pairs, rd, [an])
                    P.act(pt[:, :], at[:, :], AF.Exp, [an], [pn])
                    if S >= 4 * g:
                        P.tt('gpsimd', pt[:, :], pt[:, :], mlt[:, S - 4 * g, :], ALU.mult, [pn, 'mlt'], [pn])
                    if i < n - 1:
                        if i == 0:
                            P.cp('vector', Lsum[:, :], L_[:, :], [Lk], ['Lsum'])
                        else:
                            P.tt('vector', Lsum[:, :], Lsum[:, :], L_[:, :], ALU.add, [Lk, 'Lsum'], ['Lsum'])
                        P.cp('vector', Lsb[i % 2][:, :], Lsum[:, :], ['Lsum'], [('Lsb', i % 2)])
                if kk >= 2:
                    i = kk - 2
                    S = Slist[i]
                    pn, pt = cx.pT[i % 3]
                    P.mm(ot[:, :], [(v[:, S, :], pt[:, :])], [pn, 'v'], [on], start=(i == 0), stop=(i == n - 1))
            j, sb_ = ost.next()
            P.cp('vector', sb_[:, :], ot[:, :], [on], [('ostg', j)])
            ost.store(j, g)
        P.finish([('out', g) for g in range(8)])
        P.emit()
    return nc


def tm_layout(v):
    return np.ascontiguousarray(v.reshape(32, 128, 128).transpose(1, 0, 2))


def fox_coef():
    co = np.zeros((6, 8), np.float32)
    co[0, 0] = -1; co[1, 1] = -1; co[2, 2] = -1; co[3:6, 3] = 1
    co[0:3, 7] = 1; co[3, 4] = 1; co[4, 5] = 1; co[5, 6] = 1
    return co


def ntri_const():
    j = np.arange(128)[:, None]
    s = np.arange(128)[None, :]
    return (-(j >= s).astype(np.float32)).astype(NPBF)


def mask_layout(m):
    return np.ascontiguousarray(m.transpose(1, 0, 2))


def nsa_consts():
    c = np.arange(128)[:, None]
    t = np.arange(512)[None, :]
    cm = []
    for g in range(5):
        cm.append((16 * c + 31 <= 512 * g + t))
    for g in range(4, 8):
        cm.append((16 * (c + 128) + 31 <= 512 * g + t) & (c + 128 < N_C))
    cmask = np.stack(cm).astype(np.float32).astype(NPBF)
    overlap = np.zeros((256, 64), np.float32)
    for r in range(2):
        sub = np.arange(N_C) * 16 + r * 16
        np.add.at(overlap, (np.arange(N_C), sub // 64), 1.0)
    ovl = np.ascontiguousarray(overlap.reshape(2, 128, 64).transpose(1, 0, 2)).astype(NPBF)
    tt_ = np.arange(SEQ)[:, None]
    j = np.arange(64)[None, :]
    cur = tt_ // 64
    valid = j <= cur
    forced = valid & ((j == 0) | (j >= cur - 1))
    A = (valid & ~forced).astype(np.float32)
    Bm = np.where(forced, np.float32(1e9), np.where(valid, np.float32(0), np.float32(-1e30))).astype(np.float32)
    ab = np.stack([A.reshape(32, 128, 64).transpose(1, 0, 2), Bm.reshape(32, 128, 64).transpose(1, 0, 2)], axis=1)
    eall = ((np.arange(SEQ)[None, :] // 64) == np.arange(64)[:, None]).astype(np.float32) * BIGNEG
    ident = np.eye(128, dtype=np.float32)
    return dict(cmask=np.ascontiguousarray(cmask.transpose(1, 0, 2)), ovl=ovl, absel=np.ascontiguousarray(ab),
                eall=eall.astype(NPBF), ident=ident.astype(NPBF))


def build_nsa():
    nc = bass.Bass("TRN2", target_bir_lowering=False)
    gates_d = nc.dram_tensor("gates", [3, SEQ], F32, kind="ExternalInput").ap()
    w1k_d = nc.dram_tensor("w1k", [128, 8192], F32, kind="ExternalInput").ap()
    w1v_d = nc.dram_tensor("w1v", [128, 8192], F32, kind="ExternalInput").ap()
    w2_d = nc.dram_tensor("w2", [128, 4, 128], F32, kind="ExternalInput").ap()
    pe_d = nc.dram_tensor("pe", [128, 64], F32, kind="ExternalInput").ap()
    cmask_d = nc.dram_tensor("cmask", [128, 9, 512], BF16, kind="ExternalInput").ap()
    ovl_d = nc.dram_tensor("ovl", [128, 2, 64], BF16, kind="ExternalInput").ap()
    ab_d = nc.dram_tensor("absel", [128, 2, 32, 64], F32, kind="ExternalInput").ap()
    eall_d = nc.dram_tensor("eall", [64, SEQ], BF16, kind="ExternalInput").ap()
    ident_d = nc.dram_tensor("ident", [128, 128], BF16, kind="ExternalInput").ap()
    o_out = nc.dram_tensor("o_out", [128, SEQ], BF16, kind="ExternalOutput").ap()
    with ExitStack() as es:
        P = Prog(nc, es)
        cx = AttnCtx(P)
        T = load_qkv(P, nc, ['aqr', 'aq0', 'aq1', 'aq2', 'aq3', 'ks', 'kw', 'vs', 'vw'])
        xkv_d = [nc.dram_tensor(nm, [128, SEQ], BF16, kind="ExternalInput").ap() for nm in ('xk', 'xv')]
        xkv = P.sb("xkv", [128, SEQ], BF16)
        mle = load_masks(P, nc, 'mle')
        mwin = load_masks(P, nc, 'mwin')
        ost = OutStage(P, o_out)
        cmask = P.sb("cmask_sb", [128, 9, 512], BF16)
        ovl = P.sb("ovl_sb", [128, 2, 64], BF16)
        ab = P.sb("ab_sb", [128, 2, 32, 64], F32)
        eall = P.sb("eall_sb", [64, SEQ], BF16)
        ident = P.sb("ident_sb", [128, 128], BF16)
        selT = P.sb("selT", [64, SEQ], BF16)
        res = P.sb("res", [128, SEQ], F32)
        w1 = P.sb("w1", [128, 8192], BF16)
        w2 = P.sb("w2_sb", [128, 4, 128], BF16)
        pe32 = P.sb("pe32", [128, 64], F32)
        peb = P.sb("peb", [128, 64], BF16)
        bias = P.sb("bias", [128, 4], F32)
        hid = [P.sb("hid%d" % j, [128, 256], BF16) for j in range(2)]
        kcT = P.sb("kcT", [128, 256], BF16)
        vc = P.sb("vc", [128, 2, 128], BF16)
        ecm = [P.sb("ecm%d" % i, [128, 512], BF16) for i in range(4)]
        PT = [P.sb("PT%d" % i, [128, 512], F32) for i in range(2)]
        PTb = [P.sb("PTb%d" % i, [128, 512], BF16) for i in range(2)]
        tmp = P.sb("tmp", [128, 512], F32)
        rl = P.sb("rl", [128, 512], F32)
        gsb = [P.sb("gsb%d" % i, [128, 512], F32) for i in range(3)]
        sc = P.sb("sc", [128, 64], F32)
        sc2 = P.sb("sc2", [128, 64], F32)
        m8 = P.sb("m8", [128, 16], F32)
        selm = P.sb("selm", [128, 64], BF16)
        for s_, d_, k_ in ((cmask[:, :, :], cmask_d, 'cmask'), (ovl[:, :, :], ovl_d, 'ovl'), (ab[:, :, :, :], ab_d, 'ab'),
                           (eall[:, :], eall_d, 'eall'), (ident[:, :], ident_d, 'ident'), (pe32[:, :], pe_d, 'pe32')):
            P.dma('sync', s_, d_, [], [k_], chan=k_)
        P.dma('gpsimd', w2[:, :, :], w2_d, [], ['w2'], chan='w2')
        P.cp('vector', peb[:, :], pe32[:, :], ['pe32'], ['peb'])
        P.memset('vector', kcT[:, :], 0.0, ['kcT'])
        P.memset('vector', hid[0][:, :], 0.0, [('hid', 0)])
        P.memset('vector', hid[1][:, :], 0.0, [('hid', 1)])
        xn, xt = cx.psx
        for kv in range(2):
            P.dma('gpsimd', w1[:, :], (w1k_d, w1v_d)[kv], [], ['w1'], chan='w1')
            P.dma('sync', xkv[:, :], xkv_d[kv], [], ['xkv'], chan='xkv')
            xr = xkv[:, :].rearrange("p (c r) -> p c r", r=16)
            for j in range(2):
                P.mm(xt[:, 0:1], [(w1[:, l * 256 + j * 128:l * 256 + j * 128 + 128], peb[:, kv * 32 + l:kv * 32 + l + 1])
                                  for l in range(32)], ['w1', 'peb'], [xn])
                P.cp('vector', bias[:, 2 * kv + j:2 * kv + j + 1], xt[:, 0:1], [xn], ['bias'])
                (sn, st), _ = cx.next_s()
                P.mm(st[:, 0:255], [(w1[:, l * 256 + j * 128:l * 256 + j * 128 + 128],
                                     (xr[:, 0:255, l] if l < 16 else xr[:, 1:256, l - 16])) for l in range(32)],
                     ['w1', 'xkv'], [sn])
                P.act(hid[j][:, 0:255], st[:, 0:255], AF.Gelu_apprx_tanh, [sn, 'bias'], [('hid', j)],
                      bias=bias[:, 2 * kv + j:2 * kv + j + 1])
            if kv == 0:
                P.mm(xt[:, 0:255], [(w2[:, j, :], hid[j][:, 0:255]) for j in range(2)],
                     ['w2', ('hid', 0), ('hid', 1)], [xn])
                P.cp('vector', kcT[:, 0:255], xt[:, 0:255], [xn], ['kcT'])
            else:
                for ct in range(2):
                    P.mm(xt[:, 0:128], [(hid[j][:, ct * 128:ct * 128 + 128], w2[:, 2 + j, :]) for j in range(2)],
                         ['w2', ('hid', 0), ('hid', 1)], [xn])
                    P.cp('vector', vc[:, ct, :], xt[:, 0:128], [xn], ['vc'])

        def load_gate(r, g):
            P.dma('sync', gsb[r][:, :], gates_d[r:r + 1, g * 512:(g + 1) * 512].partition_broadcast(128), [],
                  [('gs', r)], chan='gs%d' % r)
            P.act(gsb[r][:, :], gsb[r][:, :], AF.Sigmoid, [('gs', r)], [('gs', r)])
        aq = [T['aq%d' % h] for h in range(4)]
        ei = 0
        for g in range(8):
            qs = slice(g * 512, g * 512 + 512)
            cts = [0] if g < 4 else [0, 1]
            load_gate(0, g)
            for h in range(4):
                (on, ot), (ln, lt) = cx.next_acc()
                es_ = []
                for ci, ct in enumerate(cts):
                    (sn, st), _ = cx.next_s()
                    P.mm(st[:, :], [(kcT[:, ct * 128:ct * 128 + 128], aq[h][:, qs])], ['kcT', 'aq%d' % h], [sn])
                    e_ = ecm[ei % 4]
                    ek = ('ecm', ei % 4)
                    ei += 1
                    P.act(e_[:, :], st[:, :], AF.Exp, [sn], [ek])
                    mi = None
                    if ct == 0 and g <= 4:
                        mi = g
                    elif ct == 1:
                        mi = 5 + (g - 4)
                    if mi is not None:
                        P.tt('gpsimd', e_[:, :], e_[:, :], cmask[:, mi, :], ALU.mult, [ek, 'cmask'], [ek])
                    P.mm(lt[:, :], [(cx.ones[:, :], e_[:, :])], [ek, 'ones'], [ln], start=(ci == 0),
                         stop=(ci == len(cts) - 1))
                    if h == 0:
                        P.mm(ot[:, :], [(vc[:, ct, :], e_[:, :])], [ek, 'vc'], [on], start=(ci == 0),
                             stop=(ci == len(cts) - 1))
                    es_.append((ct, e_, ek))
                P.ts('vector', rl[:, :], lt[:, :], 1e-30, None, ALU.max, None, [ln], ['rl'])
                P.op('vector', lambda e: e.reciprocal(out=rl[:, :], in_=rl[:, :]), ['rl'], ['rl'])
                for ct, e_, ek in es_:
                    if h == 0:
                        P.tt('vector', PT[ct][:, :], e_[:, :], rl[:, :], ALU.mult, [ek, 'rl'], [('PT', ct)])
                    else:
                        P.tt('vector', tmp[:, :], e_[:, :], rl[:, :], ALU.mult, [ek, 'rl'], ['tmp'])
                        P.tt('vector', PT[ct][:, :], PT[ct][:, :], tmp[:, :], ALU.add, ['tmp', ('PT', ct)], [('PT', ct)])
                if h == 0:
                    P.tt('vector', tmp[:, :], ot[:, :], rl[:, :], ALU.mult, [on, 'rl'], ['tmp'])
                    P.tt('vector', res[:, qs], tmp[:, :], gsb[0][:, :], ALU.mult, ['tmp', ('gs', 0)], [('res', g)])
            for ct in cts:
                P.cp('gpsimd', PTb[ct][:, :], PT[ct][:, :], [('PT', ct)], [('PTb', ct)])
            for qt in range(4):
                Tq = 4 * g + qt
                P.mm(xt[:, 0:64], [(PTb[ct][:, qt * 128:qt * 128 + 128], ovl[:, ct, :]) for ct in cts],
                     [('PTb', ct) for ct in cts] + ['ovl'], [xn])
                P.tt('vector', sc[:, :], xt[:, 0:64], ab[:, 0, Tq, :], ALU.mult, [xn, 'ab'], ['sc'])
                P.tt('vector', sc[:, :], sc[:, :], ab[:, 1, Tq, :], ALU.add, ['sc', 'ab'], ['sc'])
                P.op('vector', lambda e: e.max(out=m8[:, 0:8], in_=sc[:, :]), ['sc'], ['m8'])
                P.op('vector', lambda e: e.match_replace(out=sc2[:, :], in_to_replace=m8[:, 0:8], in_values=sc[:, :],
                                                         imm_value=-1e30), ['sc', 'm8'], ['sc2'])
                P.op('vector', lambda e: e.max(out=m8[:, 8:16], in_=sc2[:, :]), ['sc2', 'm8'], ['m8'])
                P.ts('vector', selm[:, :], sc[:, :], m8[:, 15:16], -1.0, ALU.is_ge, ALU.add, ['sc', 'm8'], ['selm'])
                P.mm(xt[0:64, 128:256], [(selm[:, :], ident[:, :])], ['selm', 'ident'], [xn])
                P.cp('vector', selT[:, Tq * 128:Tq * 128 + 128], xt[0:64, 128:256], [xn], [('selT', g)])
        for phase in (3, 4):
            r_ = phase - 2
            kk = T['ks'] if phase == 3 else T['kw']
            kn = 'ks' if phase == 3 else 'kw'
            vv = T['vs'] if phase == 3 else T['vw']
            vn = 'vs' if phase == 3 else 'vw'
            for g in range(8):
                qs = slice(g * 512, g * 512 + 512)
                load_gate(r_, g)
                steps = []
                S0 = 0 if phase == 3 else max(0, 4 * g - 4)
                for S in range(S0, 4 * g + 4):
                    ks = slice(S * 128, S * 128 + 128)
                    pairs = [(kk[:, ks], T['aqr'][:, qs])]
                    rd = [kn, 'aqr']
                    if phase == 3:
                        pairs.append((eall[:, ks], selT[:, qs]))
                        rd += ['eall', ('selT', g)]
                    st = dict(pairs=pairs, reads=rd, v=vv[:, S, :], vreads=[vn])
                    if S >= 4 * g:
                        st['mask'] = mle[:, S - 4 * g, :]
                        st['mkey'] = 'mle'
                    elif phase == 4:
                        st['mask'] = mwin[:, S - (4 * g - 4), :]
                        st['mkey'] = 'mwin'
                    steps.append(st)
                (on, ot), (ln, lt) = attn_group(cx, steps)
                P.op('vector', lambda e, lt=lt: e.reciprocal(out=rl[:, :], in_=lt[:, :]), [ln], ['rl'])
                P.tt('vector', tmp[:, :], ot[:, :], rl[:, :], ALU.mult, [on, 'rl'], ['tmp'])
                P.tt('vector', tmp[:, :], tmp[:, :], gsb[r_][:, :], ALU.mult, ['tmp', ('gs', r_)], ['tmp'])
                if phase == 3:
                    P.tt('vector', res[:, qs], res[:, qs], tmp[:, :], ALU.add, ['tmp', ('res', g)], [('res', g)])
                else:
                    i, sb_ = ost.next()
                    P.tt('vector', sb_[:, :], res[:, qs], tmp[:, :], ALU.add, ['tmp', ('res', g)], [('ostg', i)])
                    ost.store(i, g)
        P.finish([('out', g) for g in range(8)])
        P.emit()
    return nc


def nsa_weights(l, p):
    w1k = np.ascontiguousarray(p['nsa_cmp_k1'][l].reshape(32, 128, 256).transpose(1, 0, 2).reshape(128, 8192))
    w1v = np.ascontiguousarray(p['nsa_cmp_v1'][l].reshape(32, 128, 256).transpose(1, 0, 2).reshape(128, 8192))
    w2 = np.stack([p['nsa_cmp_k2'][l][0:128], p['nsa_cmp_k2'][l][128:256],
                   p['nsa_cmp_v2'][l][0:128], p['nsa_cmp_v2'][l][128:256]], axis=1)
    pe = np.concatenate([p['nsa_pe_k'][l].T, p['nsa_pe_v'][l].T], axis=1)
    return dict(w1k=w1k, w1v=w1v, w2=np.ascontiguousarray(w2.astype(np.float32)),
                pe=np.ascontiguousarray(pe.astype(np.float32)))


_PROGS = {}


def _prog(name):
    if name not in _PROGS:
        _PROGS[name] = dict(A=build_A, C=build_C, nsa=build_nsa, sb=build_sb, fox=build_fox, diff=build_diff)[name]()
    return _PROGS[name]


def _run(name, in_maps):
    res = run_bass_kernel_spmd(_prog(name), in_maps, core_ids=list(range(8)))
    return res.results


def _halo(arr_b, j, n):
    t0 = j * n
    if j == 0:
        sl = np.concatenate([np.zeros((2, arr_b.shape[1]), arr_b.dtype), arr_b[0:n]], axis=0)
    else:
        sl = arr_b[t0 - 2:t0 + n]
    return np.ascontiguousarray(sl.T.reshape(16, 128, n + 2).transpose(1, 0, 2))


def kernel(x, norm_mix_pre, norm_mix_post, norm_mlp_pre, norm_mlp_post, w_in, w_out,
           nsa_pe_k, nsa_pe_v, nsa_cmp_k1, nsa_cmp_k2, nsa_cmp_v1, nsa_cmp_v2,
           fox_forget_bias, diff_lambda, diff_norm, mlp_w_up, mlp_conv_w, mlp_conv_b, mlp_w_down):
    p = dict(nsa_pe_k=np.asarray(nsa_pe_k), nsa_pe_v=np.asarray(nsa_pe_v), nsa_cmp_k1=np.asarray(nsa_cmp_k1),
             nsa_cmp_k2=np.asarray(nsa_cmp_k2), nsa_cmp_v1=np.asarray(nsa_cmp_v1), nsa_cmp_v2=np.asarray(nsa_cmp_v2))
    x = np.array(x, dtype=np.float32, copy=True)
    w_in = np.asarray(w_in); w_out = np.asarray(w_out); mlp_w_up = np.asarray(mlp_w_up)
    mlp_w_down = np.asarray(mlp_w_down); mlp_conv_w = np.asarray(mlp_conv_w); mlp_conv_b = np.asarray(mlp_conv_b)
    le, lt, win = causal_masks()
    mle, mlt, mwin = mask_layout(le), mask_layout(lt), mask_layout(win)
    nmask = mask_layout(((le.astype(np.float32) - 1.0) * BIGNEG).astype(NPBF))
    ident = np.eye(128, dtype=np.float32).astype(NPBF)
    ncst = nsa_consts()
    ntri = ntri_const()
    coef = fox_coef()
    tabs = [rope_tables(j * TOK, TOK) for j in range(4)]

    def fmaj(v):
        return np.ascontiguousarray(np.asarray(v, np.float32).reshape(16, 128).T)
    for l in range(DEPTH):
        wblk = host_w_in_blocks(w_in[l]).reshape(16, 128, 8192)
        gpre = fmaj(norm_mix_pre[l])
        in_maps = []
        for c in range(8):
            b, j = c // 4, c % 4
            xs = x[b, j * TOK:(j + 1) * TOK, :]
            xT = np.ascontiguousarray(xs.T.reshape(16, 128, TOK).transpose(1, 0, 2))
            in_maps.append(dict(xT=xT, gpre=gpre, wblk=wblk, tabs=tabs[j]))
        rA = _run('A', in_maps)
        del wblk, in_maps
        fm = [np.concatenate([np.asarray(rA[4 * b + j]['fm_out']) for j in range(4)], axis=2) for b in range(NB)]
        sm = [np.concatenate([np.asarray(rA[4 * b + j]['sm_out'])[0:16] for j in range(4)], axis=1) for b in range(NB)]
        tm = [np.concatenate([np.asarray(rA[4 * b + j]['tm_out']) for j in range(4)], axis=0) for b in range(NB)]
        del rA
        nw = nsa_weights(l, p)
        lam_init = 0.8 - 0.6 * math.exp(-0.3 * l)
        lamp = np.ascontiguousarray(np.tile(np.asarray(diff_lambda[l], np.float32).reshape(1, 256), (128, 1)))
        dn = np.ascontiguousarray(np.stack([np.asarray(diff_norm[l], np.float32),
                                            np.full(128, lam_init, np.float32)], axis=1))
        maps = dict(nsa=[], sb=[], fox=[], diff=[])
        for c in range(8):
            b, h = c // 4, c % 4
            f, s_, t_ = fm[b], sm[b], tm[b]
            order = [h] + [i for i in range(4) if i != h]
            m = dict(aqr=f[h], xk=f[8], xv=f[9], ks=f[10], kw=f[11], vs=tm_layout(t_[:, 0:128]),
                     vw=tm_layout(t_[:, 128:256]), gates=np.ascontiguousarray(s_[3 * h:3 * h + 3]), mle=mle, mwin=mwin)
            for i, hh in enumerate(order):
                m['aq%d' % i] = f[4 + hh]
            m.update(ncst)
            m.update(nw)
            maps['nsa'].append(m)
            maps['sb'].append(dict(q=f[12 + h], k=f[16 + h], v=tm_layout(t_[:, 256 + 128 * h:384 + 128 * h]),
                                   mlt=mlt, ntri=ntri))
            maps['fox'].append(dict(q=f[20 + h], k=f[24 + h], v=tm_layout(t_[:, 768 + 128 * h:896 + 128 * h]),
                                    nmask=nmask, ident=ident,
                                    cf=np.ascontiguousarray(np.tile(s_[12 + h][None, :], (6, 1))),
                                    fb=np.full((6, 1), np.asarray(fox_forget_bias)[l][h], np.float32), coef=coef))
            maps['diff'].append(dict(q=f[28 + h], k=f[32 + h], v=tm_layout(t_[:, 1280 + 128 * h:1408 + 128 * h]),
                                     mle=mle, lamp=lamp, dn=dn))
        o_b = [np.zeros((SEQ, D), NPBF) for _ in range(NB)]
        for mi, name in enumerate(('nsa', 'sb', 'fox', 'diff')):
            r = _run(name, maps[name])
            for c in range(8):
                b, h = c // 4, c % 4
                o_b[b][:, mi * 512 + h * 128:mi * 512 + h * 128 + 128] = np.asarray(r[c]['o_out']).T
            maps[name] = None
        del fm, sm, tm
        blocks, cw = host_c_weights(w_out[l], mlp_w_up[l], mlp_w_down[l], mlp_conv_w[l], mlp_conv_b[l])
        gn = np.ascontiguousarray(np.stack([fmaj(norm_mix_post[l]), fmaj(norm_mlp_pre[l]), fmaj(norm_mlp_post[l])],
                                           axis=1))
        in_maps = []
        for c in range(8):
            b, j = c // 4, c % 4
            in_maps.append(dict(xT=_halo(x[b], j, TOK), oT=_halo(o_b[b], j, TOK), gn=gn, cw=cw, wblk=blocks))
        rC = _run('C', in_maps)
        del blocks, in_maps
        for c in range(8):
            b, j = c // 4, c % 4
            xo = np.asarray(rC[c]['x_out'])
            x[b, j * TOK:(j + 1) * TOK, :] = xo.transpose(1, 0, 2).reshape(D, TOK).T
        del rC
    return x
```

```python
import math
import numpy as np
import ml_dtypes
import concourse.bass as bass
import concourse.mybir as mybir
from concourse.bass_utils import run_bass_kernel_spmd
from contextlib import ExitStack

F32 = mybir.dt.float32
BF16 = mybir.dt.bfloat16
ALU = mybir.AluOpType
AF = mybir.ActivationFunctionType
AX = mybir.AxisListType
NPBF = ml_dtypes.bfloat16

ENGS = ['sync', 'scalar', 'vector', 'gpsimd', 'tensor']

D = 2048
SEQ = 4096
NB = 2
DEPTH = 4
HD = 128
DFF = 5632
TOK = 1024
EPS = 1e-6
N_C = 255


class _Op:
    __slots__ = ('fn', 'waits', 'inc', 'chan', 'idx')

    def __init__(self, fn):
        self.fn = fn
        self.waits = []
        self.inc = False
        self.chan = None
        self.idx = 0


class Prog:
    def __init__(self, nc, es):
        self.nc = nc
        self.es = es
        self.ops = {e: [] for e in ENGS}
        self.chan_cnt = {}
        self.last_w = {}
        self.readers = {}
        self.waited = {e: {} for e in ENGS}

    def sb(self, name, shape, dt):
        return self.es.enter_context(self.nc.sbuf_tensor(name, list(shape), dt))

    def ps(self, name, shape, dt=F32):
        return self.es.enter_context(self.nc.psum_tensor(name, list(shape), dt))

    def _dep(self, eng, op, key, idx):
        if key == 'tensor' and eng == 'tensor':
            return
        if self.waited[eng].get(key, -1) >= idx:
            return
        self.waited[eng][key] = idx
        op.waits.append((key, idx))
        if key in self.ops:
            self.ops[key][idx].inc = True

    def op(self, eng, fn, reads=(), writes=(), chan=None):
        o = _Op(fn)
        for r in reads:
            t = self.last_w.get(r)
            if t is not None:
                self._dep(eng, o, t[0], t[1])
        for w in writes:
            t = self.last_w.get(w)
            if t is not None:
                self._dep(eng, o, t[0], t[1])
            rd = self.readers.get(w)
            if rd:
                for k, i in rd.items():
                    self._dep(eng, o, k, i)
        lst = self.ops[eng]
        o.idx = len(lst)
        lst.append(o)
        if chan is not None:
            o.chan = chan
            c = self.chan_cnt.get(chan, 0) + 1
            self.chan_cnt[chan] = c
            tok = (('dma', chan), c)
        else:
            tok = (eng, o.idx)
        for r in reads:
            rd = self.readers.setdefault(r, {})
            if rd.get(tok[0], -1) < tok[1]:
                rd[tok[0]] = tok[1]
        for w in writes:
            self.last_w[w] = tok
            self.readers[w] = {}
        return tok

    def dma(self, eng, out, in_, reads=(), writes=(), chan=None):
        return self.op(eng, lambda e: e.dma_start(out=out, in_=in_), reads, writes, chan=chan)

    def finish(self, keys, eng='sync'):
        o = _Op(None)
        for k in keys:
            t = self.last_w.get(k)
            if t is not None:
                self._dep(eng, o, t[0], t[1])
        o.idx = len(self.ops[eng])
        self.ops[eng].append(o)

    def emit(self):
        nc = self.nc
        es = self.es
        sems = {}
        for e in ENGS:
            sems[e] = es.enter_context(nc.semaphore('s_' + e))
        for i, ch in enumerate(self.chan_cnt):
            sems[('dma', ch)] = es.enter_context(nc.semaphore('d%d' % i))
        val = {}
        for e in ENGS:
            c = 0
            v = []
            for o in self.ops[e]:
                if o.inc and o.chan is None:
                    c += 1
                v.append(c)
            val[e] = v
        block = es.enter_context(nc.Block())

        def mk(ename):
            def body(eng):
                for o in self.ops[ename]:
                    for key, idx in o.waits:
                        if key in val:
                            eng.wait_ge(sems[key], val[key][idx])
                        else:
                            eng.wait_ge(sems[key], 16 * idx)
                    if o.fn is None:
                        continue
                    ins = o.fn(eng)
                    if o.chan is not None:
                        ins.then_inc(sems[('dma', o.chan)], 16)
                    elif o.inc:
                        ins.then_inc(sems[ename], 1)
            return body
        for e in ENGS:
            if self.ops[e]:
                getattr(block, e)(mk(e))

    def mm(self, ps_ap, pairs, reads, writes, start=True, stop=True):
        pairs = list(pairs)

        def fn(e):
            n = len(pairs)
            ins = None
            for i, (l, r) in enumerate(pairs):
                ins = e.matmul(ps_ap, lhsT=l, rhs=r, start=(start and i == 0),
                               stop=(stop and i == n - 1))
            return ins
        return self.op('tensor', fn, reads, writes)

    def act(self, out, in_, func, reads, writes, bias=None, scale=None, eng='scalar'):
        kw = {}
        if bias is not None:
            kw['bias'] = bias
        if scale is not None:
            kw['scale'] = scale
        return self.op(eng, lambda e: e.activation(out=out, in_=in_, func=func, **kw), reads, writes)

    def tt(self, eng, out, in0, in1, op, reads, writes):
        return self.op(eng, lambda e: e.tensor_tensor(out=out, in0=in0, in1=in1, op=op), reads, writes)

    def ts(self, eng, out, in0, s1, s2, op0, op1, reads, writes):
        if s2 is None:
            return self.op(eng, lambda e: e.tensor_scalar(out=out, in0=in0, scalar1=s1, scalar2=None, op0=op0),
                           reads, writes)
        return self.op(eng, lambda e: e.tensor_scalar(out=out, in0=in0, scalar1=s1, scalar2=s2, op0=op0, op1=op1),
                       reads, writes)

    def stt(self, eng, out, in0, scalar, in1, op0, op1, reads, writes):
        return self.op(eng, lambda e: e.scalar_tensor_tensor(out=out, in0=in0, scalar=scalar, in1=in1,
                                                              op0=op0, op1=op1), reads, writes)

    def cp(self, eng, out, in_, reads, writes):
        if eng == 'scalar':
            return self.op(eng, lambda e: e.copy(out=out, in_=in_), reads, writes)
        return self.op(eng, lambda e: e.tensor_copy(out=out, in_=in_), reads, writes)

    def memset(self, eng, ap, v, writes):
        return self.op(eng, lambda e: e.memset(ap, v), [], writes)


class PsumRot:
    def __init__(self, P, names, n):
        self.tiles = [(nm, P.ps(nm, [128, 512])) for nm in names[:n]]
        self.i = 0

    def next(self):
        t = self.tiles[self.i % len(self.tiles)]
        self.i += 1
        return t


def rms_stats(P, src_fn, nk, N, ones_bf, sq_bufs, ps_name, ps_tile, rstd_ap, rstd_key, src_keys, tag):
    for k in range(nk):
        nm, sq = sq_bufs[k % len(sq_bufs)]
        P.act(sq[:, 0:N], src_fn(k), AF.Square, [src_keys(k)], [nm])
        P.mm(ps_tile[:, 0:N], [(ones_bf[:, :], sq[:, 0:N])], [nm, 'ones'], [ps_name],
             start=(k == 0), stop=(k == nk - 1))
    P.act(rstd_ap, ps_tile[:, 0:N], AF.Sqrt, [ps_name], [rstd_key], bias=EPS, scale=1.0 / (nk * 128))
    P.op('vector', lambda e: e.reciprocal(out=rstd_ap, in_=rstd_ap), [rstd_key], [rstd_key])


OFF = dict(a_q=0, a_kc=512, a_vc=640, a_ks=768, a_vs=896, a_kw=1024, a_vw=1152, a_g=1280,
           b_q=1292, b_k=1804, b_v=2316, c_q=2828, c_k=3340, c_v=3852, c_f=4364,
           d_q=4368, d_k=4880, d_v=5392)
SC128 = 128 ** -0.5
SC64 = 64 ** -0.5
NFM = 36


def _swap_h(cols):
    p = np.arange(128)
    p[:16] += 16
    p[16:32] -= 16
    return cols[p]


def _swap_d(cols):
    p = np.arange(128)
    for b in (0, 64):
        p[b:b + 8] += 8
        p[b + 8:b + 16] -= 8
    return cols[p]


def a_plan():
    tiles = []
    def c128(o):
        return np.arange(o, o + 128)
    for h in range(4):
        c = c128(OFF['a_q'] + 128 * h)
        tiles.append((c, ('rope', 'H', SC128, h, 4 + h)))
        tiles.append((_swap_h(c), ('swap',)))
    tiles.append((c128(OFF['a_kc']), ('plain', 1.0, 8)))
    tiles.append((c128(OFF['a_vc']), ('plain', 1.0, 9)))
    for nm, oi in (('a_ks', 10), ('a_kw', 11)):
        c = c128(OFF[nm])
        tiles.append((c, ('rope', 'H', 1.0, oi, None)))
        tiles.append((_swap_h(c), ('swap',)))
    for nm, o0, sc in (('b_q', 12, SC128), ('b_k', 16, 1.0), ('c_q', 20, SC128), ('c_k', 24, 1.0)):
        for h in range(4):
            tiles.append((c128(OFF[nm] + 128 * h), ('plain', sc, o0 + h)))
    for nm, o0, sc in (('d_q', 28, SC64), ('d_k', 32, 1.0)):
        for h in range(4):
            c = c128(OFF[nm] + 128 * h)
            tiles.append((c, ('rope', 'D', sc, o0 + h, None)))
            tiles.append((_swap_d(c), ('swap',)))
    sm = np.zeros(128, np.int64)
    sm[:12] = np.arange(OFF['a_g'], OFF['a_g'] + 12)
    sm[12:16] = np.arange(OFF['c_f'], OFF['c_f'] + 4)
    tiles.append((sm, ('small',)))
    tiles.append((np.zeros(128, np.int64), ('skip',)))
    assert len(tiles) == 48
    tm = np.concatenate([np.arange(OFF['a_vs'], OFF['a_vs'] + 128), np.arange(OFF['a_vw'], OFF['a_vw'] + 128),
                         np.arange(OFF['b_v'], OFF['b_v'] + 512), np.arange(OFF['c_v'], OFF['c_v'] + 512),
                         np.arange(OFF['d_v'], OFF['d_v'] + 512), np.zeros(256, np.int64)])
    return tiles, tm


NTM = 1792


def host_w_in_blocks(w_in_l):
    tiles, tm = a_plan()
    cols = np.concatenate([t[0] for t in tiles] + [tm])
    w = w_in_l[:, cols]
    w = w.reshape(16, 128, 16, 512)
    return np.ascontiguousarray(w.transpose(2, 1, 0, 3))


def rope_tables(t0, n):
    pos = np.arange(t0, t0 + n, dtype=np.float32)
    theta = np.float32(500000.0)

    def tab(rot, period, ):
        half = rot // 2
        inv = theta ** (-np.arange(0, rot, 2, dtype=np.float32) / np.float32(rot))
        ang = (pos[:, None] * inv[None, :]).astype(np.float32)
        cos = np.cos(ang).T.astype(np.float32)
        sin = np.sin(ang).T.astype(np.float32)
        ct = np.ones((128, n), np.float32)
        st = np.zeros((128, n), np.float32)
        for b in range(0, 128, period):
            ct[b:b + half] = cos
            ct[b + half:b + rot] = cos
            st[b:b + half] = -sin
            st[b + half:b + rot] = sin
        return ct, st
    cH, sH = tab(32, 128)
    cD, sD = tab(16, 64)
    return np.ascontiguousarray(np.stack([cH * np.float32(SC128), sH * np.float32(SC128), cH, sH,
                                          cD * np.float32(SC64), sD * np.float32(SC64), cD, sD],
                                         axis=1).astype(np.float32))


def build_A():
    nc = bass.Bass("TRN2", target_bir_lowering=False)
    xT = nc.dram_tensor("xT", [128, 16, TOK], F32, kind="ExternalInput").ap()
    gpre = nc.dram_tensor("gpre", [128, 16], F32, kind="ExternalInput").ap()
    wblk = nc.dram_tensor("wblk", [16, 128, 16 * 512], F32, kind="ExternalInput").ap()
    tabs = nc.dram_tensor("tabs", [128, 8, TOK], F32, kind="ExternalInput").ap()
    fm_out = nc.dram_tensor("fm_out", [NFM, 128, TOK], BF16, kind="ExternalOutput").ap()
    sm_out = nc.dram_tensor("sm_out", [128, TOK], F32, kind="ExternalOutput").ap()
    tm_out = nc.dram_tensor("tm_out", [TOK, NTM], BF16, kind="ExternalOutput").ap()
    tiles, _ = a_plan()
    with ExitStack() as es:
        P = Prog(nc, es)
        x_sb = P.sb("x_sb", [128, 16, TOK], F32)
        hin = P.sb("hin", [128, 16, TOK], BF16)
        g_sb = P.sb("g_sb", [128, 16], F32)
        tab_sb = P.sb("tab_sb", [128, 8, TOK], F32)
        ones = P.sb("ones", [128, 128], BF16)
        rstd = P.sb("rstd", [128, 512], F32)
        sqb = [("sq%d" % i, P.sb("sq%d" % i, [128, 512], BF16)) for i in range(3)]
        NWB = 3
        wb = [P.sb("wb%d" % i, [128, 16 * 512], BF16) for i in range(NWB)]
        stg = [P.sb("stg%d" % i, [128, 512], BF16) for i in range(4)]
        t1 = [P.sb("t1_%d" % i, [128, 512], F32) for i in range(2)]
        t2 = [P.sb("t2_%d" % i, [128, 512], F32) for i in range(2)]
        smb = P.sb("smb", [128, TOK], F32)
        rot = PsumRot(P, ["ps%d" % i for i in range(7)], 7)
        ps_ss = P.ps("ps_ss", [128, 512])

        P.memset('vector', ones[:, :], 1.0, ['ones'])
        for q in range(4):
            P.dma('sync', x_sb[:, 4 * q:4 * q + 4, :], xT[:, 4 * q:4 * q + 4, :], [], [('x', q)], chan='x%d' % q)
        P.dma('sync', g_sb[:, :], gpre, [], ['g'], chan='g')
        P.dma('sync', tab_sb[:, :, :], tabs, [], ['tab'], chan='tab')
        wq = {'next': 0}

        def load_w(b):
            i = b % NWB
            P.dma('gpsimd', wb[i][:, :], wblk[b], [], [('wb', i)], chan='wb%d' % i)
        for b in range(NWB):
            load_w(b)
        for tg in range(2):
            sl = slice(tg * 512, tg * 512 + 512)
            rms_stats(P, lambda k: x_sb[:, k, sl], 16, 512, ones, sqb, 'ps_ss', ps_ss, rstd[:, :], 'rstd',
                      lambda k: ('x', k // 4), 'A')
            for k in range(16):
                P.stt('vector', hin[:, k, sl], x_sb[:, k, sl], g_sb[:, k:k + 1], rstd[:, :], ALU.mult, ALU.mult,
                      [('x', k // 4), 'g', 'rstd'], [('hin', tg)])
        sti = [0]

        def stage_out(dst_ap):
            i = sti[0] % 4
            sti[0] += 1
            return i, stg[i]
        pend = {}
        for b in range(12):
            bi = b % NWB
            for ct in range(4):
                cols, job = tiles[b * 4 + ct]
                if job[0] == 'skip':
                    continue
                for tg in range(2):
                    sl = slice(tg * 512, tg * 512 + 512)
                    pn, pt = rot.next()
                    P.mm(pt[:, :], [(wb[bi][:, k * 512 + ct * 128:k * 512 + ct * 128 + 128], hin[:, k, sl])
                                    for k in range(16)], [('wb', bi), ('hin', tg)], [pn])
                    if job[0] == 'plain':
                        i, st = stage_out(None)
                        if job[1] == 1.0:
                            P.cp('scalar', st[:, :], pt[:, :], [pn], [('stg', i)])
                        else:
                            P.ts('vector', st[:, :], pt[:, :], float(job[1]), None, ALU.mult, None, [pn], [('stg', i)])
                        P.dma('sync', fm_out[job[2], :, sl], st[:, :], [('stg', i)], [('fm', job[2], tg)],
                              chan='stg%d' % i)
                    elif job[0] == 'small':
                        P.cp('scalar', smb[:, sl], pt[:, :], [pn], [('smb', tg)])
                        P.dma('sync', sm_out[:, sl], smb[:, sl], [('smb', tg)], [('sm', tg)], chan='smo%d' % tg)
                    elif job[0] == 'rope':
                        pend[tg] = (pn, pt, job)
                    elif job[0] == 'swap':
                        mn, mt, mj = pend[tg]
                        tb = (0 if mj[1] == 'H' else 4) + (0 if mj[2] != 1.0 else 2)
                        j = tg
                        P.tt('vector', t1[j][:, :], mt[:, :], tab_sb[:, tb, sl], ALU.mult, [mn, 'tab'], [('t1', j)])
                        P.tt('vector', t2[j][:, :], pt[:, :], tab_sb[:, tb + 1, sl], ALU.mult, [pn, 'tab'], [('t2', j)])
                        i, st = stage_out(None)
                        P.tt('vector', st[:, :], t1[j][:, :], t2[j][:, :], ALU.add, [('t1', j), ('t2', j)], [('stg', i)])
                        P.dma('sync', fm_out[mj[3], :, sl], st[:, :], [('stg', i)], [('fm', mj[3], tg)],
                              chan='stg%d' % i)
                        if mj[4] is not None:
                            i, st = stage_out(None)
                            P.ts('vector', st[:, :], mt[:, :], float(mj[2]), None, ALU.mult, None, [mn], [('stg', i)])
                            P.dma('sync', fm_out[mj[4], :, sl], st[:, :], [('stg', i)], [('fm', mj[4], tg)],
                                  chan='stg%d' % i)
            if b + NWB < 16:
                load_w(b + NWB)
        for b in range(12, 16):
            bi = b % NWB
            ncol = 512 if b < 15 else 256
            c0 = (b - 12) * 512
            for tt_ in range(8):
                tg = tt_ // 4
                pn, pt = rot.next()
                P.mm(pt[:, 0:ncol], [(hin[:, k, tt_ * 128:tt_ * 128 + 128], wb[bi][:, k * 512:k * 512 + ncol])
                                     for k in range(16)], [('wb', bi), ('hin', tg)], [pn])
                i, st = stage_out(None)
                P.cp('scalar', st[:, 0:ncol], pt[:, 0:ncol], [pn], [('stg', i)])
                P.dma('sync', tm_out[tt_ * 128:tt_ * 128 + 128, c0:c0 + ncol], st[:, 0:ncol], [('stg', i)],
                      [('tm', b, tt_)], chan='stg%d' % i)
            if b + NWB < 16:
                load_w(b + NWB)
        P.finish([k for k in P.last_w if isinstance(k, tuple) and k[0] in ('fm', 'sm', 'tm')])
        P.emit()
    return nc


NBLK_C = 42


def host_c_weights(w_out_l, w_up_l, w_down_l, conv_w_l, conv_b_l):
    blocks = np.zeros((NBLK_C, 128, 8192), np.float32)
    wo = w_out_l.reshape(16, 128, 4, 512)
    blocks[0:4] = wo.transpose(2, 1, 0, 3).reshape(4, 128, 8192)
    ucols = []
    for j in range(22):
        for ct in range(4):
            i = 2 * j + ct // 2
            base = i * 128 if ct % 2 == 0 else DFF + i * 128
            ucols.append(np.arange(base, base + 128))
    ucols = np.concatenate(ucols)
    wu = w_up_l[:, ucols].reshape(16, 128, 22, 512)
    blocks[4:26] = wu.transpose(2, 1, 0, 3).reshape(22, 128, 8192)
    wd = w_down_l.reshape(44, 128, 16, 128)
    blocks[26:42, :, 0:44 * 128] = wd.transpose(2, 1, 0, 3).reshape(16, 128, 44 * 128)
    cw = np.zeros((128, 88, 4), np.float32)
    cc = ucols.reshape(88, 128)
    for tap in range(3):
        cw[:, :, tap] = conv_w_l[tap][cc].T
    cw[:, :, 3] = conv_b_l[cc].T
    return blocks, cw


def build_C():
    nc = bass.Bass("TRN2", target_bir_lowering=False)
    NT = TOK + 2
    xT = nc.dram_tensor("xT", [128, 16, NT], F32, kind="ExternalInput").ap()
    oT = nc.dram_tensor("oT", [128, 16, NT], BF16, kind="ExternalInput").ap()
    gn = nc.dram_tensor("gn", [128, 3, 16], F32, kind="ExternalInput").ap()
    cwd = nc.dram_tensor("cw", [128, 88, 4], F32, kind="ExternalInput").ap()
    wblk = nc.dram_tensor("wblk", [NBLK_C, 128, 8192], F32, kind="ExternalInput").ap()
    x_out = nc.dram_tensor("x_out", [128, 16, TOK], F32, kind="ExternalOutput").ap()
    with ExitStack() as es:
        P = Prog(nc, es)
        xg = P.sb("xg", [128, 16, 512], F32)
        og = P.sb("og", [128, 16, 512], BF16)
        mixed = P.sb("mixed", [128, 16, 512], F32)
        hin2 = P.sb("hin2", [128, 16, 512], BF16)
        hT = P.sb("hT", [128, 44, 512], BF16)
        g_sb = P.sb("g_sb", [128, 3, 16], F32)
        cw = P.sb("cw_sb", [128, 88, 4], F32)
        utail = P.sb("utail", [128, 88, 2], F32)
        ones = P.sb("ones", [128, 128], BF16)
        rstd = P.sb("rstd", [128, 512], F32)
        sqb = [("sq%d" % i, P.sb("sq%d" % i, [128, 512], BF16)) for i in range(2)]
        NWB = 3
        wb = [P.sb("wb%d" % i, [128, 8192], BF16) for i in range(NWB)]
        ubuf = [P.sb("ubuf%d" % i, [128, 516], F32) for i in range(2)]
        cbuf = [P.sb("cbuf%d" % i, [128, 512], F32) for i in range(2)]
        gbuf = [P.sb("gbuf%d" % i, [128, 512], F32) for i in range(2)]
        rot = PsumRot(P, ["ps%d" % i for i in range(7)], 7)
        ps_ss = P.ps("ps_ss", [128, 512])

        P.memset('vector', ones[:, :], 1.0, ['ones'])
        P.dma('sync', g_sb[:, :, :], gn, [], ['g'], chan='g')
        P.dma('sync', cw[:, :, :], cwd, [], ['cw'], chan='cw')
        xh = P.sb("xh", [128, 16, 2], F32)
        oh = P.sb("oh", [128, 16, 2], BF16)
        mixh = P.sb("mixh", [128, 16, 2], F32)
        hin2h = P.sb("hin2h", [128, 16, 2], BF16)
        BUF = dict(G=(xg, og, mixed, hin2), H=(xh, oh, mixh, hin2h))
        passes = [[('H', 0, 2), ('G', 2, 512)], [('G', 514, 512)]]
        seq = list(range(42)) * 2
        wstate = {'issued': 0}

        def load_next():
            n = wstate['issued']
            if n >= len(seq):
                return
            b = seq[n]
            i = n % NWB
            ne = 8192 if b < 26 else 44 * 128
            P.dma('gpsimd', wb[i][:, 0:ne], wblk[b, :, 0:ne], [], [('wb', i)], chan='wb%d' % i)
            wstate['issued'] = n + 1
        used = [0]

        def cur_w():
            i = used[0] % NWB
            used[0] += 1
            return i
        for _ in range(NWB):
            load_next()
        for pi, subs in enumerate(passes):
            for (tg, c0, N) in subs:
                xb, ob, mb, hb = BUF[tg]
                if tg == 'G':
                    for q in range(4):
                        P.dma('sync', xb[:, 4 * q:4 * q + 4, 0:N], xT[:, 4 * q:4 * q + 4, c0:c0 + N], [],
                              [('xg', tg, q)], chan='xg%d' % q)
                else:
                    P.dma('sync', xb[:, :, 0:N], xT[:, :, c0:c0 + N], [], [('xg', tg, q) for q in range(4)],
                          chan='xh')
                P.dma('sync', ob[:, :, 0:N], oT[:, :, c0:c0 + N], [], ['og' + tg], chan='og' + tg)
            for j in range(4):
                bi = cur_w()
                for ct in range(4):
                    m = 4 * j + ct
                    for (tg, c0, N) in subs:
                        xb, ob, mb, hb = BUF[tg]
                        pn, pt = rot.next()
                        P.mm(pt[:, 0:N], [(wb[bi][:, k * 512 + ct * 128:k * 512 + ct * 128 + 128], ob[:, k, 0:N])
                                          for k in range(16)], [('wb', bi), 'og' + tg], [pn])
                        P.cp('scalar', mb[:, m, 0:N], pt[:, 0:N], [pn], [('mx', tg, m)])
                load_next()
            for (tg, c0, N) in subs:
                xb, ob, mb, hb = BUF[tg]
                rms_stats(P, lambda k: mb[:, k, 0:N], 16, N, ones, sqb, 'ps_ss', ps_ss, rstd[:, 0:N], 'rstd',
                          lambda k: ('mx', tg, k), 'C1')
                for m in range(16):
                    P.stt('vector', mb[:, m, 0:N], mb[:, m, 0:N], g_sb[:, 0, m:m + 1], rstd[:, 0:N],
                          ALU.mult, ALU.mult, [('mx', tg, m), 'g', 'rstd'], [('mx', tg, m)])
                    P.tt('vector', xb[:, m, 0:N], xb[:, m, 0:N], mb[:, m, 0:N], ALU.add,
                         [('mx', tg, m), ('xg', tg, m // 4)], [('xg', tg, m // 4)])
                rms_stats(P, lambda k: xb[:, k, 0:N], 16, N, ones, sqb, 'ps_ss', ps_ss, rstd[:, 0:N], 'rstd',
                          lambda k: ('xg', tg, k // 4), 'C2')
                for m in range(16):
                    P.stt('vector', hb[:, m, 0:N], xb[:, m, 0:N], g_sb[:, 1, m:m + 1], rstd[:, 0:N],
                          ALU.mult, ALU.mult, [('xg', tg, m // 4), 'g', 'rstd'], ['hin2' + tg])
            for j in range(22):
                bi = cur_w()
                for ct in range(4):
                    ut = 4 * j + ct
                    i = 2 * j + ct // 2
                    for (tg, c0, N) in subs:
                        xb, ob, mb, hb = BUF[tg]
                        pn, pt = rot.next()
                        P.mm(pt[:, 0:N], [(wb[bi][:, k * 512 + ct * 128:k * 512 + ct * 128 + 128], hb[:, k, 0:N])
                                          for k in range(16)], [('wb', bi), 'hin2' + tg], [pn])
                        if tg == 'H':
                            P.cp('scalar', utail[:, ut, :], pt[:, 0:2], [pn], [('ut', ut)])
                            continue
                        u = ubuf[ut % 2]
                        uk = ('ub', ut % 2)
                        c = cbuf[ut % 2]
                        ck = ('cb', ut % 2)
                        P.cp('scalar', u[:, 2:2 + N], pt[:, 0:N], [pn], [uk])
                        P.cp('vector', u[:, 0:2], utail[:, ut, :], [('ut', ut)], [uk])
                        P.ts('vector', c[:, 0:N], u[:, 2:2 + N], cw[:, ut, 2:3], cw[:, ut, 3:4], ALU.mult, ALU.add,
                             [uk, 'cw'], [ck])
                        P.stt('vector', c[:, 0:N], u[:, 1:1 + N], cw[:, ut, 1:2], c[:, 0:N], ALU.mult, ALU.add,
                              [uk, 'cw', ck], [ck])
                        P.stt('vector', c[:, 0:N], u[:, 0:N], cw[:, ut, 0:1], c[:, 0:N], ALU.mult, ALU.add,
                              [uk, 'cw', ck], [ck])
                        P.cp('vector', utail[:, ut, :], u[:, N:N + 2], [uk], [('ut', ut)])
                        gb = gbuf[i % 2]
                        if ct % 2 == 0:
                            P.act(gb[:, 0:N], c[:, 0:N], AF.Gelu_apprx_tanh, [ck], [('gb', i % 2)])
                        else:
                            P.tt('vector', hT[:, i, 0:N], gb[:, 0:N], c[:, 0:N], ALU.mult, [('gb', i % 2), ck],
                                 [('h', i)])
                load_next()
            tg, c0, N = subs[-1]
            xb, ob, mb, hb = BUF[tg]
            for m in range(16):
                bi = cur_w()
                pn, pt = rot.next()
                P.mm(pt[:, 0:N], [(wb[bi][:, i * 128:i * 128 + 128], hT[:, i, 0:N]) for i in range(44)],
                     [('wb', bi)] + [('h', i) for i in range(44)], [pn])
                P.cp('scalar', mb[:, m, 0:N], pt[:, 0:N], [pn], [('mx', tg, m)])
                load_next()
            rms_stats(P, lambda k: mb[:, k, 0:N], 16, N, ones, sqb, 'ps_ss', ps_ss, rstd[:, 0:N], 'rstd',
                      lambda k: ('mx', tg, k), 'C3')
            for m in range(16):
                P.stt('vector', mb[:, m, 0:N], mb[:, m, 0:N], g_sb[:, 2, m:m + 1], rstd[:, 0:N],
                      ALU.mult, ALU.mult, [('mx', tg, m), 'g', 'rstd'], [('mx', tg, m)])
                P.tt('vector', xb[:, m, 0:N], xb[:, m, 0:N], mb[:, m, 0:N], ALU.add,
                     [('mx', tg, m), ('xg', tg, m // 4)], [('xg', tg, m // 4)])
            for q in range(4):
                P.dma('sync', x_out[:, 4 * q:4 * q + 4, c0 - 2:c0 - 2 + N], xb[:, 4 * q:4 * q + 4, 0:N],
                      [('xg', tg, q)], [('xo', pi, q)], chan='xo%d' % q)
        P.finish([k for k in P.last_w if isinstance(k, tuple) and k[0] == 'xo'])
        P.emit()
    return nc


BIGNEG = 30000.0


def causal_masks():
    s = np.arange(128)[:, None]
    t = np.arange(512)[None, :]
    le = np.stack([(128 * r + s <= t) for r in range(4)]).astype(np.float32)
    lt = np.stack([(128 * r + s < t) for r in range(4)]).astype(np.float32)
    win = np.stack([(t < s + 128 * r) for r in range(4)]).astype(np.float32)
    return le.astype(NPBF), lt.astype(NPBF), win.astype(NPBF)


class AttnCtx:
    def __init__(self, P):
        self.P = P
        self.pss = [("pss%d" % i, P.ps("pss%d" % i, [128, 512])) for i in range(3)]
        self.pso = [("pso%d" % i, P.ps("pso%d" % i, [128, 512])) for i in range(2)]
        self.psl = [("psl%d" % i, P.ps("psl%d" % i, [128, 512])) for i in range(2)]
        self.psx = ("psx", P.ps("psx", [128, 512]))
        self.pT = [("pT%d" % i, P.sb("pT%d" % i, [128, 512], BF16)) for i in range(6)]
        self.ones = P.sb("ones", [128, 128], BF16)
        P.memset('vector', self.ones[:, :], 1.0, ['ones'])
        self.si = 0
        self.gi = 0

    def next_s(self):
        t = self.pss[self.si % 3]
        p = self.pT[self.si % 6]
        self.si += 1
        return t, p

    def next_acc(self):
        o = self.pso[self.gi % 2]
        l = self.psl[self.gi % 2]
        self.gi += 1
        return o, l


def attn_group(cx, steps, want_l=True, lag=2):
    P = cx.P
    (on, ot), (ln, lt) = cx.next_acc()
    n = len(steps)
    slots = []
    for i in range(n + lag):
        if i < n:
            st = steps[i]
            (sn, stile), (pn, pt) = cx.next_s()
            P.mm(stile[:, :], st['pairs'], st['reads'], [sn])
            P.act(pt[:, :], stile[:, :], AF.Exp, [sn], [pn])
            if st.get('mask') is not None:
                P.tt('gpsimd', pt[:, :], pt[:, :], st['mask'], ALU.mult, [pn, st['mkey']], [pn])
            slots.append((pn, pt))
        j = i - lag
        if j >= 0:
            st = steps[j]
            pn, pt = slots[j]
            P.mm(ot[:, :], [(st['v'], pt[:, :])], [pn] + st['vreads'], [on], start=(j == 0), stop=(j == n - 1))
            if want_l:
                P.mm(lt[:, :], [(cx.ones[:, :], pt[:, :])], [pn, 'ones'], [ln], start=(j == 0), stop=(j == n - 1))
    return (on, ot), (ln, lt)


def load_qkv(P, nc, names, with_v=True):
    out = {}
    for nm in names:
        if nm.startswith('v'):
            d_ = nc.dram_tensor(nm, [128, 32, 128], BF16, kind="ExternalInput").ap()
            s_ = P.sb(nm + "_sb", [128, 32, 128], BF16)
            P.dma('sync', s_[:, :, :], d_, [], [nm], chan=nm)
        else:
            d_ = nc.dram_tensor(nm, [128, SEQ], BF16, kind="ExternalInput").ap()
            s_ = P.sb(nm + "_sb", [128, SEQ], BF16)
            P.dma('sync', s_[:, :], d_, [], [nm], chan=nm)
        out[nm] = s_
    return out


def load_masks(P, nc, name):
    d_ = nc.dram_tensor(name, [128, 4, 512], BF16, kind="ExternalInput").ap()
    s_ = P.sb(name + "_sb", [128, 4, 512], BF16)
    P.dma('sync', s_[:, :, :], d_, [], [name], chan=name)
    return s_


class OutStage:
    def __init__(self, P, out_ap, n=2):
        self.P = P
        self.out = out_ap
        self.bufs = [P.sb("ostg%d" % i, [128, 512], BF16) for i in range(n)]
        self.i = 0

    def next(self):
        i = self.i % len(self.bufs)
        self.i += 1
        return i, self.bufs[i]

    def store(self, i, g):
        self.P.dma('sync', self.out[:, g * 512:(g + 1) * 512], self.bufs[i][:, :], [('ostg', i)], [('out', g)],
                   chan='ostg%d' % i)


def build_fox():
    nc = bass.Bass("TRN2", target_bir_lowering=False)
    cf = nc.dram_tensor("cf", [6, SEQ], F32, kind="ExternalInput").ap()
    fb = nc.dram_tensor("fb", [6, 1], F32, kind="ExternalInput").ap()
    coef = nc.dram_tensor("coef", [6, 8], F32, kind="ExternalInput").ap()
    o_out = nc.dram_tensor("o_out", [128, SEQ], BF16, kind="ExternalOutput").ap()
    with ExitStack() as es:
        P = Prog(nc, es)
        cx = AttnCtx(P)
        T = load_qkv(P, nc, ['q', 'k', 'v'])
        nmask = load_masks(P, nc, 'nmask')
        ident_d = nc.dram_tensor("ident", [128, 128], BF16, kind="ExternalInput").ap()
        ident = P.sb("ident_sb", [128, 128], BF16)
        P.dma('sync', ident[:, :], ident_d, [], ['ident'], chan='ident')
        ost = OutStage(P, o_out)
        cf_sb = P.sb("cf_sb", [6, SEQ], F32)
        w1 = P.sb("w1", [6, SEQ], F32)
        w2 = P.sb("w2", [6, SEQ], F32)
        w3 = P.sb("w3", [6, SEQ], F32)
        hb = P.sb("hb", [6, SEQ], BF16)
        hi = P.sb("hi", [6, SEQ], F32)
        mid = P.sb("mid", [6, SEQ], F32)
        lo = P.sb("lo", [6, SEQ], F32)
        kaug = P.sb("kaug", [6, SEQ], BF16)
        qaug = P.sb("qaug", [6, SEQ], BF16)
        fb_sb = P.sb("fb_sb", [6, 1], F32)
        co = P.sb("co", [6, 8], F32)
        rl = P.sb("rl", [128, 512], F32)
        P.dma('sync', cf_sb[:, :], cf, [], ['cf'], chan='cf')
        P.dma('sync', fb_sb[:, :], fb, [], ['fb'], chan='fb')
        P.dma('sync', co[:, :], coef, [], ['co'], chan='co')
        P.ts('vector', fb_sb[:, :], fb_sb[:, :], -1.0, None, ALU.mult, None, ['fb'], ['fb'])
        P.act(w1[:, :], cf_sb[:, :], AF.Exp, ['cf', 'fb'], ['w1'], bias=fb_sb[:, 0:1], scale=-1.0)
        P.act(w1[:, :], w1[:, :], AF.Ln, ['w1'], ['w1'], bias=1.0)
        P.ts('vector', w1[:, :], w1[:, :], -1.0, None, ALU.mult, None, ['w1'], ['w1'])
        P.memset('vector', w2[:, :], 1.0, ['w2'])
        P.op('vector', lambda e: e.tensor_tensor_scan(out=w3[:, :], data0=w2[:, :], data1=w1[:, :], initial=0.0,
                                                      op0=ALU.mult, op1=ALU.add), ['w1', 'w2'], ['w3'])
        P.cp('vector', hb[:, :], w3[:, :], ['w3'], ['hb'])
        P.cp('vector', hi[:, :], hb[:, :], ['hb'], ['hi'])
        P.tt('vector', w1[:, :], w3[:, :], hi[:, :], ALU.subtract, ['w3', 'hi'], ['w1'])
        P.cp('vector', hb[:, :], w1[:, :], ['w1'], ['hb'])
        P.cp('vector', mid[:, :], hb[:, :], ['hb'], ['mid'])
        P.tt('vector', w2[:, :], w1[:, :], mid[:, :], ALU.subtract, ['w1', 'mid'], ['w2'])
        P.cp('vector', hb[:, :], w2[:, :], ['w2'], ['hb'])
        P.cp('vector', lo[:, :], hb[:, :], ['hb'], ['lo'])
        for dst, dk, c0 in ((kaug, 'kaug', 0), (qaug, 'qaug', 4)):
            P.ts('vector', w1[:, :], hi[:, :], co[:, c0:c0 + 1], co[:, c0 + 3:c0 + 4], ALU.mult, ALU.add,
                 ['hi', 'co'], ['w1'])
            P.stt('vector', w2[:, :], mid[:, :], co[:, c0 + 1:c0 + 2], w1[:, :], ALU.mult, ALU.add,
                  ['mid', 'co', 'w1'], ['w2'])
            P.stt('vector', dst[:, :], lo[:, :], co[:, c0 + 2:c0 + 3], w2[:, :], ALU.mult, ALU.add,
                  ['lo', 'co', 'w2'], [dk])
        q, k, v = T['q'], T['k'], T['v']
        for g in range(8):
            qs = slice(g * 512, g * 512 + 512)
            steps = []
            for S in range(4 * g + 4):
                ks = slice(S * 128, S * 128 + 128)
                st = dict(pairs=[(k[:, ks], q[:, qs]), (kaug[:, ks], qaug[:, qs])], reads=['q', 'k', 'kaug', 'qaug'],
                          v=v[:, S, :], vreads=['v'])
                if S >= 4 * g:
                    st['pairs'].append((ident[:, :], nmask[:, S - 4 * g, :]))
                    st['reads'] += ['ident', 'nmask']
                steps.append(st)
            (on, ot), (ln, lt) = attn_group(cx, steps)
            P.op('vector', lambda e, lt=lt: e.reciprocal(out=rl[:, :], in_=lt[:, :]), [ln], ['rl'])
            i, sb_ = ost.next()
            P.tt('vector', sb_[:, :], ot[:, :], rl[:, :], ALU.mult, [on, 'rl'], [('ostg', i)])
            ost.store(i, g)
        P.finish([('out', g) for g in range(8)])
        P.emit()
    return nc


def build_diff():
    nc = bass.Bass("TRN2", target_bir_lowering=False)
    lamp = nc.dram_tensor("lamp", [128, 256], F32, kind="ExternalInput").ap()
    dn = nc.dram_tensor("dn", [128, 2], F32, kind="ExternalInput").ap()
    o_out = nc.dram_tensor("o_out", [128, SEQ], BF16, kind="ExternalOutput").ap()
    with ExitStack() as es:
        P = Prog(nc, es)
        cx = AttnCtx(P)
        T = load_qkv(P, nc, ['q', 'k', 'v'])
        mle = load_masks(P, nc, 'mle')
        ost = OutStage(P, o_out)
        lp = P.sb("lp", [128, 256], F32)
        dn_sb = P.sb("dn_sb", [128, 2], F32)
        pr = P.sb("pr", [128, 128], F32)
        sc = P.sb("sc", [128, 8], F32)
        rl = P.sb("rl", [128, 512], F32)
        o1 = P.sb("o1", [128, 512], F32)
        o2 = P.sb("o2", [128, 512], F32)
        sq = P.sb("sq", [128, 512], BF16)
        rstd = P.sb("rstd", [128, 512], F32)
        P.dma('sync', lp[:, :], lamp, [], ['lp'], chan='lp')
        P.dma('sync', dn_sb[:, :], dn, [], ['dn'], chan='dn')
        P.tt('vector', pr[:, 0:64], lp[:, 0:64], lp[:, 64:128], ALU.mult, ['lp'], ['pr'])
        P.tt('vector', pr[:, 64:128], lp[:, 128:192], lp[:, 192:256], ALU.mult, ['lp', 'pr'], ['pr'])
        P.op('vector', lambda e: e.reduce_sum(out=sc[:, 0:1], in_=pr[:, 0:64], axis=AX.X), ['pr'], ['sc'])
        P.op('vector', lambda e: e.reduce_sum(out=sc[:, 1:2], in_=pr[:, 64:128], axis=AX.X), ['pr', 'sc'], ['sc'])
        P.act(sc[:, 2:4], sc[:, 0:2], AF.Exp, ['sc'], ['sc'])
        P.tt('vector', sc[:, 4:5], sc[:, 3:4], sc[:, 2:3], ALU.subtract, ['sc'], ['sc'])
        P.tt('vector', sc[:, 4:5], sc[:, 4:5], dn_sb[:, 1:2], ALU.subtract, ['sc', 'dn'], ['sc'])
        P.ts('vector', sc[:, 5:6], dn_sb[:, 1:2], -1.0, 1.0, ALU.mult, ALU.add, ['dn', 'sc'], ['sc'])
        P.tt('vector', sc[:, 5:6], sc[:, 5:6], dn_sb[:, 0:1], ALU.mult, ['sc', 'dn'], ['sc'])
        q, k, v = T['q'], T['k'], T['v']
        for g in range(8):
            qs = slice(g * 512, g * 512 + 512)
            for half, od, ok_ in ((0, o1, 'o1'), (1, o2, 'o2')):
                hs = slice(64 * half, 64 * half + 64)
                steps = []
                for S in range(4 * g + 4):
                    ks = slice(S * 128, S * 128 + 128)
                    st = dict(pairs=[(k[hs, ks], q[hs, qs])], reads=['q', 'k'], v=v[:, S, :], vreads=['v'])
                    if S >= 4 * g:
                        st['mask'] = mle[:, S - 4 * g, :]
                        st['mkey'] = 'mle'
                    steps.append(st)
                (on, ot), (ln, lt) = attn_group(cx, steps)
                P.op('vector', lambda e, lt=lt: e.reciprocal(out=rl[:, :], in_=lt[:, :]), [ln], ['rl'])
                P.tt('vector', od[:, :], ot[:, :], rl[:, :], ALU.mult, [on, 'rl'], [ok_])
            P.stt('vector', o1[:, :], o2[:, :], sc[:, 4:5], o1[:, :], ALU.mult, ALU.add, ['o1', 'o2', 'sc'], ['o1'])
            P.act(sq[:, :], o1[:, :], AF.Square, ['o1'], ['sq'])
            xn, xt = cx.psx
            P.mm(xt[:, :], [(cx.ones[:, :], sq[:, :])], ['sq', 'ones'], [xn])
            P.act(rstd[:, :], xt[:, :], AF.Sqrt, [xn], ['rstd'], bias=EPS, scale=1.0 / 128)
            P.op('vector', lambda e: e.reciprocal(out=rstd[:, :], in_=rstd[:, :]), ['rstd'], ['rstd'])
            i, sb_ = ost.next()
            P.stt('vector', sb_[:, :], o1[:, :], sc[:, 5:6], rstd[:, :], ALU.mult, ALU.mult,
                  ['o1', 'sc', 'rstd'], [('ostg', i)])
            ost.store(i, g)
        P.finish([('out', g) for g in range(8)])
        P.emit()
    return nc


def build_sb():
    nc = bass.Bass("TRN2", target_bir_lowering=False)
    trid = nc.dram_tensor("ntri", [128, 128], BF16, kind="ExternalInput").ap()
    o_out = nc.dram_tensor("o_out", [128, SEQ], BF16, kind="ExternalOutput").ap()
    with ExitStack() as es:
        P = Prog(nc, es)
        cx = AttnCtx(P)
        T = load_qkv(P, nc, ['q', 'k', 'v'])
        mlt = load_masks(P, nc, 'mlt')
        ost = OutStage(P, o_out)
        ntri = P.sb("ntri_sb", [128, 128], BF16)
        nones = P.sb("nones", [128, 128], BF16)
        ebuf = [P.sb("ebuf%d" % i, [128, 512], F32) for i in range(2)]
        Lm = [P.sb("Lm%d" % i, [128, 512], BF16) for i in range(3)]
        Lsum = P.sb("Lsum", [128, 512], F32)
        Lsb = [P.sb("Lsb%d" % i, [128, 512], BF16) for i in range(2)]
        P.dma('sync', ntri[:, :], trid, [], ['ntri'], chan='ntri')
        P.memset('vector', nones[:, :], -1.0, ['nones'])
        q, k, v = T['q'], T['k'], T['v']
        zb = [cx.pss[0], cx.pss[1]]
        ab = [cx.pss[2], cx.psx]
        for g in range(8):
            qs = slice(g * 512, g * 512 + 512)
            (on, ot), _ = cx.next_acc()
            Slist = list(range(4 * g + 3, -1, -1))
            n = len(Slist)
            for kk in range(n + 2):
                if kk < n:
                    S = Slist[kk]
                    ks = slice(S * 128, S * 128 + 128)
                    zn, zt = zb[kk % 2]
                    e_ = ebuf[kk % 2]
                    ek = ('e', kk % 2)
                    L_ = Lm[kk % 3]
                    Lk = ('L', kk % 3)
                    P.mm(zt[:, :], [(k[:, ks], q[:, qs])], ['q', 'k'], [zn])
                    P.act(e_[:, :], zt[:, :], AF.Exp, [zn], [ek])
                    P.act(L_[:, :], e_[:, :], AF.Ln, [ek], [Lk], bias=1.0)
                    if S >= 4 * g:
                        P.tt('gpsimd', L_[:, :], L_[:, :], mlt[:, S - 4 * g, :], ALU.mult, [Lk, 'mlt'], [Lk])
                if 1 <= kk <= n:
                    i = kk - 1
                    S = Slist[i]
                    ks = slice(S * 128, S * 128 + 128)
                    an, at = ab[i % 2]
                    L_ = Lm[i % 3]
                    Lk = ('L', i % 3)
                    pn, pt = cx.pT[i % 3]
                    pairs = [(k[:, ks], q[:, qs]), (ntri[:, :], L_[:, :])]
                    rd = ['q', 'k', 'ntri', Lk]
                    if i > 0:
                        pairs.append((nones[:, :], Lsb[(i - 1) % 2][:, :]))
                        rd += ['nones', ('Lsb', (i - 1) % 2)]
                    P.mm(at[:, :], pairs, rd, [an])
                    P.act(pt[:, :], at[:, :], AF.Exp, [an], [pn])
                    if S >= 4 * g:
                        P.tt('gpsimd', pt[:, :], pt[:, :], mlt[:, S - 4 * g, :], ALU.mult, [pn, 'mlt'], [pn])
                    if i < n - 1:
                        if i == 0:
                            P.cp('vector', Lsum[:, :], L_[:, :], [Lk], ['Lsum'])
                        else:
                            P.tt('vector', Lsum[:, :], Lsum[:, :], L_[:, :], ALU.add, [Lk, 'Lsum'], ['Lsum'])
                        P.cp('vector', Lsb[i % 2][:, :], Lsum[:, :], ['Lsum'], [('Lsb', i % 2)])
                if kk >= 2:
                    i = kk - 2
                    S = Slist[i]
                    pn, pt = cx.pT[i % 3]
                    P.mm(ot[:, :], [(v[:, S, :], pt[:, :])], [pn, 'v'], [on], start=(i == 0), stop=(i == n - 1))
            j, sb_ = ost.next()
            P.cp('vector', sb_[:, :], ot[:, :], [on], [('ostg', j)])
            ost.store(j, g)
        P.finish([('out', g) for g in range(8)])
        P.emit()
    return nc


def tm_layout(v):
    return np.ascontiguousarray(v.reshape(32, 128, 128).transpose(1, 0, 2))


def fox_coef():
    co = np.zeros((6, 8), np.float32)
    co[0, 0] = -1; co[1, 1] = -1; co[2, 2] = -1; co[3:6, 3] = 1
    co[0:3, 7] = 1; co[3, 4] = 1; co[4, 5] = 1; co[5, 6] = 1
    return co


def ntri_const():
    j = np.arange(128)[:, None]
    s = np.arange(128)[None, :]
    return (-(j >= s).astype(np.float32)).astype(NPBF)


def mask_layout(m):
    return np.ascontiguousarray(m.transpose(1, 0, 2))


def nsa_consts():
    c = np.arange(128)[:, None]
    t = np.arange(512)[None, :]
    cm = []
    for g in range(5):
        cm.append((16 * c + 31 <= 512 * g + t))
    for g in range(4, 8):
        cm.append((16 * (c + 128) + 31 <= 512 * g + t) & (c + 128 < N_C))
    cmask = np.stack(cm).astype(np.float32).astype(NPBF)
    overlap = np.zeros((256, 64), np.float32)
    for r in range(2):
        sub = np.arange(N_C) * 16 + r * 16
        np.add.at(overlap, (np.arange(N_C), sub // 64), 1.0)
    ov = np.concatenate([overlap.reshape(2, 128, 64), np.ones((2, 128, 1), np.float32)], axis=2)
    ovl = np.ascontiguousarray(ov.transpose(1, 0, 2)).astype(NPBF)
    tt_ = np.arange(SEQ)[:, None]
    j = np.arange(64)[None, :]
    cur = tt_ // 64
    valid = j <= cur
    forced = valid & ((j == 0) | (j >= cur - 1))
    A = (valid & ~forced).astype(np.float32)
    Bm = np.where(forced, np.float32(1e9), np.where(valid, np.float32(0), np.float32(-1e30))).astype(np.float32)
    ab = np.stack([A.reshape(32, 128, 64).transpose(1, 0, 2), Bm.reshape(32, 128, 64).transpose(1, 0, 2)], axis=1)
    eall = ((np.arange(SEQ)[None, :] // 64) == np.arange(64)[:, None]).astype(np.float32) * BIGNEG
    ident = np.eye(128, dtype=np.float32)
    return dict(cmask=np.ascontiguousarray(cmask.transpose(1, 0, 2)), ovl=ovl, absel=np.ascontiguousarray(ab),
                eall=eall.astype(NPBF), ident=ident.astype(NPBF))


def build_nsa():
    nc = bass.Bass("TRN2", target_bir_lowering=False)
    gates_d = nc.dram_tensor("gates", [3, SEQ], F32, kind="ExternalInput").ap()
    w1k_d = nc.dram_tensor("w1k", [128, 8192], F32, kind="ExternalInput").ap()
    w1v_d = nc.dram_tensor("w1v", [128, 8192], F32, kind="ExternalInput").ap()
    w2_d = nc.dram_tensor("w2", [128, 4, 128], F32, kind="ExternalInput").ap()
    pe_d = nc.dram_tensor("pe", [128, 64], F32, kind="ExternalInput").ap()
    cmask_d = nc.dram_tensor("cmask", [128, 9, 512], BF16, kind="ExternalInput").ap()
    ovl_d = nc.dram_tensor("ovl", [128, 2, 65], BF16, kind="ExternalInput").ap()
    ab_d = nc.dram_tensor("absel", [128, 2, 32, 64], F32, kind="ExternalInput").ap()
    eall_d = nc.dram_tensor("eall", [64, SEQ], BF16, kind="ExternalInput").ap()
    ident_d = nc.dram_tensor("ident", [128, 128], BF16, kind="ExternalInput").ap()
    o_out = nc.dram_tensor("o_out", [128, SEQ], BF16, kind="ExternalOutput").ap()
    with ExitStack() as es:
        P = Prog(nc, es)
        cx = AttnCtx(P)
        T = load_qkv(P, nc, ['aqr', 'aq0', 'aq1', 'aq2', 'aq3', 'ks', 'kw', 'vs', 'vw'])
        xkv_d = [nc.dram_tensor(nm, [128, SEQ], BF16, kind="ExternalInput").ap() for nm in ('xk', 'xv')]
        xkv = P.sb("xkv", [128, SEQ], BF16)
        mle = load_masks(P, nc, 'mle')
        mwin = load_masks(P, nc, 'mwin')
        ost = OutStage(P, o_out)
        cmask = P.sb("cmask_sb", [128, 9, 512], BF16)
        ovl = P.sb("ovl_sb", [128, 2, 65], BF16)
        ab = P.sb("ab_sb", [128, 2, 32, 64], F32)
        eall = P.sb("eall_sb", [64, SEQ], BF16)
        ident = P.sb("ident_sb", [128, 128], BF16)
        selT = P.sb("selT", [64, SEQ], BF16)
        res = P.sb("res", [128, SEQ], F32)
        w1 = P.sb("w1", [128, 8192], BF16)
        w2 = P.sb("w2_sb", [128, 4, 128], BF16)
        pe32 = P.sb("pe32", [128, 64], F32)
        peb = P.sb("peb", [128, 64], BF16)
        bias = P.sb("bias", [128, 4], F32)
        hid = [P.sb("hid%d" % j, [128, 256], BF16) for j in range(2)]
        kcT = P.sb("kcT", [128, 256], BF16)
        vc = P.sb("vc", [128, 2, 128], BF16)
        ecm = [P.sb("ecm%d" % i, [128, 512], BF16) for i in range(4)]
        PT = [P.sb("PT%d" % i, [128, 512], F32) for i in range(2)]
        PTb = [P.sb("PTb%d" % i, [128, 512], BF16) for i in range(2)]
        tmp = P.sb("tmp", [128, 512], F32)
        rl = P.sb("rl", [128, 512], F32)
        gsb2 = [[P.sb("gsb%d_%d" % (i, j), [128, 512], F32) for j in range(2)] for i in range(3)]
        sc = P.sb("sc", [128, 64], F32)
        scq = P.sb("scq", [128, 4, 64], F32)
        rcol = P.sb("rcol", [128, 4], F32)
        sc2 = P.sb("sc2", [128, 64], F32)
        m8 = P.sb("m8", [128, 16], F32)
        selm = P.sb("selm", [128, 64], BF16)
        for s_, d_, k_ in ((cmask[:, :, :], cmask_d, 'cmask'), (ovl[:, :, :], ovl_d, 'ovl'), (ab[:, :, :, :], ab_d, 'ab'),
                           (eall[:, :], eall_d, 'eall'), (ident[:, :], ident_d, 'ident'), (pe32[:, :], pe_d, 'pe32')):
            P.dma('sync', s_, d_, [], [k_], chan=k_)
        P.dma('gpsimd', w2[:, :, :], w2_d, [], ['w2'], chan='w2')
        P.cp('vector', peb[:, :], pe32[:, :], ['pe32'], ['peb'])
        P.memset('vector', kcT[:, :], 0.0, ['kcT'])
        P.memset('vector', hid[0][:, :], 0.0, [('hid', 0)])
        P.memset('vector', hid[1][:, :], 0.0, [('hid', 1)])
        xn, xt = cx.psx
        for kv in range(2):
            P.dma('gpsimd', w1[:, :], (w1k_d, w1v_d)[kv], [], ['w1'], chan='w1')
            P.dma('sync', xkv[:, :], xkv_d[kv], [], ['xkv'], chan='xkv')
            xr = xkv[:, :].rearrange("p (c r) -> p c r", r=16)
            for j in range(2):
                P.mm(xt[:, 0:1], [(w1[:, l * 256 + j * 128:l * 256 + j * 128 + 128], peb[:, kv * 32 + l:kv * 32 + l + 1])
                                  for l in range(32)], ['w1', 'peb'], [xn])
                P.cp('vector', bias[:, 2 * kv + j:2 * kv + j + 1], xt[:, 0:1], [xn], ['bias'])
                (sn, st), _ = cx.next_s()
                P.mm(st[:, 0:255], [(w1[:, l * 256 + j * 128:l * 256 + j * 128 + 128],
                                     (xr[:, 0:255, l] if l < 16 else xr[:, 1:256, l - 16])) for l in range(32)],
                     ['w1', 'xkv'], [sn])
                P.act(hid[j][:, 0:255], st[:, 0:255], AF.Gelu_apprx_tanh, [sn, 'bias'], [('hid', j)],
                      bias=bias[:, 2 * kv + j:2 * kv + j + 1])
            if kv == 0:
                P.mm(xt[:, 0:255], [(w2[:, j, :], hid[j][:, 0:255]) for j in range(2)],
                     ['w2', ('hid', 0), ('hid', 1)], [xn])
                P.cp('vector', kcT[:, 0:255], xt[:, 0:255], [xn], ['kcT'])
            else:
                for ct in range(2):
                    P.mm(xt[:, 0:128], [(hid[j][:, ct * 128:ct * 128 + 128], w2[:, 2 + j, :]) for j in range(2)],
                         ['w2', ('hid', 0), ('hid', 1)], [xn])
                    P.cp('vector', vc[:, ct, :], xt[:, 0:128], [xn], ['vc'])

        def load_gate(r, g):
            if g >= 8:
                return
            gb_ = gsb2[r][g % 2]
            key = ('gs', r, g % 2)
            P.dma('sync', gb_[:, :], gates_d[r:r + 1, g * 512:(g + 1) * 512].partition_broadcast(128), [],
                  [key], chan='gs%d_%d' % (r, g % 2))
            P.act(gb_[:, :], gb_[:, :], AF.Exp, [key], [key], scale=-1.0)
            P.act(gb_[:, :], gb_[:, :], AF.Ln, [key], [key], bias=1.0)
            P.act(gb_[:, :], gb_[:, :], AF.Exp, [key], [key], scale=-1.0)
        aq = [T['aq%d' % h] for h in range(4)]
        ei = 0
        for g in range(8):
            qs = slice(g * 512, g * 512 + 512)
            cts = [0] if g < 4 else [0, 1]
            if g == 0:
                load_gate(0, 0)
            load_gate(0, g + 1)
            for h in range(4):
                (on, ot), (ln, lt) = cx.next_acc()
                es_ = []
                if h == 0:
                    (l0n, l0t), _ = cx.next_s()
                for ci, ct in enumerate(cts):
                    (sn, st), _ = cx.next_s()
                    P.mm(st[:, :], [(kcT[:, ct * 128:ct * 128 + 128], aq[h][:, qs])], ['kcT', 'aq%d' % h], [sn])
                    e_ = ecm[ei % 4]
                    ek = ('ecm', ei % 4)
                    ei += 1
                    P.act(e_[:, :], st[:, :], AF.Exp, [sn], [ek])
                    mi = None
                    if ct == 0 and g <= 4:
                        mi = g
                    elif ct == 1:
                        mi = 5 + (g - 4)
                    if mi is not None:
                        P.tt('gpsimd', e_[:, :], e_[:, :], cmask[:, mi, :], ALU.mult, [ek, 'cmask'], [ek])
                    if h == 0:
                        P.mm(l0t[:, :], [(cx.ones[:, :], e_[:, :])], [ek, 'ones'], [l0n], start=(ci == 0),
                             stop=(ci == len(cts) - 1))
                        P.mm(ot[:, :], [(vc[:, ct, :], e_[:, :])], [ek, 'vc'], [on], start=(ci == 0),
                             stop=(ci == len(cts) - 1))
                    es_.append((ct, e_, ek))
                for qt in range(4):
                    P.mm(lt[:, qt * 65:qt * 65 + 65], [(e_[:, qt * 128:qt * 128 + 128], ovl[:, ct, :])
                                                       for ct, e_, ek in es_],
                         [ek for ct, e_, ek in es_] + ['ovl'], [ln])
                lv = lt[:, 0:260].rearrange("p (q c) -> p q c", c=65)
                P.ts('vector', rcol[:, :], lv[:, :, 64], 1e-30, None, ALU.max, None, [ln], ['rcol'])
                P.op('vector', lambda e: e.reciprocal(out=rcol[:, :], in_=rcol[:, :]), ['rcol'], ['rcol'])
                for qt in range(4):
                    if h == 0:
                        P.ts('vector', scq[:, qt, :], lt[:, qt * 65:qt * 65 + 64], rcol[:, qt:qt + 1], None,
                             ALU.mult, None, [ln, 'rcol'], [('scq', qt)])
                    else:
                        P.stt('vector', scq[:, qt, :], lt[:, qt * 65:qt * 65 + 64], rcol[:, qt:qt + 1],
                              scq[:, qt, :], ALU.mult, ALU.add, [ln, 'rcol', ('scq', qt)], [('scq', qt)])
                if h == 0:
                    P.ts('vector', rl[:, :], l0t[:, :], 1e-30, None, ALU.max, None, [l0n], ['rl'])
                    P.act(rl[:, :], rl[:, :], AF.Ln, ['rl'], ['rl'])
                    P.act(rl[:, :], rl[:, :], AF.Exp, ['rl'], ['rl'], scale=-1.0)
                    P.tt('vector', tmp[:, :], ot[:, :], rl[:, :], ALU.mult, [on, 'rl'], ['tmp'])
                    P.tt('vector', res[:, qs], tmp[:, :], gsb2[0][g % 2][:, :], ALU.mult, ['tmp', ('gs', 0, g % 2)],
                         [('res', g)])
            for qt in range(4):
                Tq = 4 * g + qt
                P.tt('vector', sc[:, :], scq[:, qt, :], ab[:, 0, Tq, :], ALU.mult, [('scq', qt), 'ab'], ['sc'])
                P.tt('vector', sc[:, :], sc[:, :], ab[:, 1, Tq, :], ALU.add, ['sc', 'ab'], ['sc'])
                P.op('vector', lambda e: e.max(out=m8[:, 0:8], in_=sc[:, :]), ['sc'], ['m8'])
                P.op('vector', lambda e: e.match_replace(out=sc2[:, :], in_to_replace=m8[:, 0:8], in_values=sc[:, :],
                                                         imm_value=-1e30), ['sc', 'm8'], ['sc2'])
                P.op('vector', lambda e: e.max(out=m8[:, 8:16], in_=sc2[:, :]), ['sc2', 'm8'], ['m8'])
                P.ts('vector', selm[:, :], sc[:, :], m8[:, 15:16], -1.0, ALU.is_ge, ALU.add, ['sc', 'm8'], ['selm'])
                P.mm(xt[0:64, 128:256], [(selm[:, :], ident[:, :])], ['selm', 'ident'], [xn])
                P.cp('vector', selT[:, Tq * 128:Tq * 128 + 128], xt[0:64, 128:256], [xn], [('selT', g)])
        for phase in (3, 4):
            r_ = phase - 2
            kk = T['ks'] if phase == 3 else T['kw']
            kn = 'ks' if phase == 3 else 'kw'
            vv = T['vs'] if phase == 3 else T['vw']
            vn = 'vs' if phase == 3 else 'vw'
            for g in range(8):
                qs = slice(g * 512, g * 512 + 512)
                if g == 0:
                    load_gate(r_, 0)
                load_gate(r_, g + 1)
                steps = []
                S0 = 0 if phase == 3 else max(0, 4 * g - 4)
                for S in range(S0, 4 * g + 4):
                    ks = slice(S * 128, S * 128 + 128)
                    pairs = [(kk[:, ks], T['aqr'][:, qs])]
                    rd = [kn, 'aqr']
                    if phase == 3:
                        pairs.append((eall[:, ks], selT[:, qs]))
                        rd += ['eall', ('selT', g)]
                    st = dict(pairs=pairs, reads=rd, v=vv[:, S, :], vreads=[vn])
                    if S >= 4 * g:
                        st['mask'] = mle[:, S - 4 * g, :]
                        st['mkey'] = 'mle'
                    elif phase == 4:
                        st['mask'] = mwin[:, S - (4 * g - 4), :]
                        st['mkey'] = 'mwin'
                    steps.append(st)
                (on, ot), (ln, lt) = attn_group(cx, steps)
                P.op('vector', lambda e, lt=lt: e.reciprocal(out=rl[:, :], in_=lt[:, :]), [ln], ['rl'])
                P.tt('vector', tmp[:, :], ot[:, :], rl[:, :], ALU.mult, [on, 'rl'], ['tmp'])
                P.tt('vector', tmp[:, :], tmp[:, :], gsb2[r_][g % 2][:, :], ALU.mult, ['tmp', ('gs', r_, g % 2)], ['tmp'])
                if phase == 3:
                    P.tt('vector', res[:, qs], res[:, qs], tmp[:, :], ALU.add, ['tmp', ('res', g)], [('res', g)])
                else:
                    i, sb_ = ost.next()
                    P.tt('vector', sb_[:, :], res[:, qs], tmp[:, :], ALU.add, ['tmp', ('res', g)], [('ostg', i)])
                    ost.store(i, g)
        P.finish([('out', g) for g in range(8)])
        P.emit()
    return nc


def nsa_weights(l, p):
    w1k = np.ascontiguousarray(p['nsa_cmp_k1'][l].reshape(32, 128, 256).transpose(1, 0, 2).reshape(128, 8192))
    w1v = np.ascontiguousarray(p['nsa_cmp_v1'][l].reshape(32, 128, 256).transpose(1, 0, 2).reshape(128, 8192))
    w2 = np.stack([p['nsa_cmp_k2'][l][0:128], p['nsa_cmp_k2'][l][128:256],
                   p['nsa_cmp_v2'][l][0:128], p['nsa_cmp_v2'][l][128:256]], axis=1)
    pe = np.concatenate([p['nsa_pe_k'][l].T, p['nsa_pe_v'][l].T], axis=1)
    return dict(w1k=w1k, w1v=w1v, w2=np.ascontiguousarray(w2.astype(np.float32)),
                pe=np.ascontiguousarray(pe.astype(np.float32)))


_PROGS = {}


def _prog(name):
    if name not in _PROGS:
        _PROGS[name] = dict(A=build_A, C=build_C, nsa=build_nsa, sb=build_sb, fox=build_fox, diff=build_diff)[name]()
    return _PROGS[name]


def _run(name, in_maps):
    res = run_bass_kernel_spmd(_prog(name), in_maps, core_ids=list(range(8)))
    return res.results


def _halo(arr_b, j, n):
    t0 = j * n
    if j == 0:
        sl = np.concatenate([np.zeros((2, arr_b.shape[1]), arr_b.dtype), arr_b[0:n]], axis=0)
    else:
        sl = arr_b[t0 - 2:t0 + n]
    return np.ascontiguousarray(sl.T.reshape(16, 128, n + 2).transpose(1, 0, 2))


def kernel(x, norm_mix_pre, norm_mix_post, norm_mlp_pre, norm_mlp_post, w_in, w_out,
           nsa_pe_k, nsa_pe_v, nsa_cmp_k1, nsa_cmp_k2, nsa_cmp_v1, nsa_cmp_v2,
           fox_forget_bias, diff_lambda, diff_norm, mlp_w_up, mlp_conv_w, mlp_conv_b, mlp_w_down):
    p = dict(nsa_pe_k=np.asarray(nsa_pe_k), nsa_pe_v=np.asarray(nsa_pe_v), nsa_cmp_k1=np.asarray(nsa_cmp_k1),
             nsa_cmp_k2=np.asarray(nsa_cmp_k2), nsa_cmp_v1=np.asarray(nsa_cmp_v1), nsa_cmp_v2=np.asarray(nsa_cmp_v2))
    x = np.array(x, dtype=np.float32, copy=True)
    w_in = np.asarray(w_in); w_out = np.asarray(w_out); mlp_w_up = np.asarray(mlp_w_up)
    mlp_w_down = np.asarray(mlp_w_down); mlp_conv_w = np.asarray(mlp_conv_w); mlp_conv_b = np.asarray(mlp_conv_b)
    le, lt, win = causal_masks()
    mle, mlt, mwin = mask_layout(le), mask_layout(lt), mask_layout(win)
    nmask = mask_layout(((le.astype(np.float32) - 1.0) * BIGNEG).astype(NPBF))
    ident = np.eye(128, dtype=np.float32).astype(NPBF)
    ncst = nsa_consts()
    ntri = ntri_const()
    coef = fox_coef()
    tabs = [rope_tables(j * TOK, TOK) for j in range(4)]

    def fmaj(v):
        return np.ascontiguousarray(np.asarray(v, np.float32).reshape(16, 128).T)
    for l in range(DEPTH):
        wblk = host_w_in_blocks(w_in[l]).reshape(16, 128, 8192)
        gpre = fmaj(norm_mix_pre[l])
        in_maps = []
        for c in range(8):
            b, j = c // 4, c % 4
            xs = x[b, j * TOK:(j + 1) * TOK, :]
            xT = np.ascontiguousarray(xs.T.reshape(16, 128, TOK).transpose(1, 0, 2))
            in_maps.append(dict(xT=xT, gpre=gpre, wblk=wblk, tabs=tabs[j]))
        rA = _run('A', in_maps)
        del wblk, in_maps
        fm = [np.concatenate([np.asarray(rA[4 * b + j]['fm_out']) for j in range(4)], axis=2) for b in range(NB)]
        sm = [np.concatenate([np.asarray(rA[4 * b + j]['sm_out'])[0:16] for j in range(4)], axis=1) for b in range(NB)]
        tm = [np.concatenate([np.asarray(rA[4 * b + j]['tm_out']) for j in range(4)], axis=0) for b in range(NB)]
        del rA
        nw = nsa_weights(l, p)
        lam_init = 0.8 - 0.6 * math.exp(-0.3 * l)
        lamp = np.ascontiguousarray(np.tile(np.asarray(diff_lambda[l], np.float32).reshape(1, 256), (128, 1)))
        dn = np.ascontiguousarray(np.stack([np.asarray(diff_norm[l], np.float32),
                                            np.full(128, lam_init, np.float32)], axis=1))
        maps = dict(nsa=[], sb=[], fox=[], diff=[])
        for c in range(8):
            b, h = c // 4, c % 4
            f, s_, t_ = fm[b], sm[b], tm[b]
            order = [h] + [i for i in range(4) if i != h]
            m = dict(aqr=f[h], xk=f[8], xv=f[9], ks=f[10], kw=f[11], vs=tm_layout(t_[:, 0:128]),
                     vw=tm_layout(t_[:, 128:256]), gates=np.ascontiguousarray(s_[3 * h:3 * h + 3]), mle=mle, mwin=mwin)
            for i, hh in enumerate(order):
                m['aq%d' % i] = f[4 + hh]
            m.update(ncst)
            m.update(nw)
            maps['nsa'].append(m)
            maps['sb'].append(dict(q=f[12 + h], k=f[16 + h], v=tm_layout(t_[:, 256 + 128 * h:384 + 128 * h]),
                                   mlt=mlt, ntri=ntri))
            maps['fox'].append(dict(q=f[20 + h], k=f[24 + h], v=tm_layout(t_[:, 768 + 128 * h:896 + 128 * h]),
                                    nmask=nmask, ident=ident,
                                    cf=np.ascontiguousarray(np.tile(s_[12 + h][None, :], (6, 1))),
                                    fb=np.full((6, 1), np.asarray(fox_forget_bias)[l][h], np.float32), coef=coef))
            maps['diff'].append(dict(q=f[28 + h], k=f[32 + h], v=tm_layout(t_[:, 1280 + 128 * h:1408 + 128 * h]),
                                     mle=mle, lamp=lamp, dn=dn))
        o_b = [np.zeros((SEQ, D), NPBF) for _ in range(NB)]
        for mi, name in enumerate(('nsa', 'sb', 'fox', 'diff')):
            r = _run(name, maps[name])
            for c in range(8):
                b, h = c // 4, c % 4
                o_b[b][:, mi * 512 + h * 128:mi * 512 + h * 128 + 128] = np.asarray(r[c]['o_out']).T
            maps[name] = None
        del fm, sm, tm
        blocks, cw = host_c_weights(w_out[l], mlp_w_up[l], mlp_w_down[l], mlp_conv_w[l], mlp_conv_b[l])
        gn = np.ascontiguousarray(np.stack([fmaj(norm_mix_post[l]), fmaj(norm_mlp_pre[l]), fmaj(norm_mlp_post[l])],
                                           axis=1))
        in_maps = []
        for c in range(8):
            b, j = c // 4, c % 4
            in_maps.append(dict(xT=_halo(x[b], j, TOK), oT=_halo(o_b[b], j, TOK), gn=gn, cw=cw, wblk=blocks))
        rC = _run('C', in_maps)
        del blocks, in_maps
        for c in range(8):
            b, j = c // 4, c % 4
            xo = np.asarray(rC[c]['x_out'])
            x[b, j * TOK:(j + 1) * TOK, :] = xo.transpose(1, 0, 2).reshape(D, TOK).T
        del rC
    return x
```
